# Optimizing a Trainium2 kernel written in Bass

```python
import jax, jax.numpy as jnp
from jax import lax
import numpy as np

D_MODEL = 1024
BATCH = 4
SEQ = 8192
DEPTH = 4

CHUNK = 64
EPS = 1e-6

A_HEADS = 4
A_WIDTH = D_MODEL
A_HEAD_DIM = A_WIDTH // A_HEADS
CONV_WIDTH = 4

B_HEADS = 8
B_WIDTH = D_MODEL
B_HEAD_DIM = B_WIDTH // B_HEADS
SB_BLOCK = 128

POOL_WINDOWS = (2, 4, 8, 16)
C_GROUPS = len(POOL_WINDOWS)
C_WIDTH = 2 * D_MODEL
C_GROUP_DIM = C_WIDTH // C_GROUPS

N_EVEN = (DEPTH + 1) // 2
N_ODD = DEPTH // 2
EVEN_SPLITS = [A_WIDTH] * 5 + [A_HEADS] * 2 + [B_WIDTH] * 4
EVEN_IN = sum(EVEN_SPLITS)
EVEN_OUT = A_WIDTH + B_WIDTH
ODD_IN = 2 * C_WIDTH

kernel_name = "hybrid_mlstm_stickbreak_pool_trunk"


def rmsnorm(x, g):
    xf = x.astype(jnp.float32)
    y = xf * lax.rsqrt(jnp.mean(xf * xf, axis=-1, keepdims=True) + EPS)
    return y * g.astype(jnp.float32)


def causal_dwconv(x, w):
    K = w.shape[0]
    S = x.shape[1]
    xp = jnp.pad(x, ((0, 0), (K - 1, 0), (0, 0)))
    return sum(xp[:, k:k + S] * w[k].astype(jnp.float32) for k in range(K))


def split_cols(u, sizes):
    idx = list(np.cumsum(sizes)[:-1])
    return jnp.split(u, idx, axis=-1)


def mlstm(q, k, v, ig, lf):
    Bn, S, H, dh = q.shape
    nc = S // CHUNK
    k = k * (dh ** -0.5)

    def chunks4(t):
        return t.reshape(Bn, nc, CHUNK, H, dh).transpose(1, 0, 3, 2, 4)

    def chunks3(t):
        return t.reshape(Bn, nc, CHUNK, H).transpose(1, 0, 3, 2)

    xs = (chunks4(q), chunks4(k), chunks4(v), chunks3(ig), chunks3(lf))
    tri = jnp.tril(jnp.ones((CHUNK, CHUNK), dtype=bool))

    def step(carry, inp):
        C, n, m = carry
        qc, kc, vc, ic, fc = inp
        b = jnp.cumsum(fc, axis=-1)
        Dm = jnp.where(tri, b[..., :, None] - b[..., None, :] + ic[..., None, :], -1e30)
        m_inter = b + m[..., None]
        m_t = jnp.maximum(m_inter, jnp.max(Dm, axis=-1))
        W = jnp.exp(Dm - m_t[..., None])
        Sc = jnp.einsum('bhtd,bhsd->bhts', qc, kc) * W
        s_inter = jnp.exp(m_inter - m_t)
        num = (jnp.einsum('bhts,bhsd->bhtd', Sc, vc)
               + s_inter[..., None] * jnp.einsum('bhvk,bhtk->bhtv', C, qc))
        den = jnp.sum(Sc, axis=-1) + s_inter * jnp.einsum('bhk,bhtk->bht', n, qc)
        h = num / jnp.maximum(jnp.abs(den), jnp.exp(-m_t))[..., None]
        g = b[..., -1]
        a = g[..., None] - b + ic
        m_new = jnp.maximum(g + m, jnp.max(a, axis=-1))
        decay = jnp.exp(g + m - m_new)
        wa = jnp.exp(a - m_new[..., None])
        C_new = decay[..., None, None] * C + jnp.einsum('bhs,bhsv,bhsk->bhvk', wa, vc, kc)
        n_new = decay[..., None] * n + jnp.einsum('bhs,bhsk->bhk', wa, kc)
        return (C_new, n_new, m_new), h

    init = (jnp.zeros((Bn, H, dh, dh), jnp.float32),
            jnp.zeros((Bn, H, dh), jnp.float32),
            jnp.zeros((Bn, H), jnp.float32))
    _, hs = lax.scan(step, init, xs)
    return hs.transpose(1, 0, 3, 2, 4).reshape(Bn, S, H * dh)


def stick_breaking(q, k, v):
    S, dh = q.shape[2], q.shape[3]
    scale = dh ** -0.5
    outs = []
    for blk in range(S // SB_BLOCK):
        q0 = blk * SB_BLOCK
        q1 = q0 + SB_BLOCK
        z = jnp.einsum('bhqd,bhkd->bhqk', q[:, :, q0:q1], k[:, :, :q1]) * scale
        mask = jnp.arange(q1)[None, :] < (q0 + jnp.arange(SB_BLOCK))[:, None]
        log_1mb = jnp.where(mask, jax.nn.log_sigmoid(-z), 0.0)
        after = lax.cumsum(log_1mb, axis=3, reverse=True) - log_1mb
        A = jnp.where(mask, jnp.exp(jax.nn.log_sigmoid(z) + after), 0.0)
        outs.append(jnp.einsum('bhqk,bhkd->bhqd', A, v[:, :, :q1]))
    return jnp.concatenate(outs, axis=2)


def multiscale_pool(p):
    S = p.shape[1]
    cs = jnp.cumsum(p, axis=1)
    cs0 = jnp.pad(cs, ((0, 0), (1, 0), (0, 0), (0, 0)))
    t = jnp.arange(S)
    outs = []
    for g, w in enumerate(POOL_WINDOWS):
        upper = cs0[:, 1:, g]
        lower = jnp.pad(cs0[:, :S + 1 - w, g], ((0, 0), (w - 1, 0), (0, 0)))
        cnt = jnp.minimum(t + 1, w).astype(jnp.float32)
        outs.append((upper - lower) / cnt[None, :, None] - p[:, :, g])
    return jnp.stack(outs, axis=2)


def head_norm(h, g, n_heads):
    Bn, S, W = h.shape
    hh = h.reshape(Bn, S, n_heads, W // n_heads)
    hh = hh * lax.rsqrt(jnp.mean(hh * hh, axis=-1, keepdims=True) + EPS)
    return hh.reshape(Bn, S, W) * g.astype(jnp.float32)


def even_layer(x, pre_g, post_g, w_in, conv_w, b_i, b_f, hn_g, w_out):
    Bn, S, _ = x.shape
    h = rmsnorm(x, pre_g)
    u = h @ w_in.astype(jnp.float32)
    qa, ka, va, oa, za, ia, fa, qb, kb, vb, zb = split_cols(u, EVEN_SPLITS)
    qk = jax.nn.silu(causal_dwconv(jnp.concatenate([qa, ka], axis=-1), conv_w))
    qa, ka = qk[..., :A_WIDTH], qk[..., A_WIDTH:]
    ig = ia + b_i.astype(jnp.float32)
    lf = jax.nn.log_sigmoid(fa + b_f.astype(jnp.float32))
    hs = (Bn, S, A_HEADS, A_HEAD_DIM)
    ha = mlstm(qa.reshape(hs), ka.reshape(hs), va.reshape(hs), ig, lf)
    ha = jax.nn.sigmoid(oa) * ha
    ya = head_norm(ha, hn_g, A_HEADS) * jax.nn.silu(za)
    def heads(t):
        return t.reshape(Bn, S, B_HEADS, B_HEAD_DIM).transpose(0, 2, 1, 3)
    hb = stick_breaking(heads(qb), heads(kb), heads(vb))
    hb = hb.transpose(0, 2, 1, 3).reshape(Bn, S, B_WIDTH)
    yb = hb * jax.nn.silu(zb)
    y = jnp.concatenate([ya, yb], axis=-1) @ w_out.astype(jnp.float32)
    return x + rmsnorm(y, post_g).astype(x.dtype)


def odd_layer(x, pre_g, post_g, w_in, pool_w, pool_scale, w_out):
    Bn, S, _ = x.shape
    h = rmsnorm(x, pre_g)
    u = h @ w_in.astype(jnp.float32)
    p, z = u[..., :C_WIDTH], u[..., C_WIDTH:]
    pooled = multiscale_pool(p.reshape(Bn, S, C_GROUPS, C_GROUP_DIM))
    mixed = jnp.einsum('bsgc,gce->bsge', pooled, pool_w.astype(jnp.float32))
    mixed = mixed.reshape(Bn, S, C_WIDTH) * pool_scale.astype(jnp.float32)
    y = (mixed * jax.nn.silu(z)) @ w_out.astype(jnp.float32)
    return x + rmsnorm(y, post_g).astype(x.dtype)


def setup_inputs(seed: int = 0) -> dict:
    key = jax.random.key(seed)
    ks = jax.random.split(key, 14)
    f32 = jnp.float32
    x = jax.random.normal(ks[0], (BATCH, SEQ, D_MODEL), f32)
    pre_norm_g = 1.0 + 0.02 * jax.random.normal(ks[1], (DEPTH, D_MODEL), f32)
    post_norm_g = 1.0 + 0.02 * jax.random.normal(ks[2], (DEPTH, D_MODEL), f32)
    w_in_ab = jax.random.normal(ks[3], (N_EVEN, D_MODEL, EVEN_IN), f32) * D_MODEL ** -0.5
    conv_qk = jax.random.normal(ks[4], (N_EVEN, CONV_WIDTH, 2 * A_WIDTH), f32) * CONV_WIDTH ** -0.5
    bias_i = 0.1 * jax.random.normal(ks[5], (N_EVEN, A_HEADS), f32)
    bias_f = 3.0 + 3.0 * jax.random.uniform(ks[6], (N_EVEN, A_HEADS), f32)
    head_norm_g = 1.0 + 0.02 * jax.random.normal(ks[7], (N_EVEN, A_WIDTH), f32)
    w_out_ab = jax.random.normal(ks[8], (N_EVEN, EVEN_OUT, D_MODEL), f32) * EVEN_OUT ** -0.5
    w_in_c = jax.random.normal(ks[9], (N_ODD, D_MODEL, ODD_IN), f32) * D_MODEL ** -0.5
    pool_w = jax.random.normal(ks[10], (N_ODD, C_GROUPS, C_GROUP_DIM, C_GROUP_DIM), f32) * C_GROUP_DIM ** -0.5
    pool_scale = 1.0 + 0.1 * jax.random.normal(ks[11], (N_ODD, C_WIDTH), f32)
    w_out_c = jax.random.normal(ks[12], (N_ODD, C_WIDTH, D_MODEL), f32) * C_WIDTH ** -0.5
    return {"x": x, "pre_norm_g": pre_norm_g, "post_norm_g": post_norm_g,
            "w_in_ab": w_in_ab, "conv_qk": conv_qk, "bias_i": bias_i, "bias_f": bias_f,
            "head_norm_g": head_norm_g, "w_out_ab": w_out_ab,
            "w_in_c": w_in_c, "pool_w": pool_w, "pool_scale": pool_scale, "w_out_c": w_out_c}


def reference(x, pre_norm_g, post_norm_g, w_in_ab, conv_qk, bias_i, bias_f, head_norm_g,
              w_out_ab, w_in_c, pool_w, pool_scale, w_out_c):
    for layer in range(DEPTH):
        if layer % 2 == 0:
            e = layer // 2
            x = even_layer(x, pre_norm_g[layer], post_norm_g[layer], w_in_ab[e], conv_qk[e],
                           bias_i[e], bias_f[e], head_norm_g[e], w_out_ab[e])
        else:
            o = layer // 2
            x = odd_layer(x, pre_norm_g[layer], post_norm_g[layer], w_in_c[o], pool_w[o],
                          pool_scale[o], w_out_c[o])
    return x
```

```python
import math
import numpy as np
import concourse.bass as bass
import concourse.mybir as mybir
from concourse.bass_utils import run_bass_kernel_spmd

F32 = mybir.dt.float32
BF16 = mybir.dt.bfloat16
AF = mybir.ActivationFunctionType
ALU = mybir.AluOpType
AX = mybir.AxisListType

D = 1024
SEQ = 8192
BATCH = 4
DEPTH = 4
EPS = 1e-6
TT = 512
LN16 = math.log(16.0)
DBG = {}


class Buf:
    __slots__ = ("name", "writers", "readers")

    def __init__(self, name):
        self.name = name
        self.writers = []
        self.readers = []


class Op:
    __slots__ = ("eng", "fn", "deps", "signal", "ev", "stream", "idx", "is_dma", "inc")

    def __init__(self, eng, fn, stream, is_dma):
        self.inc = 16 if is_dma else 1
        self.eng = eng
        self.fn = fn
        self.deps = set()
        self.signal = False
        self.ev = None
        self.stream = stream
        self.is_dma = is_dma


class Sched:
    ENGS = ("sync", "scalar", "vector", "gpsimd", "tensor")

    def __init__(self, nc, tag=""):
        self.nc = nc
        self.ops = []
        self.tag = tag
        self.bar = None
        self.last_eng = {}
        self.last_stream = {}

    def barrier(self):
        deps = set(self.last_eng.values()) | set(self.last_stream.values())
        dummy = self.dummy
        op = self._add("vector", lambda e: e.memset(dummy[0:1, 0:1], 0.0), (), (), None, False)
        op.deps |= deps
        op.deps.discard(op.idx)
        self.bar = op.idx
        return op

    def _add(self, eng, fn, reads, writes, stream, is_dma):
        op = Op(eng, fn, stream, is_dma)
        op.idx = len(self.ops)
        ops = self.ops
        for b in reads:
            op.deps.update(b.writers)
        for b in writes:
            op.deps.update(b.writers)
            op.deps.update(b.readers)
        op.deps.discard(op.idx)
        if eng == "tensor" and not is_dma:
            op.deps = {d for d in op.deps if not (ops[d].eng == "tensor" and not ops[d].is_dma)}
        if self.bar is not None:
            op.deps.add(self.bar)
        for b in reads:
            b.readers.append(op.idx)
        for b in writes:
            b.writers = [op.idx]
            b.readers = []
        ops.append(op)
        if is_dma:
            self.last_stream[stream] = op.idx
        else:
            self.last_eng[eng] = op.idx
        return op

    def op(self, eng, fn, reads=(), writes=()):
        return self._add(eng, fn, reads, writes, None, False)

    def dma(self, eng, fn, reads=(), writes=(), stream=None):
        return self._add(eng, fn, reads, writes, stream, True)

    def cc(self, eng, fn, reads=(), writes=(), stream=None, inc=1):
        op = self._add(eng, fn, reads, writes, stream, True)
        op.inc = inc
        return op

    def finish(self):
        nc = self.nc
        ops = self.ops
        for op in ops:
            for d in op.deps:
                ops[d].signal = True
        last_dma = {}
        for op in ops:
            if op.is_dma:
                op.signal = True
                last_dma[op.stream] = op.idx
        sems = {}
        cnt = {}
        for op in ops:
            if not op.signal:
                continue
            if op.is_dma:
                key = ("d", op.stream)
                cnt[key] = cnt.get(key, 0) + op.inc
            else:
                key = ("e", op.eng)
                cnt[key] = cnt.get(key, 0) + 1
            op.ev = (key, cnt[key])
            if key not in sems:
                sems[key] = nc.alloc_semaphore(self.tag + "s_" + "_".join(str(k) for k in key))
        known = {e: {} for e in self.ENGS}
        per_eng = {e: [] for e in self.ENGS}
        for op in ops:
            waits = {}
            kn = known[op.eng]
            for d in op.deps:
                key, val = ops[d].ev
                if kn.get(key, 0) >= val:
                    continue
                if waits.get(key, 0) < val:
                    waits[key] = val
            kn.update(waits)
            per_eng[op.eng].append((op, list(waits.items())))
        finals = [ops[i].ev for i in last_dma.values()]
        self.stats = dict(n_ops=len(ops), n_sems=len(sems),
                          per_eng={e: len(v) for e, v in per_eng.items()})
        with nc.Block() as block:
            def make(engname):
                def body(eng):
                    for op, waits in per_eng[engname]:
                        for key, val in waits:
                            eng.wait_ge(sems[key], val)
                        ins = op.fn(eng)
                        if op.ev is not None:
                            ins.then_inc(sems[op.ev[0]], op.inc)
                    if engname == "sync":
                        for key, val in finals:
                            eng.wait_ge(sems[key], val)
                return body
            block.sync(make("sync"))
            block.scalar(make("scalar"))
            block.vector(make("vector"))
            block.gpsimd(make("gpsimd"))
            block.tensor(make("tensor"))


class T:
    __slots__ = ("h", "b", "ps")

    def __init__(self, h, name, buf=None, ps=False):
        self.h = h
        self.b = buf if buf is not None else Buf(name)
        self.ps = ps

    def __getitem__(self, k):
        return self.h[k]


SB_BASE = 16640
SB_LIMIT = 229376


class Ctx:
    def __init__(self, nc, S):
        self.nc = nc
        self.S = S
        self._n = 0
        self.off = SB_BASE
        self.pairs = [nc.alloc_psum_tensor(f"pbank{i}", [128, 1024], F32) for i in range(4)]
        self.bank_bufs = [Buf(f"bank{i}") for i in range(8)]
        S.dummy = self.sb([128, 8], F32, "dummy")

    def sb(self, shape, dt, name=None):
        self._n += 1
        name = f"{name or 't'}_{self._n}"
        nbytes = int(np.prod(shape[1:])) * (4 if dt == F32 else 2)
        nbytes = (nbytes + 31) // 32 * 32
        off = self.off
        self.off += nbytes
        assert self.off <= SB_LIMIT, f"SBUF overflow at {name}: {self.off}"
        return T(self.nc.alloc_sbuf_tensor_at(name, list(shape), dt, offset=off), name)

    def bank(self, i, dt=F32, c0=0, c1=None, name=None):
        ap = self.pairs[i // 2][:, (i % 2) * 512:(i % 2 + 1) * 512]
        if dt != F32:
            ap = ap.bitcast(dt)
        if c1 is None:
            c1 = ap.shape[1]
        self._n += 1
        return T(ap[:, c0:c1], f"{name or 'ps'}_{self._n}", buf=self.bank_bufs[i], ps=True)

    def bank2(self, k, name=None):
        self._n += 1
        t = T(self.pairs[k][:].rearrange("p (h q) -> p h q", h=2), f"{name or 'ps2'}_{self._n}", buf=self.bank_bufs[2 * k], ps=True)
        t.b = (self.bank_bufs[2 * k], self.bank_bufs[2 * k + 1])
        return t

    @staticmethod
    def _bufs(t):
        return list(t.b) if isinstance(t.b, tuple) else [t.b]

    @classmethod
    def _rw(cls, r, w):
        reads = [b for t in r if not getattr(t, "ps", False) for b in cls._bufs(t)]
        writes = [b for t in w for b in cls._bufs(t)] + [b for t in r if getattr(t, "ps", False) for b in cls._bufs(t)]
        return reads, writes

    def V(self, fn, r=(), w=()):
        reads, writes = self._rw(r, w)
        self.S.op("vector", fn, reads, writes)

    def A(self, fn, r=(), w=()):
        reads, writes = self._rw(r, w)
        self.S.op("scalar", fn, reads, writes)

    def G(self, fn, r=(), w=()):
        reads, writes = self._rw(r, w)
        self.S.op("gpsimd", fn, reads, writes)

    def PE(self, fn, r=(), w=()):
        reads, writes = self._rw(r, w)
        self.S.op("tensor", fn, reads, writes)

    def DMA(self, q, fn, r=(), w=(), stream=None):
        self.S.dma(q, fn, [t.b for t in r], [t.b for t in w], stream=stream)


class DT:
    __slots__ = ("b",)

    def __init__(self, name):
        self.b = Buf(name)


def make_consts(cx):
    c = {}
    tmp = cx.sb([128, 512], F32, "ctmp")
    identf = cx.sb([128, 128], F32, "identf")
    identb = cx.sb([128, 128], BF16, "identb")
    cx.G(lambda e: e.memset(identf[:], 1.0), w=[identf])
    cx.G(lambda e: e.affine_select(out=identf[:], in_=identf[:], pattern=[[-1, 128]], compare_op=ALU.is_equal,
                                   fill=0.0, base=0, channel_multiplier=1), r=[identf], w=[identf])
    cx.V(lambda e: e.tensor_copy(out=identb[:], in_=identf[:]), r=[identf], w=[identb])
    c["identf"], c["identb"] = identf, identb
    maskA = cx.sb([128, 128], F32, "maskA")
    cx.G(lambda e: e.memset(maskA[:], 1.0), w=[maskA])
    cx.G(lambda e: e.affine_select(out=maskA[:], in_=maskA[:], pattern=[[1, 128]], compare_op=ALU.is_ge,
                                   fill=0.0, base=0, channel_multiplier=-1), r=[maskA], w=[maskA])
    c["maskA"] = maskA
    m0 = cx.sb([128, 512], BF16, "m0")
    cx.G(lambda e: e.memset(tmp[:], 1.0), w=[tmp])
    cx.G(lambda e: e.affine_select(out=tmp[:], in_=tmp[:], pattern=[[1, 512]], compare_op=ALU.is_gt,
                                   fill=0.0, base=0, channel_multiplier=-1), r=[tmp], w=[tmp])
    cx.V(lambda e: e.tensor_copy(out=m0[:], in_=tmp[:]), r=[tmp], w=[m0])
    c["m0"] = m0
    m0p = cx.sb([128, 2, 512], BF16, "m0p")
    for hh in range(2):
        cx.V(lambda e, hh=hh: e.tensor_copy(out=m0p[:, hh, :], in_=tmp[:]), r=[tmp], w=[m0p])
    c["m0p"] = m0p
    tneg = cx.sb([128, 128], BF16, "tneg")
    onesneg = cx.sb([128, 128], BF16, "onesneg")
    tmp2 = cx.sb([128, 128], F32, "ctmp2")
    cx.G(lambda e: e.memset(tmp2[:], -1.0), w=[tmp2])
    cx.V(lambda e: e.tensor_copy(out=onesneg[:], in_=tmp2[:]), r=[tmp2], w=[onesneg])
    cx.G(lambda e: e.affine_select(out=tmp2[:], in_=tmp2[:], pattern=[[-1, 128]], compare_op=ALU.is_ge,
                                   fill=0.0, base=0, channel_multiplier=1), r=[tmp2, onesneg], w=[tmp2])
    cx.V(lambda e: e.tensor_copy(out=tneg[:], in_=tmp2[:]), r=[tmp2], w=[tneg])
    c["tneg"], c["onesneg"] = tneg, onesneg
    sel = cx.sb([2, 2, 128], F32, "sel")
    cx.G(lambda e: e.memset(sel[:], 1.0), w=[sel])
    for h in range(2):
        cx.G(lambda e, h=h: e.affine_select(out=sel[:, h, :], in_=sel[:, h, :], pattern=[[0, 128]],
                                            compare_op=ALU.is_equal, fill=0.0, base=-h, channel_multiplier=1),
             r=[sel], w=[sel])
    c["sel"] = sel
    ones2 = cx.sb([2, 512], F32, "ones2")
    cx.G(lambda e: e.memset(ones2[:], 1.0), w=[ones2])
    c["ones2"] = ones2
    return c


def load_weight_bf16(cx, W, wdram, ncols, stg, tagname):
    nk = wdram.shape[0] // 128
    for kc in range(nk):
        cx.DMA("gpsimd", lambda e, kc=kc: e.dma_start(out=W[:, kc, 0:ncols], in_=wdram[kc * 128:(kc + 1) * 128, 0:ncols]),
               w=[W], stream=f"{tagname}w{kc % 4}")


class ARHook:
    GROUPS = [[0, 1], [2, 3], [4, 5], [6, 7]]

    def __init__(self, cx, ypart_ap, ysum_ap, SL):
        self.cx, self.yp, self.ys = cx, ypart_ap, ysum_ap
        self.row_dt = [DT(f"yprow{i}") for i in range(SL // 128)]
        self.sum_dt = [DT(f"ysum{i}") for i in range(SL // 1024)]
        self.sum_half = [DT(f"ysumh{i}") for i in range(SL // TT)]

    def store_dt(self, r0):
        return self.row_dt[r0 // 128]

    def after_tile(self, Tn):
        r0 = Tn * TT
        yp, ys = self.yp, self.ys
        self.cx.S.cc("gpsimd", lambda e: e.collective_compute("AllReduce", ALU.add, replica_groups=self.GROUPS,
                                                              ins=[yp[r0:r0 + TT]], outs=[ys[r0:r0 + TT]]),
                     reads=[d.b for d in self.row_dt[Tn * 4:(Tn + 1) * 4]], writes=[self.sum_half[Tn].b], stream="cc", inc=1)
        if Tn % 2 == 1:
            self.sum_dt[Tn // 2].b.writers = list(self.sum_half[Tn - 1].b.writers) + list(self.sum_half[Tn].b.writers)


class NormStage:
    def __init__(self, cx, consts, x_ap, parts, gpost_ap, gpre_ap, xcur_ap, name="n", pbank=6, parts_dt=None, nhb=1,
                 nyt=1):
        self.cx, self.c = cx, consts
        self.parts_dt = parts_dt
        self.x_ap, self.parts, self.xcur_ap = x_ap, parts, xcur_ap
        self.xt = [cx.sb([128, D], F32, "xt") for _ in range(2)]
        self.nyt = nyt
        self.yt = [[cx.sb([128, D], F32, "yt") for _ in range(len(parts))] for _ in range(nyt)]
        self.junk = cx.sb([128, D], BF16, "junk")
        self.hb = [cx.sb([128, D], BF16, "hb") for _ in range(nhb)]
        self.st = [cx.sb([128, 4], F32, "nst") for _ in range(2)]
        self.pT = [cx.bank(pbank, BF16, 0, 512, "pT"), cx.bank(pbank + 1, BF16, 0, 512, "pT")]
        self.gpre = cx.sb([128, D], F32, "gpre")
        cx.DMA("sync", lambda e: e.dma_start(out=self.gpre[:], in_=gpre_ap.partition_broadcast(128)),
               w=[self.gpre], stream=name + "gpre")
        if parts:
            self.gpost = cx.sb([128, D], F32, "gpost")
            cx.DMA("sync", lambda e: e.dma_start(out=self.gpost[:], in_=gpost_ap.partition_broadcast(128)),
                   w=[self.gpost], stream=name + "gpost")
        self.n = 0
        self.name = name

    def load(self, r0):
        cx = self.cx
        i = self.n
        xt = self.xt[i % 2]
        cx.DMA("sync", lambda e: e.dma_start(out=xt[:], in_=self.x_ap[r0:r0 + 128, :]), w=[xt],
               stream=f"{self.name}x{i % 2}")
        return xt

    def run(self, r0, hT, j):
        hb = self.pre(r0, want_h=hT is not None)
        if hT is not None:
            self.post(hb, hT, j)

    def pre(self, r0, want_h=True):
        cx, c = self.cx, self.c
        i = self.n
        self.n += 1
        xt = self.load(r0)
        st = self.st[i % 2]
        hb = self.hb[i % len(self.hb)]
        junk = self.junk
        if self.parts:
            yts = self.yt[i % self.nyt]
            for k, p_ap in enumerate(self.parts):
                yt = yts[k]
                rdt = [self.parts_dt[k][r0 // 1024]] if self.parts_dt is not None else []
                cx.DMA("sync", lambda e, yt=yt, p_ap=p_ap: e.dma_start(out=yt[:], in_=p_ap[r0:r0 + 128, :]),
                       r=rdt, w=[yt], stream=f"{self.name}y{k}_{i % self.nyt}")
            y0 = yts[0]
            if len(self.parts) == 2:
                y1 = yts[1]
                cx.G(lambda e: e.tensor_tensor(out=y0[:], in0=y0[:], in1=y1[:], op=ALU.add), r=[y0, y1], w=[y0])
            cx.A(lambda e: e.activation(out=junk[:], in_=y0[:], func=AF.Square, accum_out=st[:, 0:1]),
                 r=[y0], w=[junk, st])
            cx.A(lambda e: e.activation(out=st[:, 1:2], in_=st[:, 0:1], func=AF.Sqrt, scale=1.0 / D, bias=EPS),
                 r=[st], w=[st])
            cx.V(lambda e: e.reciprocal(out=st[:, 1:2], in_=st[:, 1:2]), r=[st], w=[st])
            cx.V(lambda e: e.scalar_tensor_tensor(out=y0[:], in0=y0[:], scalar=st[:, 1:2], in1=self.gpost[:],
                                                  op0=ALU.mult, op1=ALU.mult), r=[y0, st, self.gpost], w=[y0])
            cx.G(lambda e: e.tensor_tensor(out=xt[:], in0=xt[:], in1=y0[:], op=ALU.add), r=[xt, y0], w=[xt])
            if self.xcur_ap is not None:
                cx.DMA("gpsimd", lambda e: e.dma_start(out=self.xcur_ap[r0:r0 + 128, :], in_=xt[:]), r=[xt],
                       stream=f"{self.name}xo{i % 2}")
        if not want_h:
            return None
        cx.A(lambda e: e.activation(out=junk[:], in_=xt[:], func=AF.Square, accum_out=st[:, 2:3]),
             r=[xt], w=[junk, st])
        cx.A(lambda e: e.activation(out=st[:, 3:4], in_=st[:, 2:3], func=AF.Sqrt, scale=1.0 / D, bias=EPS),
             r=[st], w=[st])
        cx.V(lambda e: e.reciprocal(out=st[:, 3:4], in_=st[:, 3:4]), r=[st], w=[st])
        cx.V(lambda e: e.scalar_tensor_tensor(out=hb[:], in0=xt[:], scalar=st[:, 3:4], in1=self.gpre[:],
                                              op0=ALU.mult, op1=ALU.mult), r=[xt, st, self.gpre], w=[hb])
        return hb

    def post(self, hb, hT, j):
        cx, c = self.cx, self.c
        for g in range(2):
            pT = self.pT[g]
            for k in range(4):
                cc = g * 4 + k
                cx.PE(lambda e, pT=pT, k=k, cc=cc: e.transpose(out=pT[:, k * 128:(k + 1) * 128],
                                                              in_=hb[:, cc * 128:(cc + 1) * 128],
                                                              identity=c["identb"][:]),
                      r=[hb, c["identb"]], w=[pT])
            src = pT[:].rearrange("p (c t) -> p c t", c=4)
            if g == 0:
                cx.V(lambda e, src=src, g=g: e.tensor_copy(out=hT[:, g * 4:(g + 1) * 4, j * 128:(j + 1) * 128], in_=src),
                     r=[pT], w=[hT])
            else:
                cx.A(lambda e, src=src, g=g: e.activation(out=hT[:, g * 4:(g + 1) * 4, j * 128:(j + 1) * 128], in_=src,
                                                          func=AF.Copy), r=[pT], w=[hT])


NCOL_E = 4612
C_AQ, C_AK, C_BQ, C_BK, C_BZ = 0, 512, 1024, 1536, 2048
C_TM = 2560
C_GI, C_GF = 4608, 4610


def even_pass1(cx, c, SL, x_ap, parts, gpost_ap, gpre_ap, xcur_ap, win_ap, convw_ap, bi_ap, nbf_ap, hng_ap, scr,
               scr_dt, parts_dt=None):
    S = cx.S
    NT = SL // TT
    W = cx.sb([128, 8, NCOL_E], BF16, "W")
    stg = [cx.sb([128, 1024], F32, "wstg") for _ in range(1)]
    load_weight_bf16(cx, W, win_ap, NCOL_E, stg, "w")
    convw = cx.sb([128, 8, 4], F32, "convw")
    cx.DMA("sync", lambda e: e.dma_start(out=convw[:], in_=convw_ap), w=[convw], stream="convw")
    bi = cx.sb([2, 1], F32, "bi")
    nbf = cx.sb([2, 1], F32, "nbf")
    cx.DMA("sync", lambda e: e.dma_start(out=bi[:], in_=bi_ap), w=[bi], stream="bi")
    cx.DMA("sync", lambda e: e.dma_start(out=nbf[:], in_=nbf_ap), w=[nbf], stream="nbf")
    hng = cx.sb([128, 512], F32, "hng")
    cx.DMA("sync", lambda e: e.dma_start(out=hng[:], in_=hng_ap.partition_broadcast(128)), w=[hng], stream="hng")
    cx.V(lambda e: e.tensor_scalar_mul(out=nbf[:], in0=nbf[:], scalar1=-1.0), r=[nbf], w=[nbf])

    if DBG.get("stop") == "weights":
        return
    norm = NormStage(cx, c, x_ap, parts, gpost_ap, gpre_ap, xcur_ap, "n1", parts_dt=parts_dt, nhb=4)
    hT2 = [cx.sb([128, 8, TT], BF16, "hT") for _ in range(2)]
    cv = [cx.sb([128, 3 + TT], F32, "cv") for _ in range(2)]
    halo = cx.sb([128, 8, 3], F32, "halo")
    cx.G(lambda e: e.memset(halo[:], 0.0), w=[halo])
    acc = [cx.sb([128, TT], F32, "cacc") for _ in range(2)]
    qkT = cx.sb([128, 8, TT], BF16, "qkT")
    bst = [cx.sb([128, 4, TT], BF16, "bst") for _ in range(3)]
    bvst = cx.sb([128, 4, 512], BF16, "bvst")
    vextj = [cx.sb([128, 2, 257], BF16, "vext") for _ in range(4)]
    for t in vextj:
        cx.G(lambda e, t=t: e.memset(t[:], 1.0), w=[t])
    sgoj = [cx.sb([128, 512], BF16, "sgo") for _ in range(4)]
    gzgj = [cx.sb([128, 512], BF16, "gzg") for _ in range(4)]
    C32 = [[cx.sb([128, 257], F32, "C32") for _ in range(2)] for _ in range(2)]
    Cb = [[cx.sb([128, 257], BF16, "Cb") for _ in range(2)] for _ in range(2)]
    for h in range(2):
        for d in range(2):
            cx.G(lambda e, t=C32[h][d]: e.memset(t[:], 0.0), w=[C32[h][d]])
            cx.G(lambda e, t=Cb[h][d]: e.memset(t[:], 0.0), w=[Cb[h][d]])
    Bcarry = cx.sb([2, 1], F32, "Bcarry")
    mucarry = cx.sb([2, 1], F32, "mucarry")
    cx.G(lambda e: e.memset(Bcarry[:], 0.0), w=[Bcarry])
    cx.G(lambda e: e.memset(mucarry[:], 0.0), w=[mucarry])
    g_e1 = cx.sb([2, TT], F32, "g_e1")
    g_B = cx.sb([2, TT], F32, "g_B")
    g_ag = cx.sb([2, TT], F32, "g_ag")
    g_w = cx.sb([2, TT], F32, "g_w")
    g_c = cx.sb([2, TT], F32, "g_c")
    g_s = cx.sb([2, 32], F32, "g_s")
    wc = cx.sb([128, 16], F32, "wc")
    decb = cx.sb([128, 8], F32, "decb")
    scT = [cx.sb([128, 128], BF16, "scT") for _ in range(2)]
    kp = [cx.sb([128, 256], BF16, "kp") for _ in range(2)]
    hg = [cx.sb([128, 256], F32, "hg") for _ in range(2)]
    ya = [cx.sb([128, 256], BF16, "ya") for _ in range(2)]
    cst = [cx.sb([128, 4], F32, "cst") for _ in range(2)]
    junk2 = cx.sb([128, 256], BF16, "junk2")
    yaTs = cx.sb([128, 4, TT], BF16, "yaTs")
    pj = [cx.bank(0, name="pj"), cx.bank(1, name="pj"), cx.bank(2, name="pj")]
    p_num = cx.bank(3, name="pnum")
    p_misc = cx.bank(4, name="pmisc")
    p_mbk = cx.bank(5, BF16, 0, 256, "pmbk")
    p_mby = cx.bank(6, BF16, 0, 256, "pmby")
    pjn = [0]

    def pslot():
        p = pj[pjn[0] % 3]
        pjn[0] += 1
        return p

    for j in range(4):
        norm.run(j * 128, hT2[0], j)
    def _tile(Tn):
        t0 = Tn * TT
        hT = hT2[Tn % 2]
        hbs = [norm.pre(t0 + TT + j * 128) for j in range(4)] if Tn + 1 < NT else None
        if DBG.get("stop") == "fm":
            return
        pgi = pslot()
        for kc in range(8):
            cx.PE(lambda e, kc=kc: e.matmul(pgi[0:2, :], lhsT=W[:, kc, C_GI:C_GI + 2], rhs=hT[:, kc, :],
                                            start=(kc == 0), stop=(kc == 7)), r=[W, hT], w=[pgi])
        pgf = pslot()
        for kc in range(8):
            cx.PE(lambda e, kc=kc: e.matmul(pgf[0:2, :], lhsT=W[:, kc, C_GF:C_GF + 2], rhs=hT[:, kc, :],
                                            start=(kc == 0), stop=(kc == 7)), r=[W, hT], w=[pgf])
        cx.A(lambda e: e.activation(out=g_e1[:], in_=pgf[0:2, :], func=AF.Exp, scale=-1.0, bias=nbf[:, 0:1]),
             r=[pgf, nbf], w=[g_e1])
        cx.A(lambda e: e.activation(out=g_e1[:], in_=g_e1[:], func=AF.Ln, bias=1.0), r=[g_e1], w=[g_e1])
        cx.V(lambda e: e.tensor_tensor_scan(out=g_B[:], data0=c["ones2"][:], data1=g_e1[:], initial=Bcarry[:, 0:1],
                                            op0=ALU.mult, op1=ALU.subtract), r=[c["ones2"], g_e1, Bcarry], w=[g_B])
        cx.V(lambda e: e.tensor_copy(out=Bcarry[:], in_=g_B[:, TT - 1:TT]), r=[g_B], w=[Bcarry])
        cx.V(lambda e: e.scalar_tensor_tensor(out=g_ag[:], in0=pgi[0:2, :], scalar=bi[:, 0:1], in1=g_B[:],
                                              op0=ALU.add, op1=ALU.subtract), r=[pgi, bi, g_B], w=[g_ag])
        cx.V(lambda e: e.tensor_reduce(out=g_s[:, 0:4], in_=g_ag[:].rearrange("p (c t) -> p c t", c=4),
                                       axis=AX.X, op=ALU.max), r=[g_ag], w=[g_s])
        cx.V(lambda e: e.tensor_tensor_scan(out=g_s[:, 4:8], data0=g_s[:, 0:4], data1=g_s[:, 0:4],
                                            initial=mucarry[:, 0:1], op0=ALU.max, op1=ALU.max),
             r=[g_s, mucarry], w=[g_s])
        cx.V(lambda e: e.tensor_copy(out=g_s[:, 8:9], in_=mucarry[:]), r=[mucarry, g_s], w=[g_s])
        cx.V(lambda e: e.tensor_copy(out=g_s[:, 9:12], in_=g_s[:, 4:7]), r=[g_s], w=[g_s])
        cx.V(lambda e: e.tensor_copy(out=mucarry[:], in_=g_s[:, 7:8]), r=[g_s], w=[mucarry])
        cx.V(lambda e: e.tensor_tensor(out=g_s[:, 12:16], in0=g_s[:, 8:12], in1=g_s[:, 4:8], op=ALU.subtract),
             r=[g_s], w=[g_s])
        cx.A(lambda e: e.activation(out=g_s[:, 16:20], in_=g_s[:, 12:16], func=AF.Exp), r=[g_s], w=[g_s])
        cx.V(lambda e: e.tensor_scalar(out=g_s[:, 20:24], in0=g_s[:, 4:8], scalar1=-1.0, scalar2=-LN16,
                                       op0=ALU.mult, op1=ALU.add), r=[g_s], w=[g_s])
        cx.V(lambda e: e.tensor_scalar_mul(out=g_s[:, 24:28], in0=g_s[:, 4:8], scalar1=-1.0), r=[g_s], w=[g_s])
        for j in range(4):
            cx.A(lambda e, j=j: e.activation(out=g_w[:, j * 128:(j + 1) * 128], in_=g_ag[:, j * 128:(j + 1) * 128],
                                             func=AF.Exp, bias=g_s[:, 20 + j:21 + j]), r=[g_ag, g_s], w=[g_w])
            cx.A(lambda e, j=j: e.activation(out=g_c[:, j * 128:(j + 1) * 128], in_=g_B[:, j * 128:(j + 1) * 128],
                                             func=AF.Exp, scale=-1.0, bias=g_s[:, 24 + j:25 + j]),
                 r=[g_B, g_s], w=[g_c])
        for j in range(4):
            cx.PE(lambda e, j=j: e.transpose(out=p_misc[:, 128 + 4 * j:128 + 4 * j + 2],
                                             in_=g_w[:, j * 128:(j + 1) * 128], identity=c["identf"][0:2, 0:2]),
                  r=[g_w, c["identf"]], w=[p_misc])
            cx.PE(lambda e, j=j: e.transpose(out=p_misc[:, 128 + 4 * j + 2:128 + 4 * j + 4],
                                             in_=g_c[:, j * 128:(j + 1) * 128], identity=c["identf"][0:2, 0:2]),
                  r=[g_c, c["identf"]], w=[p_misc])
        for h in range(2):
            cx.PE(lambda e, h=h: e.matmul(p_misc[:, 144 + 4 * h:148 + 4 * h], lhsT=c["sel"][:, h, :],
                                          rhs=g_s[:, 16:20], start=True, stop=True), r=[c["sel"], g_s], w=[p_misc])
        cx.V(lambda e: e.tensor_copy(out=wc[:], in_=p_misc[:, 128:144]), r=[p_misc], w=[wc])
        cx.V(lambda e: e.tensor_copy(out=decb[:], in_=p_misc[:, 144:152]), r=[p_misc], w=[decb])
        for cc in range(20):
            ps = pslot()
            for kc in range(8):
                cx.PE(lambda e, ps=ps, kc=kc, cc=cc: e.matmul(ps[:], lhsT=W[:, kc, cc * 128:(cc + 1) * 128],
                                                              rhs=hT[:, kc, :], start=(kc == 0), stop=(kc == 7)),
                      r=[W, hT], w=[ps])
            if cc < 8:
                cvt = cv[cc % 2]
                ac = acc[cc % 2]
                cx.A(lambda e, ps=ps, cvt=cvt: e.activation(out=cvt[:, 3:3 + TT], in_=ps[:], func=AF.Copy),
                     r=[ps], w=[cvt])
                cx.G(lambda e, cvt=cvt, cc=cc: e.tensor_copy(out=cvt[:, 0:3], in_=halo[:, cc, :]), r=[halo], w=[cvt])
                cx.V(lambda e, cvt=cvt, ac=ac, cc=cc: e.tensor_scalar_mul(out=ac[:], in0=cvt[:, 3:3 + TT],
                                                                         scalar1=convw[:, cc, 3:4]),
                     r=[cvt, convw], w=[ac])
                for tap in range(3):
                    cx.V(lambda e, cvt=cvt, ac=ac, cc=cc, tap=tap: e.scalar_tensor_tensor(
                        out=ac[:], in0=cvt[:, tap:tap + TT], scalar=convw[:, cc, tap:tap + 1], in1=ac[:],
                        op0=ALU.mult, op1=ALU.add), r=[cvt, convw, ac], w=[ac])
                cx.A(lambda e, ac=ac, cc=cc: e.activation(out=qkT[:, cc, :], in_=ac[:], func=AF.Silu),
                     r=[ac], w=[qkT])
                cx.G(lambda e, cvt=cvt, cc=cc: e.tensor_copy(out=halo[:, cc, :], in_=cvt[:, TT:TT + 3]), r=[cvt], w=[halo])
            else:
                g = (cc - 8) // 4
                hh = (cc - 8) % 4
                st = bst[g]
                if g == 0:
                    cx.A(lambda e, ps=ps, st=st, hh=hh: e.activation(out=st[:, hh, :], in_=ps[:], func=AF.Copy,
                                                                     scale=128.0 ** -0.5), r=[ps], w=[st])
                elif g == 1:
                    cx.V(lambda e, ps=ps, st=st, hh=hh: e.tensor_copy(out=st[:, hh, :], in_=ps[:]), r=[ps], w=[st])
                else:
                    cx.A(lambda e, ps=ps, st=st, hh=hh: e.activation(out=st[:, hh, :], in_=ps[:], func=AF.Silu),
                         r=[ps], w=[st])
                if hh == 3:
                    key = ("q", "k", "gz")[g]
                    dst = scr[key + "T"][:, :, t0:t0 + TT].rearrange("c p t -> p c t")
                    cx.S.dma("gpsimd", lambda e, dst=dst, st=st: e.dma_start(out=dst, in_=st[:]),
                             reads=[st.b], writes=[scr_dt[key][Tn].b], stream=f"bst{g}")
        if DBG.get("stop") == "gates":
            return
        def tm(j):
            for g in range(4):
                ps = pslot()
                for kc in range(8):
                    cx.PE(lambda e, ps=ps, kc=kc, g=g: e.matmul(
                        ps[:], lhsT=hT[:, kc, j * 128:(j + 1) * 128],
                        rhs=W[:, kc, C_TM + g * 512:C_TM + (g + 1) * 512], start=(kc == 0), stop=(kc == 7)),
                        r=[W, hT], w=[ps])
                if g == 0:
                    for hh in range(2):
                        cx.V(lambda e, ps=ps, hh=hh: e.tensor_copy(out=vextj[j][:, hh, 0:256],
                                                                   in_=ps[:, hh * 256:(hh + 1) * 256]),
                             r=[ps], w=[vextj[j]])
                elif g == 1:
                    cx.A(lambda e, ps=ps: e.activation(out=sgoj[j][:], in_=ps[:], func=AF.Sigmoid),
                         r=[ps], w=[sgoj[j]])
                elif g == 2:
                    cx.A(lambda e, ps=ps: e.activation(out=gzgj[j][:], in_=ps[:], func=AF.Silu),
                         r=[ps], w=[gzgj[j]])
                    cx.G(lambda e: e.tensor_tensor(out=gzgj[j][:], in0=gzgj[j][:], in1=hng[:], op=ALU.mult),
                         r=[gzgj[j], hng], w=[gzgj[j]])
                else:
                    cx.V(lambda e, ps=ps: e.tensor_copy(out=bvst[:, j, :], in_=ps[:]), r=[ps], w=[bvst])

        def unit(j, h):
            sl = slice(j * 128, (j + 1) * 128)
            n = (Tn * 4 + j) * 2 + h
            sc, kpt, hgt, yat, cs = scT[n % 2], kp[n % 2], hg[n % 2], ya[n % 2], cst[n % 2]
            vx, sg, gg = vextj[j], sgoj[j], gzgj[j]
            wcol = wc[:, 4 * j + h:4 * j + h + 1]
            ccol = wc[:, 4 * j + 2 + h:4 * j + 3 + h]
            dcol = decb[:, 4 * h + j:4 * h + j + 1]
            for d in range(2):
                cx.PE(lambda e, d=d: e.matmul(p_misc[:, 0:128], lhsT=qkT[:, 4 + 2 * h + d, sl],
                                              rhs=qkT[:, 2 * h + d, sl], start=(d == 0), stop=(d == 1)),
                      r=[qkT], w=[p_misc])
            cx.V(lambda e: e.scalar_tensor_tensor(out=sc[:], in0=p_misc[:, 0:128], scalar=wcol,
                                                  in1=c["maskA"][:], op0=ALU.mult, op1=ALU.mult),
                 r=[p_misc, wc, c["maskA"]], w=[sc])
            for d in range(2):
                cx.PE(lambda e, d=d: e.transpose(out=p_mbk[:, d * 128:(d + 1) * 128],
                                                 in_=qkT[:, 4 + 2 * h + d, sl], identity=c["identb"][:]),
                      r=[qkT, c["identb"]], w=[p_mbk])
            cx.A(lambda e: e.activation(out=kpt[:], in_=p_mbk[:, 0:256], func=AF.Copy, scale=wcol),
                 r=[p_mbk, wc], w=[kpt])
            for d in range(2):
                cx.A(lambda e, d=d: e.activation(out=Cb[h][d][:], in_=C32[h][d][:], func=AF.Copy, scale=dcol),
                     r=[C32[h][d], decb], w=[Cb[h][d]])
                cx.V(lambda e, d=d: e.tensor_scalar_mul(out=C32[h][d][:], in0=C32[h][d][:], scalar1=dcol),
                     r=[C32[h][d], decb], w=[C32[h][d]])
            yield
            cx.PE(lambda e: e.matmul(p_num[:, 0:257], lhsT=sc[:], rhs=vx[:, h, :], start=True, stop=False),
                  r=[sc, vx], w=[p_num])
            for d in range(2):
                cx.PE(lambda e, d=d: e.matmul(p_num[:, 0:257], lhsT=qkT[:, 2 * h + d, sl], rhs=Cb[h][d][:],
                                              start=False, stop=(d == 1)), r=[qkT, Cb[h][d]], w=[p_num])
            for d in range(2):
                pdc = pslot()
                cx.PE(lambda e, d=d, pdc=pdc: e.matmul(pdc[:, 0:257], lhsT=kpt[:, d * 128:(d + 1) * 128],
                                                       rhs=vx[:, h, :], start=True, stop=True),
                      r=[kpt, vx], w=[pdc])
                cx.V(lambda e, d=d, pdc=pdc: e.tensor_tensor(out=C32[h][d][:], in0=C32[h][d][:], in1=pdc[:, 0:257],
                                                             op=ALU.add), r=[C32[h][d], pdc], w=[C32[h][d]])
            cx.V(lambda e: e.tensor_copy(out=cs[:, 0:1], in_=p_num[:, 256:257]), r=[p_num], w=[cs])
            cx.V(lambda e: e.scalar_tensor_tensor(out=cs[:, 1:2], in0=cs[:, 0:1], scalar=-1.0, in1=cs[:, 0:1],
                                                  op0=ALU.mult, op1=ALU.max), r=[cs], w=[cs])
            cx.V(lambda e: e.tensor_tensor(out=cs[:, 0:1], in0=cs[:, 1:2], in1=ccol, op=ALU.max),
                 r=[cs, wc], w=[cs])
            cx.V(lambda e: e.reciprocal(out=cs[:, 1:2], in_=cs[:, 0:1]), r=[cs], w=[cs])
            cx.V(lambda e: e.scalar_tensor_tensor(out=hgt[:], in0=p_num[:, 0:256], scalar=cs[:, 1:2],
                                                  in1=sg[:, h * 256:(h + 1) * 256], op0=ALU.mult, op1=ALU.mult),
                 r=[p_num, cs, sg], w=[hgt])
            cx.A(lambda e: e.activation(out=junk2[:], in_=hgt[:], func=AF.Square, accum_out=cs[:, 2:3]),
                 r=[hgt], w=[junk2, cs])
            cx.A(lambda e: e.activation(out=cs[:, 3:4], in_=cs[:, 2:3], func=AF.Sqrt, scale=1.0 / 256, bias=EPS),
                 r=[cs], w=[cs])
            cx.V(lambda e: e.reciprocal(out=cs[:, 3:4], in_=cs[:, 3:4]), r=[cs], w=[cs])
            cx.V(lambda e: e.scalar_tensor_tensor(out=yat[:], in0=hgt[:], scalar=cs[:, 3:4],
                                                  in1=gg[:, h * 256:(h + 1) * 256], op0=ALU.mult, op1=ALU.mult),
                 r=[hgt, cs, gg], w=[yat])
            yield
            for d in range(2):
                cx.PE(lambda e, d=d: e.transpose(out=p_mby[:, d * 128:(d + 1) * 128],
                                                 in_=yat[:, d * 128:(d + 1) * 128], identity=c["identb"][:]),
                      r=[yat, c["identb"]], w=[p_mby])
            for d in range(2):
                cx.A(lambda e, d=d: e.activation(out=yaTs[:, 2 * h + d, sl], in_=p_mby[:, d * 128:(d + 1) * 128],
                                                 func=AF.Copy), r=[p_mby], w=[yaTs])
            yield

        units = [unit(j, h) for j in range(4) for h in range(2)]
        for step in range(8 + 2):
            for k in (0, 1, 2):
                u = step - k
                if 0 <= u < 8:
                    if k == 0 and u % 2 == 0:
                        tm(u // 2)
                    next(units[u])
        dstv = scr["vB"][t0:t0 + TT, :].rearrange("(j p) c -> p j c", p=128)
        cx.S.dma("gpsimd", lambda e, dstv=dstv: e.dma_start(out=dstv, in_=bvst[:]), reads=[bvst.b],
                 writes=[scr_dt["v"][Tn].b], stream="bvst")
        dsty = scr["yaT"][:, :, t0:t0 + TT].rearrange("c p t -> p c t")
        cx.S.dma("gpsimd", lambda e, dsty=dsty: e.dma_start(out=dsty, in_=yaTs[:]), reads=[yaTs.b],
                 writes=[scr_dt["ya"][Tn].b], stream="yaTs")
        if hbs is not None:
            for j in range(4):
                norm.post(hbs[j], hT2[(Tn + 1) % 2], j)

    for Tn in range(NT):
        _tile(Tn)


def even_pass2(cx, c, SL, wout_ap, scr, scr_dt, ypart_ap, pipe=True, ar=None):
    S = cx.S
    NT = SL // TT
    NB = SL // 128
    Wo = cx.sb([128, 8, D], BF16, "Wo")
    load_weight_bf16(cx, Wo, wout_ap, D, None, "wo")
    Kc = cx.sb([128, 4, SL], BF16, "Kc")
    Vc = cx.sb([128, NB, 512], BF16, "Vc")
    qt = [cx.sb([128, 4, TT], BF16, "qt") for _ in range(2)]
    gzt = cx.sb([128, 4, TT], BF16, "gzt")
    yat = cx.sb([128, 4, TT], BF16, "yat")
    ybT = cx.sb([128, 4, TT], BF16, "ybT")
    E = [cx.sb([128, 2, TT], BF16, "E") for _ in range(3)]
    sp = [cx.sb([128, 2, TT], BF16, "sp") for _ in range(2)]
    Xr = [cx.sb([128, 2, TT], BF16, "X") for _ in range(2)]
    At = [cx.sb([128, 2, TT], BF16, "At") for _ in range(2)]
    Rb = cx.sb([128, 2, TT], BF16, "Rb")
    yo = [cx.sb([128, D], F32, "yo") for _ in range(1)]
    zz = cx.bank2(0, "zz")
    zh = [cx.bank(0, name="zh"), cx.bank(1, name="zh")]
    cc2 = [cx.bank2(1, "cc"), cx.bank2(2, "cc")]
    ch = [[cx.bank(2, name="ch"), cx.bank(3, name="ch")], [cx.bank(4, name="ch"), cx.bank(5, name="ch")]]
    oph = [cx.bank(6, name="ops"), cx.bank(7, name="ops")]
    pps = zh
    m0p = c["m0p"]
    loaded = set()

    def ensure_loaded(Tn):
        if Tn in loaded:
            return
        loaded.add(Tn)
        t0 = Tn * TT
        cx.DMA("sync", lambda e: e.dma_start(out=Kc[:, :, t0:t0 + TT],
                                             in_=scr["kT"][:, :, t0:t0 + TT].rearrange("c p t -> p c t")),
               r=[scr_dt["k"][Tn]], w=[Kc], stream="kc")
        cx.DMA("sync", lambda e: e.dma_start(out=Vc[:, Tn * 4:Tn * 4 + 4, :],
                                             in_=scr["vB"][t0:t0 + TT, :].rearrange("(j p) c -> p j c", p=128)),
               r=[scr_dt["v"][Tn]], w=[Vc], stream="vc")
        q = qt[Tn % 2]
        cx.DMA("sync", lambda e: e.dma_start(out=q[:], in_=scr["qT"][:, :, t0:t0 + TT].rearrange("c p t -> p c t")),
               r=[scr_dt["q"][Tn]], w=[q], stream=f"qt{Tn % 2}")

    def load_late(Tn):
        t0 = Tn * TT
        cx.DMA("sync", lambda e: e.dma_start(out=gzt[:], in_=scr["gzT"][:, :, t0:t0 + TT].rearrange("c p t -> p c t")),
               r=[scr_dt["gz"][Tn]], w=[gzt], stream="gzt")
        cx.DMA("sync", lambda e: e.dma_start(out=yat[:], in_=scr["yaT"][:, :, t0:t0 + TT].rearrange("c p t -> p c t")),
               r=[scr_dt["ya"][Tn]], w=[yat], stream="yat")

    def s_z(blk):
        n, Tn, hp, kb, first, last, q0 = blk
        ensure_loaded(Tn)
        q = qt[Tn % 2]
        fr = slice(q0, TT)
        for hh in range(2):
            h = 2 * hp + hh
            cx.PE(lambda e, hh=hh, h=h: e.matmul(zh[hh][:, fr], lhsT=Kc[:, h, kb * 128:(kb + 1) * 128], rhs=q[:, h, fr],
                                                 start=True, stop=True), r=[Kc, q], w=[zh[hh]])

    def s_E(blk):
        n, Tn, hp, kb, first, last, q0 = blk
        Et, spt = E[n % 3], sp[n % 2]
        fr = slice(q0, TT)
        cx.A(lambda e: e.activation(out=Et[:, :, fr], in_=zz[:, :, fr], func=AF.Exp), r=[zz], w=[Et])
        cx.A(lambda e: e.activation(out=spt[:, :, fr], in_=Et[:, :, fr], func=AF.Ln, bias=1.0), r=[Et], w=[spt])
        if kb >= Tn * 4:
            cx.G(lambda e: e.tensor_tensor(out=spt[:, :, fr], in0=spt[:, :, fr], in1=m0p[:, :, 0:TT - q0], op=ALU.mult),
                 r=[spt, m0p], w=[spt])

    def s_cum(blk):
        n, Tn, hp, kb, first, last, q0 = blk
        spt, xt_ = sp[n % 2], Xr[n % 2]
        cpair, chh = cc2[n % 2], ch[n % 2]
        fr = slice(q0, TT)
        for hh in range(2):
            cx.PE(lambda e, hh=hh: e.matmul(chh[hh][:, fr], lhsT=c["tneg"][:], rhs=spt[:, hh, fr], start=True, stop=first),
                  r=[c["tneg"], spt], w=[chh[hh]])
            if not first:
                cx.PE(lambda e, hh=hh: e.matmul(chh[hh][:, fr], lhsT=c["onesneg"][:], rhs=Rb[:, hh, fr], start=False, stop=True),
                      r=[c["onesneg"], Rb], w=[chh[hh]])
        cx.A(lambda e: e.activation(out=xt_[:, :, fr], in_=cpair[:, :, fr], func=AF.Exp), r=[cpair], w=[xt_])
        if first:
            cx.G(lambda e: e.memset(Rb[:], 0.0), w=[Rb])
        if not last:
            cx.G(lambda e: e.tensor_tensor(out=Rb[:, :, fr], in0=Rb[:, :, fr], in1=spt[:, :, fr], op=ALU.add),
                 r=[Rb, spt], w=[Rb])

    def s_fin(blk):
        n, Tn, hp, kb, first, last, q0 = blk
        Et, xt_, at = E[n % 3], Xr[n % 2], At[n % 2]
        fr = slice(q0, TT)
        cx.V(lambda e: e.tensor_tensor(out=at[:, :, fr], in0=Et[:, :, fr], in1=xt_[:, :, fr], op=ALU.mult),
             r=[Et, xt_], w=[at])
        if kb >= Tn * 4:
            cx.V(lambda e: e.tensor_tensor(out=at[:, :, fr], in0=at[:, :, fr], in1=m0p[:, :, 0:TT - q0], op=ALU.mult),
                 r=[at, m0p], w=[at])
        for hh in range(2):
            h = 2 * hp + hh
            cx.PE(lambda e, hh=hh, h=h: e.matmul(oph[hh][:, fr], lhsT=Vc[:, kb, h * 128:(h + 1) * 128], rhs=at[:, hh, fr],
                                                 start=first, stop=last, skip_group_check=True), r=[Vc, at], w=[oph[hh]])
        if last:
            for hh in range(2):
                h = 2 * hp + hh
                cx.V(lambda e, hh=hh, h=h: e.tensor_tensor(out=ybT[:, h, :], in0=oph[hh][:], in1=gzt[:, h, :], op=ALU.mult),
                     r=[oph[hh], gzt], w=[ybT])

    def outproj(Tn):
        t0 = Tn * TT
        for j in range(4):
            yot = yo[0]
            for half in range(2):
                pp = pps[half]
                for kc in range(8):
                    src = yat if kc < 4 else ybT
                    cx.PE(lambda e, pp=pp, kc=kc, src=src, j=j, half=half: e.matmul(
                        pp[:], lhsT=src[:, kc % 4, j * 128:(j + 1) * 128], rhs=Wo[:, kc, half * 512:(half + 1) * 512],
                        start=(kc == 0), stop=(kc == 7)), r=[src, Wo], w=[pp])
                if half == 0:
                    cx.V(lambda e, pp=pp, yot=yot: e.tensor_copy(out=yot[:, 0:512], in_=pp[:]), r=[pp], w=[yot])
                else:
                    cx.A(lambda e, pp=pp, yot=yot: e.activation(out=yot[:, 512:1024], in_=pp[:], func=AF.Copy),
                         r=[pp], w=[yot])
            cx.DMA("gpsimd", lambda e, yot=yot, j=j, t0=t0: e.dma_start(out=ypart_ap[t0 + j * 128:t0 + (j + 1) * 128, :],
                                                                       in_=yot[:]), r=[yot],
                   w=([ar.store_dt(t0 + j * 128)] if ar is not None else []), stream="yo0")
        if ar is not None:
            ar.after_tile(Tn)

    blocks = []
    n = 0
    for Tn in range(NT):
        for hp in range(2):
            kbs = list(range(Tn * 4 + 3, -1, -1))
            for i, kb in enumerate(kbs):
                j = kb - Tn * 4
                q0 = 128 * j if j > 0 else 0
                blocks.append((n, Tn, hp, kb, i == 0, i == len(kbs) - 1, q0))
                n += 1
    if DBG.get("stop") == "p2load":
        return
    NBk = len(blocks)
    late_done = set()
    for t in range(-3, NBk):
        if 0 <= t + 2 < NBk:
            s_E(blocks[t + 2])
        if 0 <= t + 1 < NBk:
            s_cum(blocks[t + 1])
        if 0 <= t < NBk:
            blk = blocks[t]
            if blk[1] not in late_done:
                late_done.add(blk[1])
                load_late(blk[1])
            s_fin(blk)
            if t + 1 == NBk or blocks[t + 1][1] != blk[1]:
                outproj(blk[1])
        if 0 <= t + 3 < NBk:
            s_z(blocks[t + 3])


def even_pass2_old(cx, c, SL, wout_ap, scr, scr_dt, ypart_ap, pipe=True, ar=None):
    S = cx.S
    NT = SL // TT
    NB = SL // 128
    Wo = cx.sb([128, 8, D], BF16, "Wo")
    stg = [cx.sb([128, 1024], F32, "wstg") for _ in range(2)]
    load_weight_bf16(cx, Wo, wout_ap, D, stg, "wo")
    Kc = cx.sb([128, 4, SL], BF16, "Kc")
    Vc = cx.sb([128, NB, 512], BF16, "Vc")
    qt = [cx.sb([128, 4, TT], BF16, "qt") for _ in range(2)]
    gzt = cx.sb([128, 4, TT], BF16, "gzt")
    yat = cx.sb([128, 4, TT], BF16, "yat")
    ybT = cx.sb([128, 4, TT], BF16, "ybT")
    NBUF = 3
    E = [cx.sb([128, TT], BF16, "E") for _ in range(4)]
    sp = [cx.sb([128, TT], BF16, "sp") for _ in range(NBUF)]
    At = [cx.sb([128, TT], BF16, "At") for _ in range(2)]
    Rb = cx.sb([128, TT], BF16, "Rb")
    yo = [cx.sb([128, D], F32, "yo") for _ in range(2)]
    zps = [cx.bank(0, name="zps"), cx.bank(1, name="zps")]
    cps = [cx.bank(2, name="cps"), cx.bank(3, name="cps")]
    ops_ = [cx.bank(4, name="ops"), cx.bank(5, name="ops")]
    pps = [cx.bank(6, name="pps"), cx.bank(7, name="pps")]
    m0 = c["m0"]

    Xr = [cx.sb([128, TT], BF16, "X") for _ in range(3)]
    loaded = set()

    def ensure_loaded(Tn):
        if Tn in loaded:
            return
        loaded.add(Tn)
        t0 = Tn * TT
        cx.DMA("sync", lambda e: e.dma_start(out=Kc[:, :, t0:t0 + TT],
                                             in_=scr["kT"][:, :, t0:t0 + TT].rearrange("c p t -> p c t")),
               r=[scr_dt["k"][Tn]], w=[Kc], stream="kc")
        cx.DMA("sync", lambda e: e.dma_start(out=Vc[:, Tn * 4:Tn * 4 + 4, :],
                                             in_=scr["vB"][t0:t0 + TT, :].rearrange("(j p) c -> p j c", p=128)),
               r=[scr_dt["v"][Tn]], w=[Vc], stream="vc")
        q = qt[Tn % 2]
        cx.DMA("sync", lambda e: e.dma_start(out=q[:], in_=scr["qT"][:, :, t0:t0 + TT].rearrange("c p t -> p c t")),
               r=[scr_dt["q"][Tn]], w=[q], stream=f"qt{Tn % 2}")

    def load_late(Tn):
        t0 = Tn * TT
        cx.DMA("sync", lambda e: e.dma_start(out=gzt[:], in_=scr["gzT"][:, :, t0:t0 + TT].rearrange("c p t -> p c t")),
               r=[scr_dt["gz"][Tn]], w=[gzt], stream="gzt")
        cx.DMA("sync", lambda e: e.dma_start(out=yat[:], in_=scr["yaT"][:, :, t0:t0 + TT].rearrange("c p t -> p c t")),
               r=[scr_dt["ya"][Tn]], w=[yat], stream="yat")

    def s_z(blk):
        n, Tn, h, kb, first, last, q0 = blk
        ensure_loaded(Tn)
        z = zps[n % 2]
        q = qt[Tn % 2]
        fr = slice(q0, TT)
        cx.PE(lambda e: e.matmul(z[:, fr], lhsT=Kc[:, h, kb * 128:(kb + 1) * 128], rhs=q[:, h, fr],
                                 start=True, stop=True), r=[Kc, q], w=[z])

    def s_E(blk):
        n, Tn, h, kb, first, last, q0 = blk
        z = zps[n % 2]
        Et = E[n % 4]
        fr = slice(q0, TT)
        cx.A(lambda e: e.activation(out=Et[:, fr], in_=z[:, fr], func=AF.Exp), r=[z], w=[Et])

    def s_ln(blk):
        n, Tn, h, kb, first, last, q0 = blk
        Et, spt = E[n % 4], sp[n % NBUF]
        fr = slice(q0, TT)
        cx.A(lambda e: e.activation(out=spt[:, fr], in_=Et[:, fr], func=AF.Ln, bias=1.0), r=[Et], w=[spt])
        if kb >= Tn * 4:
            cx.G(lambda e: e.tensor_tensor(out=spt[:, fr], in0=spt[:, fr], in1=m0[:, 0:TT - q0], op=ALU.mult),
                 r=[spt, m0], w=[spt])

    def s_cum(blk):
        n, Tn, h, kb, first, last, q0 = blk
        cp = cps[n % 2]
        spt = sp[n % NBUF]
        fr = slice(q0, TT)
        cx.PE(lambda e: e.matmul(cp[:, fr], lhsT=c["tneg"][:], rhs=spt[:, fr], start=True, stop=first),
              r=[c["tneg"], spt], w=[cp])
        if not first:
            cx.PE(lambda e: e.matmul(cp[:, fr], lhsT=c["onesneg"][:], rhs=Rb[:, fr], start=False, stop=True),
                  r=[c["onesneg"], Rb], w=[cp])

    def s_X(blk):
        n, Tn, h, kb, first, last, q0 = blk
        cp = cps[n % 2]
        spt, xt_ = sp[n % NBUF], Xr[n % 3]
        fr = slice(q0, TT)
        cx.A(lambda e: e.activation(out=xt_[:, fr], in_=cp[:, fr], func=AF.Exp), r=[cp], w=[xt_])
        if first:
            cx.G(lambda e: e.memset(Rb[:], 0.0), w=[Rb])
        if not last:
            cx.G(lambda e: e.tensor_tensor(out=Rb[:, fr], in0=Rb[:, fr], in1=spt[:, fr], op=ALU.add),
                 r=[Rb, spt], w=[Rb])

    def s_fin(blk):
        n, Tn, h, kb, first, last, q0 = blk
        Et, xt_, at = E[n % 4], Xr[n % 3], At[n % 2]
        g = Tn * 4 + h
        op_ = ops_[g % 2]
        fr = slice(q0, TT)
        cx.V(lambda e: e.tensor_tensor(out=at[:, fr], in0=Et[:, fr], in1=xt_[:, fr], op=ALU.mult),
             r=[Et, xt_], w=[at])
        if kb >= Tn * 4:
            cx.V(lambda e: e.tensor_tensor(out=at[:, fr], in0=at[:, fr], in1=m0[:, 0:TT - q0], op=ALU.mult),
                 r=[at, m0], w=[at])
        cx.PE(lambda e: e.matmul(op_[:, fr], lhsT=Vc[:, kb, h * 128:(h + 1) * 128], rhs=at[:, fr],
                                 start=first, stop=last, skip_group_check=True), r=[Vc, at], w=[op_])
        if last:
            cx.V(lambda e: e.tensor_tensor(out=ybT[:, h, :], in0=op_[:], in1=gzt[:, h, :], op=ALU.mult),
                 r=[op_, gzt], w=[ybT])

    def outproj(Tn):
        t0 = Tn * TT
        for j in range(4):
            yot = yo[j % 2]
            for half in range(2):
                pp = pps[half]
                for kc in range(8):
                    src = yat if kc < 4 else ybT
                    cx.PE(lambda e, pp=pp, kc=kc, src=src, j=j, half=half: e.matmul(
                        pp[:], lhsT=src[:, kc % 4, j * 128:(j + 1) * 128], rhs=Wo[:, kc, half * 512:(half + 1) * 512],
                        start=(kc == 0), stop=(kc == 7)), r=[src, Wo], w=[pp])
                if half == 0:
                    cx.V(lambda e, pp=pp, yot=yot: e.tensor_copy(out=yot[:, 0:512], in_=pp[:]), r=[pp], w=[yot])
                else:
                    cx.A(lambda e, pp=pp, yot=yot: e.activation(out=yot[:, 512:1024], in_=pp[:], func=AF.Copy),
                         r=[pp], w=[yot])
            cx.DMA("gpsimd", lambda e, yot=yot, j=j, t0=t0: e.dma_start(out=ypart_ap[t0 + j * 128:t0 + (j + 1) * 128, :],
                                                                       in_=yot[:]), r=[yot],
                   w=([ar.store_dt(t0 + j * 128)] if ar is not None else []), stream=f"yo{j % 2}")
        if ar is not None:
            ar.after_tile(Tn)

    blocks = []
    n = 0
    for Tn in range(NT):
        for h in range(4):
            kbs = list(range(Tn * 4 + 3, -1, -1))
            for i, kb in enumerate(kbs):
                j = kb - Tn * 4
                q0 = 128 * j if j > 0 else 0
                blocks.append((n, Tn, h, kb, i == 0, i == len(kbs) - 1, q0))
                n += 1
    if DBG.get("stop") == "p2load":
        return
    NBk = len(blocks)
    late_done = set()
    for t in range(-3, NBk):
        if 0 <= t + 3 < NBk:
            s_z(blocks[t + 3])
        if 0 <= t + 2 < NBk:
            s_E(blocks[t + 2])
            s_ln(blocks[t + 2])
        if 0 <= t + 1 < NBk:
            s_cum(blocks[t + 1])
            s_X(blocks[t + 1])
        if 0 <= t < NBk:
            blk = blocks[t]
            if blk[1] not in late_done:
                late_done.add(blk[1])
                load_late(blk[1])
            s_fin(blk)
            if t + 1 == NBk or blocks[t + 1][1] != blk[1]:
                outproj(blk[1])


def _even_scratch(nc, SL):
    scr = {}
    for k in ("qT", "kT", "gzT", "yaT"):
        scr[k] = nc.dram_tensor("scr_" + k, [4, 128, SL], BF16).ap()
    scr["vB"] = nc.dram_tensor("scr_vB", [SL, 512], BF16).ap()
    scr_dt = {k: [DT(f"{k}{t}") for t in range(SL // TT)] for k in ("q", "k", "gz", "v", "ya")}
    return scr, scr_dt


def build_even(SL, n_parts, pipe=True):
    nc = bass.Bass("TRN2", target_bir_lowering=False)
    inp = lambda name, shape: nc.dram_tensor(name, list(shape), F32, kind="ExternalInput").ap()
    x = inp("x", [SL, D])
    parts = [inp(f"yp{k}", [SL, D]) for k in range(n_parts)]
    gpost = inp("gpost", [D]) if n_parts else None
    gpre = inp("gpre", [D])
    win = inp("win", [D, NCOL_E])
    convw = inp("convw", [128, 8, 4])
    bi = inp("bi", [2, 1])
    bf = inp("bf", [2, 1])
    hng = inp("hng", [512])
    wout = inp("wout", [D, D])
    xcur = nc.dram_tensor("xcur", [SL, D], F32, kind="ExternalOutput").ap() if n_parts else None
    ypart = nc.dram_tensor("ypart", [SL, D], F32, kind="ExternalOutput").ap()
    scr, scr_dt = _even_scratch(nc, SL)
    S = Sched(nc)
    cx = Ctx(nc, S)
    c = make_consts(cx)
    mark = cx.off
    even_pass1(cx, c, SL, x, parts, gpost, gpre, xcur, win, convw, bi, bf, hng, scr, scr_dt)
    if DBG.get("stop") is None or DBG.get("stop").startswith("p2"):
        S.barrier()
        cx.off = mark
        even_pass2(cx, c, SL, wout, scr, scr_dt, ypart, pipe=pipe)
    S.finish()
    return nc, S


def prep_even(inp, e, layer, c):
    w = np.asarray(inp["w_in_ab"][e])
    A = lambda g: w[:, g * 1024 + 512 * c: g * 1024 + 512 * c + 512]
    Bq = lambda g: w[:, 5128 + g * 1024 + 512 * c: 5128 + g * 1024 + 512 * c + 512]
    gi = w[:, 5120 + 2 * c: 5122 + 2 * c]
    gf = w[:, 5124 + 2 * c: 5126 + 2 * c]
    win = np.concatenate([A(0), A(1), Bq(0), Bq(1), Bq(3), A(2), A(3), A(4), Bq(2), gi, gf], axis=1)
    cq = np.asarray(inp["conv_qk"][e])
    cols = np.concatenate([cq[:, 512 * c:512 * c + 512], cq[:, 1024 + 512 * c:1024 + 512 * c + 512]], axis=1)
    convw = np.ascontiguousarray(cols.reshape(4, 8, 128).transpose(2, 1, 0))
    wo = np.asarray(inp["w_out_ab"][e])
    wout = np.concatenate([wo[512 * c:512 * c + 512], wo[1024 + 512 * c:1024 + 512 * c + 512]], axis=0)
    d = dict(
        gpre=np.ascontiguousarray(inp["pre_norm_g"][layer]),
        win=np.ascontiguousarray(win), convw=convw,
        bi=np.ascontiguousarray(np.asarray(inp["bias_i"][e])[2 * c:2 * c + 2].reshape(2, 1)),
        bf=np.ascontiguousarray(np.asarray(inp["bias_f"][e])[2 * c:2 * c + 2].reshape(2, 1)),
        hng=np.ascontiguousarray(np.asarray(inp["head_norm_g"][e])[512 * c:512 * c + 512]),
        wout=np.ascontiguousarray(wout),
    )
    if layer > 0:
        d["gpost"] = np.ascontiguousarray(inp["post_norm_g"][layer - 1])
    return {k: np.asarray(v, dtype=np.float32) for k, v in d.items()}


POOL_WINDOWS = (2, 4, 8, 16)
HALO = 16
SLOT_W = ((2, 4), (16, 8))


def odd_pass(cx, c, SL, wsel_ap, x_ap, parts, gpost_ap, gpre_ap, xcur_ap, win_ap, pw_ap, pscale_ap, wout_ap, ypart_ap,
             ar=None, parts_dt=None):
    NT = SL // TT
    Wc = cx.sb([128, 8, 2048], BF16, "Wc")
    stg = [cx.sb([128, 1024], F32, "wstg") for _ in range(2)]
    load_weight_bf16(cx, Wc, win_ap, 2048, stg, "wc")
    PW = cx.sb([128, 8, 512], BF16, "PW")
    load_weight_bf16(cx, PW, pw_ap, 512, stg, "pw")
    Wo = cx.sb([128, 8, D], BF16, "Wo")
    load_weight_bf16(cx, Wo, wout_ap, D, stg, "wo")
    pscale = cx.sb([128, 8], F32, "pscale")
    cx.DMA("sync", lambda e: e.dma_start(out=pscale[:], in_=pscale_ap), w=[pscale], stream="pscale")
    wsel = cx.sb([128, 4], F32, "wsel")
    cx.DMA("sync", lambda e: e.dma_start(out=wsel[:], in_=wsel_ap), w=[wsel], stream="wsel")
    kco = cx.sb([128, 4], F32, "kco")
    invc0 = []
    for si in range(2):
        for wi in range(2):
            w = SLOT_W[si][wi]
            col = 2 * si + wi
            cx.V(lambda e, col=col, w=w: e.tensor_scalar_mul(out=kco[:, col:col + 1], in0=wsel[:, col:col + 1],
                                                             scalar1=1.0 / w), r=[wsel], w=[kco])
            t = cx.sb([128, TT], F32, "invc0")
            cx.G(lambda e, t=t, w=w: e.memset(t[:], 1.0 / w), w=[t])
            for k in range(w - 1):
                cx.G(lambda e, t=t, k=k: e.memset(t[:, k:k + 1], 1.0 / (k + 1)), w=[t])
            cx.V(lambda e, t=t, col=col: e.tensor_scalar_mul(out=t[:], in0=t[:], scalar1=wsel[:, col:col + 1]),
                 r=[t, wsel], w=[t])
            invc0.append(t)
    norm = NormStage(cx, c, x_ap, parts, gpost_ap, gpre_ap, xcur_ap, "no", parts_dt=parts_dt, nhb=4, nyt=2)
    hT2 = [cx.sb([128, 8, TT], BF16, "hT") for _ in range(3)]
    pb = [cx.sb([128, HALO + TT], F32, "pb") for _ in range(8)]
    for t in pb:
        cx.G(lambda e, t=t: e.memset(t[:, 0:HALO], 0.0), w=[t])
    LV = [cx.sb([128, HALO + TT], F32, "lv") for _ in range(4)]
    tC = cx.sb([128, TT], F32, "tC")
    tD = cx.sb([128, TT], F32, "tD")
    pl2 = [cx.sb([128, 8, TT], BF16, "pl") for _ in range(2)]
    gz = [cx.sb([128, TT], BF16, "gz") for _ in range(8)]
    yT = cx.sb([128, 8, TT], BF16, "yT")
    yo = [cx.sb([128, D], F32, "yo") for _ in range(1)]
    pj = [cx.bank(0, name="pj"), cx.bank(1, name="pj"), cx.bank(2, name="pj")]
    pps = [cx.bank(3, name="pps"), cx.bank(4, name="pps")]
    pjn = [0]

    def pslot():
        p = pj[pjn[0] % 3]
        pjn[0] += 1
        return p

    W_ = HALO + TT
    for tt_ in range(min(2, NT)):
        for j in range(4):
            norm.run(tt_ * TT + j * 128, hT2[tt_], j)

    def stageA(Tn):
        hT = hT2[Tn % 3]
        pl = pl2[Tn % 2]
        for cc in range(8):
            gi = cc // 4
            ps = pslot()
            for kc in range(8):
                cx.PE(lambda e, ps=ps, kc=kc, cc=cc: e.matmul(ps[:], lhsT=Wc[:, kc, cc * 128:(cc + 1) * 128],
                                                              rhs=hT[:, kc, :], start=(kc == 0), stop=(kc == 7)),
                      r=[Wc, hT], w=[ps])
            pbt = pb[cc]
            cx.A(lambda e, ps=ps, pbt=pbt: e.activation(out=pbt[:, HALO:W_], in_=ps[:], func=AF.Copy), r=[ps], w=[pbt])
            w1, w2 = SLOT_W[gi]
            wmax = max(w1, w2)
            src, sh, lo, li = pbt, 1, 1, 0
            while sh < wmax:
                dst = LV[li]
                eng = cx.V if (li % 2 == 0) else cx.G
                eng(lambda e, src=src, dst=dst, sh=sh, lo=lo: e.tensor_tensor(out=dst[:, lo:W_], in0=src[:, lo:W_],
                                                                              in1=src[:, lo - sh:W_ - sh], op=ALU.add),
                    r=[src], w=[dst])
                src = dst
                sh *= 2
                lo += sh
                li += 1
            s1 = LV[int(math.log2(w1)) - 1]
            s2_ = LV[int(math.log2(w2)) - 1]
            c1, c2 = 2 * gi, 2 * gi + 1
            if Tn == 0:
                cx.V(lambda e, s1=s1, c1=c1: e.tensor_tensor(out=tC[:], in0=s1[:, HALO:W_], in1=invc0[c1][:], op=ALU.mult),
                     r=[s1, invc0[c1]], w=[tC])
                cx.G(lambda e, s2_=s2_, c2=c2: e.tensor_tensor(out=tD[:], in0=s2_[:, HALO:W_], in1=invc0[c2][:], op=ALU.mult),
                     r=[s2_, invc0[c2]], w=[tD])
                cx.V(lambda e: e.tensor_tensor(out=tC[:], in0=tC[:], in1=tD[:], op=ALU.add), r=[tC, tD], w=[tC])
            else:
                cx.V(lambda e, s1=s1, c1=c1, pbt=pbt: e.scalar_tensor_tensor(out=tC[:], in0=s1[:, HALO:W_], scalar=kco[:, c1:c1 + 1],
                                                                             in1=pbt[:, HALO:W_], op0=ALU.mult, op1=ALU.subtract),
                     r=[s1, kco, pbt], w=[tC])
                cx.V(lambda e, s2_=s2_, c2=c2, cc=cc: e.scalar_tensor_tensor(out=pl[:, cc, :], in0=s2_[:, HALO:W_], scalar=kco[:, c2:c2 + 1],
                                                                             in1=tC[:], op0=ALU.mult, op1=ALU.add),
                     r=[s2_, kco, tC], w=[pl])
            if Tn == 0:
                cx.V(lambda e, pbt=pbt, cc=cc: e.tensor_tensor(out=pl[:, cc, :], in0=tC[:], in1=pbt[:, HALO:W_],
                                                               op=ALU.subtract), r=[tC, pbt], w=[pl])
            cx.G(lambda e, pbt=pbt: e.tensor_copy(out=pbt[:, 0:HALO], in_=pbt[:, TT:W_]), r=[pbt], w=[pbt])

    def stageB(Tn):
        t0 = Tn * TT
        hT = hT2[Tn % 3]
        pl = pl2[Tn % 2]
        for cc in range(8):
            pz = pslot()
            for kc in range(8):
                cx.PE(lambda e, pz=pz, kc=kc, cc=cc: e.matmul(pz[:], lhsT=Wc[:, kc, 1024 + cc * 128:1024 + (cc + 1) * 128],
                                                              rhs=hT[:, kc, :], start=(kc == 0), stop=(kc == 7)),
                      r=[Wc, hT], w=[pz])
            gzt = gz[cc]
            cx.A(lambda e, pz=pz, gzt=gzt: e.activation(out=gzt[:], in_=pz[:], func=AF.Silu), r=[pz], w=[gzt])
        for cc in range(8):
            gi, ec = cc // 4, cc % 4
            gzt = gz[cc]
            pm = pslot()
            for kc in range(4):
                cx.PE(lambda e, pm=pm, kc=kc, gi=gi, ec=ec: e.matmul(pm[:], lhsT=PW[:, gi * 4 + kc, ec * 128:(ec + 1) * 128],
                                                                      rhs=pl[:, gi * 4 + kc, :], start=(kc == 0), stop=(kc == 3)),
                      r=[PW, pl], w=[pm])
            cx.V(lambda e, pm=pm, gzt=gzt, cc=cc: e.scalar_tensor_tensor(out=yT[:, cc, :], in0=pm[:], scalar=pscale[:, cc:cc + 1],
                                                                         in1=gzt[:], op0=ALU.mult, op1=ALU.mult),
                 r=[pm, pscale, gzt], w=[yT])
        for j in range(4):
            yot = yo[0]
            for half in range(2):
                pp = pps[half]
                for kc in range(8):
                    cx.PE(lambda e, pp=pp, kc=kc, j=j, half=half: e.matmul(
                        pp[:], lhsT=yT[:, kc, j * 128:(j + 1) * 128], rhs=Wo[:, kc, half * 512:(half + 1) * 512],
                        start=(kc == 0), stop=(kc == 7)), r=[yT, Wo], w=[pp])
                if half == 0:
                    cx.V(lambda e, pp=pp, yot=yot: e.tensor_copy(out=yot[:, 0:512], in_=pp[:]), r=[pp], w=[yot])
                else:
                    cx.A(lambda e, pp=pp, yot=yot: e.activation(out=yot[:, 512:1024], in_=pp[:], func=AF.Copy),
                         r=[pp], w=[yot])
            cx.DMA("gpsimd", lambda e, yot=yot, j=j, t0=t0: e.dma_start(out=ypart_ap[t0 + j * 128:t0 + (j + 1) * 128, :],
                                                                       in_=yot[:]), r=[yot],
                   w=([ar.store_dt(t0 + j * 128)] if ar is not None else []), stream="yo0")
        if ar is not None:
            ar.after_tile(Tn)

    stageA(0)
    for Tn in range(NT):
        hbs = [norm.pre((Tn + 2) * TT + j * 128) for j in range(4)] if Tn + 2 < NT else None
        if Tn + 1 < NT:
            stageA(Tn + 1)
        stageB(Tn)
        if hbs is not None:
            for j in range(4):
                norm.post(hbs[j], hT2[(Tn + 2) % 3], j)


def build_odd(SL, n_parts):
    nc = bass.Bass("TRN2", target_bir_lowering=False)
    inp = lambda name, shape: nc.dram_tensor(name, list(shape), F32, kind="ExternalInput").ap()
    x = inp("x", [SL, D])
    parts = [inp(f"yp{k}", [SL, D]) for k in range(n_parts)]
    gpost = inp("gpost", [D]) if n_parts else None
    gpre = inp("gpre", [D])
    win = inp("win", [D, 2048])
    pw = inp("pw", [1024, 512])
    pscale = inp("pscale", [128, 8])
    wsel = inp("wsel", [128, 4])
    wout = inp("wout", [D, D])
    xcur = nc.dram_tensor("xcur", [SL, D], F32, kind="ExternalOutput").ap() if n_parts else None
    ypart = nc.dram_tensor("ypart", [SL, D], F32, kind="ExternalOutput").ap()
    S = Sched(nc)
    cx = Ctx(nc, S)
    c = make_consts(cx)
    odd_pass(cx, c, SL, wsel, x, parts, gpost, gpre, xcur, win, pw, pscale, wout, ypart)
    S.finish()
    return nc, S


def prep_odd(inp, o, layer, groups):
    w = np.asarray(inp["w_in_c"][o])
    pcols = np.concatenate([w[:, 512 * g:512 * g + 512] for g in groups], axis=1)
    zcols = np.concatenate([w[:, 2048 + 512 * g:2048 + 512 * g + 512] for g in groups], axis=1)
    pwv = np.asarray(inp["pool_w"][o])
    pw = np.concatenate([pwv[g] for g in groups], axis=0)
    sc = np.concatenate([np.asarray(inp["pool_scale"][o])[512 * g:512 * g + 512] for g in groups])
    wo = np.asarray(inp["w_out_c"][o])
    wout = np.concatenate([wo[512 * g:512 * g + 512] for g in groups], axis=0)
    d = dict(
        gpre=inp["pre_norm_g"][layer], gpost=inp["post_norm_g"][layer - 1],
        win=np.concatenate([pcols, zcols], axis=1), pw=pw,
        pscale=sc.reshape(8, 128).T, wout=wout,
        wsel=np.tile(np.array([[float(POOL_WINDOWS[groups[si]] == SLOT_W[si][wi]) for si in range(2) for wi in range(2)]],
                              dtype=np.float32), (128, 1)),
    )
    return {k: np.ascontiguousarray(np.asarray(v, dtype=np.float32)) for k, v in d.items()}


def build_combine(NTOK):
    nc = bass.Bass("TRN2", target_bir_lowering=False)
    inp = lambda name, shape: nc.dram_tensor(name, list(shape), F32, kind="ExternalInput").ap()
    x = inp("x", [NTOK, D])
    parts = [inp(f"yp{k}", [NTOK, D]) for k in range(2)]
    gpost = inp("gpost", [D])
    gpre = inp("gpre", [D])
    out = nc.dram_tensor("xcur", [NTOK, D], F32, kind="ExternalOutput").ap()
    S = Sched(nc)
    cx = Ctx(nc, S)
    c = make_consts(cx)
    norm = NormStage(cx, c, x, parts, gpost, gpre, out, "nc")
    for r0 in range(0, NTOK, 128):
        norm.run(r0, None, 0)
    S.finish()
    return nc, S


ODD_GROUPS = ((0, 3), (1, 2))
_PROG = {}

EVEN_KEYS = ("gpre", "win", "convw", "bi", "bf", "hng", "wout")
ODD_KEYS = ("gpre", "win", "pw", "pscale", "wsel", "wout")
EVEN_SHAPES = dict(gpre=[D], gpost=[D], win=[D, NCOL_E], convw=[128, 8, 4], bi=[2, 1], bf=[2, 1], hng=[512], wout=[D, D])
ODD_SHAPES = dict(gpre=[D], gpost=[D], win=[D, 2048], pw=[1024, 512], pscale=[128, 8], wsel=[128, 4], wout=[D, D])


def build_fused(SL):
    nc = bass.Bass("TRN2", target_bir_lowering=False)
    inp = lambda name, shape: nc.dram_tensor(name, list(shape), F32, kind="ExternalInput").ap()
    x = inp("x", [SL, D])
    wts = []
    for l in range(DEPTH):
        shapes = EVEN_SHAPES if l % 2 == 0 else ODD_SHAPES
        keys = (EVEN_KEYS if l % 2 == 0 else ODD_KEYS) + (("gpost",) if l > 0 else ())
        wts.append({k: inp(f"{k}_l{l}", shapes[k]) for k in keys})
    gfin = inp("gpost_fin", [D])
    out = nc.dram_tensor("out", [SL, D], F32, kind="ExternalOutput").ap()
    ypart = [nc.dram_tensor(f"ypart{l}", [SL, D], F32).ap() for l in range(DEPTH)]
    ysum = [nc.dram_tensor(f"ysum{l}", [SL, D], F32).ap() for l in range(DEPTH)]
    xcur = [None] + [nc.dram_tensor(f"xcur{l}", [SL, D], F32).ap() for l in range(1, DEPTH)]
    scr, _ = _even_scratch(nc, SL)
    S = Sched(nc)
    cx = Ctx(nc, S)
    c = make_consts(cx)
    mark = cx.off
    prev_ar = None
    for l in range(DEPTH):
        w = wts[l]
        x_l = x if l <= 1 else xcur[l - 1]
        parts = [] if l == 0 else [ysum[l - 1]]
        parts_dt = None if l == 0 else [prev_ar.sum_dt]
        ar = ARHook(cx, ypart[l], ysum[l], SL)
        if l % 2 == 0:
            scr_dt = {k: [DT(f"{k}{t}") for t in range(SL // TT)] for k in ("q", "k", "gz", "v", "ya")}
            even_pass1(cx, c, SL, x_l, parts, w.get("gpost"), w["gpre"], xcur[l], w["win"], w["convw"], w["bi"], w["bf"],
                       w["hng"], scr, scr_dt, parts_dt=parts_dt)
            S.barrier()
            cx.off = mark
            even_pass2(cx, c, SL, w["wout"], scr, scr_dt, ypart[l], ar=ar)
        else:
            odd_pass(cx, c, SL, w["wsel"], x_l, parts, w.get("gpost"), w["gpre"], xcur[l], w["win"], w["pw"], w["pscale"],
                     w["wout"], ypart[l], ar=ar, parts_dt=parts_dt)
        S.barrier()
        cx.off = mark
        prev_ar = ar
    norm = NormStage(cx, c, xcur[DEPTH - 1], [ysum[DEPTH - 1]], gfin, wts[0]["gpre"], out, "nf", parts_dt=[prev_ar.sum_dt],
                     nyt=2)
    for r0 in range(0, SL, 128):
        norm.run(r0, None, 0)
    S.finish()
    return nc, S


def kernel(**inputs):
    inp = {k: np.asarray(v) for k, v in inputs.items()}
    x = np.ascontiguousarray(inp["x"], dtype=np.float32)
    B, SL, _ = x.shape
    maps = []
    for b in range(B):
        for c in range(2):
            d = {"x": x[b], "gpost_fin": np.ascontiguousarray(inp["post_norm_g"][DEPTH - 1], dtype=np.float32)}
            for l in range(DEPTH):
                p = prep_even(inp, l // 2, l, c) if l % 2 == 0 else prep_odd(inp, l // 2, l, ODD_GROUPS[c])
                if l == 0:
                    p.pop("gpost", None)
                for k, v in p.items():
                    d[f"{k}_l{l}"] = v
            maps.append(d)
    if ("fused", SL) not in _PROG:
        _PROG[("fused", SL)] = build_fused(SL)[0]
    res = run_bass_kernel_spmd(_PROG[("fused", SL)], maps, core_ids=list(range(2 * B))).results
    return np.stack([res[2 * b]["out"] for b in range(B)], axis=0).astype(np.float32)
```

```python
import math
import numpy as np
import concourse.bass as bass
import concourse.mybir as mybir
from concourse.bass_utils import run_bass_kernel_spmd

F32 = mybir.dt.float32
BF16 = mybir.dt.bfloat16
AF = mybir.ActivationFunctionType
ALU = mybir.AluOpType
AX = mybir.AxisListType

D = 1024
SEQ = 8192
BATCH = 4
DEPTH = 4
EPS = 1e-6
TT = 512
LN16 = math.log(16.0)
DBG = {}


class Buf:
    __slots__ = ("name", "writers", "readers")

    def __init__(self, name):
        self.name = name
        self.writers = []
        self.readers = []


class Op:
    __slots__ = ("eng", "fn", "deps", "signal", "ev", "stream", "idx", "is_dma", "inc")

    def __init__(self, eng, fn, stream, is_dma):
        self.inc = 16 if is_dma else 1
        self.eng = eng
        self.fn = fn
        self.deps = set()
        self.signal = False
        self.ev = None
        self.stream = stream
        self.is_dma = is_dma


class Sched:
    ENGS = ("sync", "scalar", "vector", "gpsimd", "tensor")

    def __init__(self, nc, tag=""):
        self.nc = nc
        self.ops = []
        self.tag = tag
        self.bar = None
        self.last_eng = {}
        self.last_stream = {}

    def barrier(self):
        deps = set(self.last_eng.values()) | {v for k, v in self.last_stream.items() if k != "cc"}
        dummy = self.dummy
        op = self._add("vector", lambda e: e.memset(dummy[0:1, 0:1], 0.0), (), (), None, False)
        op.deps |= deps
        op.deps.discard(op.idx)
        self.bar = op.idx
        return op

    def _add(self, eng, fn, reads, writes, stream, is_dma):
        op = Op(eng, fn, stream, is_dma)
        op.idx = len(self.ops)
        ops = self.ops
        for b in reads:
            op.deps.update(b.writers)
        for b in writes:
            op.deps.update(b.writers)
            op.deps.update(b.readers)
        op.deps.discard(op.idx)
        if eng == "tensor" and not is_dma:
            op.deps = {d for d in op.deps if not (ops[d].eng == "tensor" and not ops[d].is_dma)}
        if self.bar is not None:
            op.deps.add(self.bar)
        for b in reads:
            b.readers.append(op.idx)
        for b in writes:
            b.writers = [op.idx]
            b.readers = []
        ops.append(op)
        if is_dma:
            self.last_stream[stream] = op.idx
        else:
            self.last_eng[eng] = op.idx
        return op

    def op(self, eng, fn, reads=(), writes=()):
        return self._add(eng, fn, reads, writes, None, False)

    def dma(self, eng, fn, reads=(), writes=(), stream=None):
        return self._add(eng, fn, reads, writes, stream, True)

    def cc(self, eng, fn, reads=(), writes=(), stream=None, inc=1):
        op = self._add(eng, fn, reads, writes, stream, True)
        op.inc = inc
        return op

    def finish(self):
        nc = self.nc
        ops = self.ops
        for op in ops:
            for d in op.deps:
                ops[d].signal = True
        last_dma = {}
        for op in ops:
            if op.is_dma:
                op.signal = True
                last_dma[op.stream] = op.idx
        sems = {}
        cnt = {}
        for op in ops:
            if not op.signal:
                continue
            if op.is_dma:
                key = ("d", op.stream)
                cnt[key] = cnt.get(key, 0) + op.inc
            else:
                key = ("e", op.eng)
                cnt[key] = cnt.get(key, 0) + 1
            op.ev = (key, cnt[key])
            if key not in sems:
                sems[key] = nc.alloc_semaphore(self.tag + "s_" + "_".join(str(k) for k in key))
        known = {e: {} for e in self.ENGS}
        per_eng = {e: [] for e in self.ENGS}
        for op in ops:
            waits = {}
            kn = known[op.eng]
            for d in op.deps:
                key, val = ops[d].ev
                if kn.get(key, 0) >= val:
                    continue
                if waits.get(key, 0) < val:
                    waits[key] = val
            kn.update(waits)
            per_eng[op.eng].append((op, list(waits.items())))
        finals = [ops[i].ev for i in last_dma.values()]
        self.stats = dict(n_ops=len(ops), n_sems=len(sems),
                          per_eng={e: len(v) for e, v in per_eng.items()})
        with nc.Block() as block:
            def make(engname):
                def body(eng):
                    for op, waits in per_eng[engname]:
                        for key, val in waits:
                            eng.wait_ge(sems[key], val)
                        ins = op.fn(eng)
                        if op.ev is not None:
                            ins.then_inc(sems[op.ev[0]], op.inc)
                    if engname == "sync":
                        for key, val in finals:
                            eng.wait_ge(sems[key], val)
                return body
            block.sync(make("sync"))
            block.scalar(make("scalar"))
            block.vector(make("vector"))
            block.gpsimd(make("gpsimd"))
            block.tensor(make("tensor"))


class T:
    __slots__ = ("h", "b", "ps")

    def __init__(self, h, name, buf=None, ps=False):
        self.h = h
        self.b = buf if buf is not None else Buf(name)
        self.ps = ps

    def __getitem__(self, k):
        return self.h[k]


SB_BASE = 16640
SB_LIMIT = 229376


class Ctx:
    def __init__(self, nc, S):
        self.nc = nc
        self.S = S
        self._n = 0
        self.off = SB_BASE
        self.pairs = [nc.alloc_psum_tensor(f"pbank{i}", [128, 1024], F32) for i in range(4)]
        self.bank_bufs = [Buf(f"bank{i}") for i in range(8)]
        S.dummy = self.sb([128, 8], F32, "dummy")

    def sb(self, shape, dt, name=None):
        self._n += 1
        name = f"{name or 't'}_{self._n}"
        nbytes = int(np.prod(shape[1:])) * (4 if dt == F32 else 2)
        nbytes = (nbytes + 31) // 32 * 32
        off = self.off
        self.off += nbytes
        assert self.off <= SB_LIMIT, f"SBUF overflow at {name}: {self.off}"
        return T(self.nc.alloc_sbuf_tensor_at(name, list(shape), dt, offset=off), name)

    def bank(self, i, dt=F32, c0=0, c1=None, name=None):
        ap = self.pairs[i // 2][:, (i % 2) * 512:(i % 2 + 1) * 512]
        if dt != F32:
            ap = ap.bitcast(dt)
        if c1 is None:
            c1 = ap.shape[1]
        self._n += 1
        return T(ap[:, c0:c1], f"{name or 'ps'}_{self._n}", buf=self.bank_bufs[i], ps=True)

    def bank2(self, k, name=None):
        self._n += 1
        t = T(self.pairs[k][:].rearrange("p (h q) -> p h q", h=2), f"{name or 'ps2'}_{self._n}", buf=self.bank_bufs[2 * k], ps=True)
        t.b = (self.bank_bufs[2 * k], self.bank_bufs[2 * k + 1])
        return t

    @staticmethod
    def _bufs(t):
        return list(t.b) if isinstance(t.b, tuple) else [t.b]

    @classmethod
    def _rw(cls, r, w):
        reads = [b for t in r if not getattr(t, "ps", False) for b in cls._bufs(t)]
        writes = [b for t in w for b in cls._bufs(t)] + [b for t in r if getattr(t, "ps", False) for b in cls._bufs(t)]
        return reads, writes

    def V(self, fn, r=(), w=()):
        reads, writes = self._rw(r, w)
        self.S.op("vector", fn, reads, writes)

    def A(self, fn, r=(), w=()):
        reads, writes = self._rw(r, w)
        self.S.op("scalar", fn, reads, writes)

    def G(self, fn, r=(), w=()):
        reads, writes = self._rw(r, w)
        self.S.op("gpsimd", fn, reads, writes)

    def PE(self, fn, r=(), w=()):
        reads, writes = self._rw(r, w)
        self.S.op("tensor", fn, reads, writes)

    def DMA(self, q, fn, r=(), w=(), stream=None):
        self.S.dma(q, fn, [t.b for t in r], [t.b for t in w], stream=stream)


class DT:
    __slots__ = ("b",)

    def __init__(self, name):
        self.b = Buf(name)


def make_consts(cx):
    c = {}
    tmp = cx.sb([128, 512], F32, "ctmp")
    identf = cx.sb([128, 128], F32, "identf")
    identb = cx.sb([128, 128], BF16, "identb")
    cx.G(lambda e: e.memset(identf[:], 1.0), w=[identf])
    cx.G(lambda e: e.affine_select(out=identf[:], in_=identf[:], pattern=[[-1, 128]], compare_op=ALU.is_equal,
                                   fill=0.0, base=0, channel_multiplier=1), r=[identf], w=[identf])
    cx.V(lambda e: e.tensor_copy(out=identb[:], in_=identf[:]), r=[identf], w=[identb])
    c["identf"], c["identb"] = identf, identb
    maskA = cx.sb([128, 128], F32, "maskA")
    cx.G(lambda e: e.memset(maskA[:], 1.0), w=[maskA])
    cx.G(lambda e: e.affine_select(out=maskA[:], in_=maskA[:], pattern=[[1, 128]], compare_op=ALU.is_ge,
                                   fill=0.0, base=0, channel_multiplier=-1), r=[maskA], w=[maskA])
    c["maskA"] = maskA
    m0 = cx.sb([128, 512], BF16, "m0")
    cx.G(lambda e: e.memset(tmp[:], 1.0), w=[tmp])
    cx.G(lambda e: e.affine_select(out=tmp[:], in_=tmp[:], pattern=[[1, 512]], compare_op=ALU.is_gt,
                                   fill=0.0, base=0, channel_multiplier=-1), r=[tmp], w=[tmp])
    cx.V(lambda e: e.tensor_copy(out=m0[:], in_=tmp[:]), r=[tmp], w=[m0])
    c["m0"] = m0
    m0p = cx.sb([128, 2, 512], BF16, "m0p")
    for hh in range(2):
        cx.V(lambda e, hh=hh: e.tensor_copy(out=m0p[:, hh, :], in_=tmp[:]), r=[tmp], w=[m0p])
    c["m0p"] = m0p
    tneg = cx.sb([128, 128], BF16, "tneg")
    onesneg = cx.sb([128, 128], BF16, "onesneg")
    tmp2 = cx.sb([128, 128], F32, "ctmp2")
    cx.G(lambda e: e.memset(tmp2[:], -1.0), w=[tmp2])
    cx.V(lambda e: e.tensor_copy(out=onesneg[:], in_=tmp2[:]), r=[tmp2], w=[onesneg])
    cx.G(lambda e: e.affine_select(out=tmp2[:], in_=tmp2[:], pattern=[[-1, 128]], compare_op=ALU.is_ge,
                                   fill=0.0, base=0, channel_multiplier=1), r=[tmp2, onesneg], w=[tmp2])
    cx.V(lambda e: e.tensor_copy(out=tneg[:], in_=tmp2[:]), r=[tmp2], w=[tneg])
    c["tneg"], c["onesneg"] = tneg, onesneg
    sel = cx.sb([2, 2, 128], F32, "sel")
    cx.G(lambda e: e.memset(sel[:], 1.0), w=[sel])
    for h in range(2):
        cx.G(lambda e, h=h: e.affine_select(out=sel[:, h, :], in_=sel[:, h, :], pattern=[[0, 128]],
                                            compare_op=ALU.is_equal, fill=0.0, base=-h, channel_multiplier=1),
             r=[sel], w=[sel])
    c["sel"] = sel
    ones2 = cx.sb([2, 512], F32, "ones2")
    cx.G(lambda e: e.memset(ones2[:], 1.0), w=[ones2])
    c["ones2"] = ones2
    return c


def load_weight_bf16(cx, W, wdram, ncols, stg, tagname):
    nk = wdram.shape[0] // 128
    for kc in range(nk):
        cx.DMA("gpsimd", lambda e, kc=kc: e.dma_start(out=W[:, kc, 0:ncols], in_=wdram[kc * 128:(kc + 1) * 128, 0:ncols]),
               w=[W], stream=f"{tagname}w{kc % 4}")


class ARHook:
    GROUPS = [[0, 1], [2, 3], [4, 5], [6, 7]]

    def __init__(self, cx, ypart_ap, ysum_ap, SL):
        self.cx, self.yp, self.ys = cx, ypart_ap, ysum_ap
        self.row_dt = [DT(f"yprow{i}") for i in range(SL // 128)]
        self.sum_dt = [DT(f"ysum{i}") for i in range(SL // 1024)]
        self.sum_half = [DT(f"ysumh{i}") for i in range(SL // TT)]

    def store_dt(self, r0):
        return self.row_dt[r0 // 128]

    def after_tile(self, Tn):
        r0 = Tn * TT
        yp, ys = self.yp, self.ys
        self.cx.S.cc("gpsimd", lambda e: e.collective_compute("AllReduce", ALU.add, replica_groups=self.GROUPS,
                                                              ins=[yp[r0:r0 + TT]], outs=[ys[r0:r0 + TT]]),
                     reads=[d.b for d in self.row_dt[Tn * 4:(Tn + 1) * 4]], writes=[self.sum_half[Tn].b], stream="cc", inc=1)
        if Tn % 2 == 1:
            self.sum_dt[Tn // 2].b.writers = list(self.sum_half[Tn - 1].b.writers) + list(self.sum_half[Tn].b.writers)


class NormStage:
    def __init__(self, cx, consts, x_ap, parts, gpost_ap, gpre_ap, xcur_ap, name="n", pbank=6, parts_dt=None, nhb=1,
                 nyt=1):
        self.cx, self.c = cx, consts
        self.parts_dt = parts_dt
        self.x_ap, self.parts, self.xcur_ap = x_ap, parts, xcur_ap
        self.xt = [cx.sb([128, D], F32, "xt") for _ in range(2)]
        self.nyt = nyt
        self.yt = [[cx.sb([128, D], F32, "yt") for _ in range(len(parts))] for _ in range(nyt)]
        self.junk = cx.sb([128, D], BF16, "junk")
        self.hb = [cx.sb([128, D], BF16, "hb") for _ in range(nhb)]
        self.st = [cx.sb([128, 4], F32, "nst") for _ in range(2)]
        self.pT = [cx.bank(pbank, BF16, 0, 512, "pT"), cx.bank(pbank + 1, BF16, 0, 512, "pT")]
        self.gpre = cx.sb([128, D], F32, "gpre")
        cx.DMA("sync", lambda e: e.dma_start(out=self.gpre[:], in_=gpre_ap.partition_broadcast(128)),
               w=[self.gpre], stream=name + "gpre")
        if parts:
            self.gpost = cx.sb([128, D], F32, "gpost")
            cx.DMA("sync", lambda e: e.dma_start(out=self.gpost[:], in_=gpost_ap.partition_broadcast(128)),
                   w=[self.gpost], stream=name + "gpost")
        self.n = 0
        self.name = name

    def load(self, r0):
        cx = self.cx
        i = self.n
        xt = self.xt[i % 2]
        cx.DMA("sync", lambda e: e.dma_start(out=xt[:], in_=self.x_ap[r0:r0 + 128, :]), w=[xt],
               stream=f"{self.name}x{i % 2}")
        return xt

    def run(self, r0, hT, j):
        hb = self.pre(r0, want_h=hT is not None)
        if hT is not None:
            self.post(hb, hT, j)

    def pre(self, r0, want_h=True):
        cx, c = self.cx, self.c
        i = self.n
        self.n += 1
        xt = self.load(r0)
        st = self.st[i % 2]
        hb = self.hb[i % len(self.hb)]
        junk = self.junk
        if self.parts:
            yts = self.yt[i % self.nyt]
            for k, p_ap in enumerate(self.parts):
                yt = yts[k]
                rdt = [self.parts_dt[k][r0 // 1024]] if self.parts_dt is not None else []
                cx.DMA("sync", lambda e, yt=yt, p_ap=p_ap: e.dma_start(out=yt[:], in_=p_ap[r0:r0 + 128, :]),
                       r=rdt, w=[yt], stream=f"{self.name}y{k}_{i % self.nyt}")
            y0 = yts[0]
            if len(self.parts) == 2:
                y1 = yts[1]
                cx.G(lambda e: e.tensor_tensor(out=y0[:], in0=y0[:], in1=y1[:], op=ALU.add), r=[y0, y1], w=[y0])
            cx.A(lambda e: e.activation(out=junk[:], in_=y0[:], func=AF.Square, accum_out=st[:, 0:1]),
                 r=[y0], w=[junk, st])
            cx.A(lambda e: e.activation(out=st[:, 1:2], in_=st[:, 0:1], func=AF.Sqrt, scale=1.0 / D, bias=EPS),
                 r=[st], w=[st])
            cx.V(lambda e: e.reciprocal(out=st[:, 1:2], in_=st[:, 1:2]), r=[st], w=[st])
            cx.V(lambda e: e.scalar_tensor_tensor(out=y0[:], in0=y0[:], scalar=st[:, 1:2], in1=self.gpost[:],
                                                  op0=ALU.mult, op1=ALU.mult), r=[y0, st, self.gpost], w=[y0])
            cx.G(lambda e: e.tensor_tensor(out=xt[:], in0=xt[:], in1=y0[:], op=ALU.add), r=[xt, y0], w=[xt])
            if self.xcur_ap is not None:
                cx.DMA("gpsimd", lambda e: e.dma_start(out=self.xcur_ap[r0:r0 + 128, :], in_=xt[:]), r=[xt],
                       stream=f"{self.name}xo{i % 2}")
        if not want_h:
            return None
        cx.A(lambda e: e.activation(out=junk[:], in_=xt[:], func=AF.Square, accum_out=st[:, 2:3]),
             r=[xt], w=[junk, st])
        cx.A(lambda e: e.activation(out=st[:, 3:4], in_=st[:, 2:3], func=AF.Sqrt, scale=1.0 / D, bias=EPS),
             r=[st], w=[st])
        cx.V(lambda e: e.reciprocal(out=st[:, 3:4], in_=st[:, 3:4]), r=[st], w=[st])
        cx.V(lambda e: e.scalar_tensor_tensor(out=hb[:], in0=xt[:], scalar=st[:, 3:4], in1=self.gpre[:],
                                              op0=ALU.mult, op1=ALU.mult), r=[xt, st, self.gpre], w=[hb])
        return hb

    def post(self, hb, hT, j):
        cx, c = self.cx, self.c
        for g in range(2):
            pT = self.pT[g]
            for k in range(4):
                cc = g * 4 + k
                cx.PE(lambda e, pT=pT, k=k, cc=cc: e.transpose(out=pT[:, k * 128:(k + 1) * 128],
                                                              in_=hb[:, cc * 128:(cc + 1) * 128],
                                                              identity=c["identb"][:]),
                      r=[hb, c["identb"]], w=[pT])
            src = pT[:].rearrange("p (c t) -> p c t", c=4)
            if g == 0:
                cx.V(lambda e, src=src, g=g: e.tensor_copy(out=hT[:, g * 4:(g + 1) * 4, j * 128:(j + 1) * 128], in_=src),
                     r=[pT], w=[hT])
            else:
                cx.A(lambda e, src=src, g=g: e.activation(out=hT[:, g * 4:(g + 1) * 4, j * 128:(j + 1) * 128], in_=src,
                                                          func=AF.Copy), r=[pT], w=[hT])


NCOL_E = 4612
C_AQ, C_AK, C_BQ, C_BK, C_BZ = 0, 512, 1024, 1536, 2048
C_TM = 2560
C_GI, C_GF = 4608, 4610


def even_pass1(cx, c, SL, x_ap, parts, gpost_ap, gpre_ap, xcur_ap, win_ap, convw_ap, bi_ap, nbf_ap, hng_ap, scr,
               scr_dt, parts_dt=None):
    S = cx.S
    NT = SL // TT
    W = cx.sb([128, 8, NCOL_E], BF16, "W")
    stg = [cx.sb([128, 1024], F32, "wstg") for _ in range(1)]
    load_weight_bf16(cx, W, win_ap, NCOL_E, stg, "w")
    convw = cx.sb([128, 8, 4], F32, "convw")
    cx.DMA("sync", lambda e: e.dma_start(out=convw[:], in_=convw_ap), w=[convw], stream="convw")
    bi = cx.sb([2, 1], F32, "bi")
    nbf = cx.sb([2, 1], F32, "nbf")
    cx.DMA("sync", lambda e: e.dma_start(out=bi[:], in_=bi_ap), w=[bi], stream="bi")
    cx.DMA("sync", lambda e: e.dma_start(out=nbf[:], in_=nbf_ap), w=[nbf], stream="nbf")
    hng = cx.sb([128, 512], F32, "hng")
    cx.DMA("sync", lambda e: e.dma_start(out=hng[:], in_=hng_ap.partition_broadcast(128)), w=[hng], stream="hng")
    cx.V(lambda e: e.tensor_scalar_mul(out=nbf[:], in0=nbf[:], scalar1=-1.0), r=[nbf], w=[nbf])

    if DBG.get("stop") == "weights":
        return
    norm = NormStage(cx, c, x_ap, parts, gpost_ap, gpre_ap, xcur_ap, "n1", parts_dt=parts_dt, nhb=4)
    hT2 = [cx.sb([128, 8, TT], BF16, "hT") for _ in range(2)]
    cv = [cx.sb([128, 3 + TT], F32, "cv") for _ in range(2)]
    halo = cx.sb([128, 8, 3], F32, "halo")
    cx.G(lambda e: e.memset(halo[:], 0.0), w=[halo])
    acc = [cx.sb([128, TT], F32, "cacc") for _ in range(2)]
    qkT = cx.sb([128, 8, TT], BF16, "qkT")
    bst = [cx.sb([128, 4, TT], BF16, "bst") for _ in range(3)]
    bvst = cx.sb([128, 4, 512], BF16, "bvst")
    vextj = [cx.sb([128, 2, 257], BF16, "vext") for _ in range(4)]
    for t in vextj:
        cx.G(lambda e, t=t: e.memset(t[:], 1.0), w=[t])
    sgoj = [cx.sb([128, 512], BF16, "sgo") for _ in range(4)]
    gzgj = [cx.sb([128, 512], BF16, "gzg") for _ in range(4)]
    C32 = [[cx.sb([128, 257], F32, "C32") for _ in range(2)] for _ in range(2)]
    Cb = [[cx.sb([128, 257], BF16, "Cb") for _ in range(2)] for _ in range(2)]
    for h in range(2):
        for d in range(2):
            cx.G(lambda e, t=C32[h][d]: e.memset(t[:], 0.0), w=[C32[h][d]])
            cx.G(lambda e, t=Cb[h][d]: e.memset(t[:], 0.0), w=[Cb[h][d]])
    Bcarry = cx.sb([2, 1], F32, "Bcarry")
    mucarry = cx.sb([2, 1], F32, "mucarry")
    cx.G(lambda e: e.memset(Bcarry[:], 0.0), w=[Bcarry])
    cx.G(lambda e: e.memset(mucarry[:], 0.0), w=[mucarry])
    g_e1 = cx.sb([2, TT], F32, "g_e1")
    g_B = cx.sb([2, TT], F32, "g_B")
    g_ag = cx.sb([2, TT], F32, "g_ag")
    g_w = cx.sb([2, TT], F32, "g_w")
    g_c = cx.sb([2, TT], F32, "g_c")
    g_s = cx.sb([2, 32], F32, "g_s")
    wc = cx.sb([128, 16], F32, "wc")
    decb = cx.sb([128, 8], F32, "decb")
    scT = [cx.sb([128, 128], BF16, "scT") for _ in range(2)]
    kp = [cx.sb([128, 256], BF16, "kp") for _ in range(2)]
    hg = [cx.sb([128, 256], F32, "hg") for _ in range(2)]
    ya = [cx.sb([128, 256], BF16, "ya") for _ in range(2)]
    cst = [cx.sb([128, 4], F32, "cst") for _ in range(2)]
    junk2 = cx.sb([128, 256], BF16, "junk2")
    yaTs = cx.sb([128, 4, TT], BF16, "yaTs")
    pj = [cx.bank(0, name="pj"), cx.bank(1, name="pj"), cx.bank(2, name="pj")]
    p_num = cx.bank(3, name="pnum")
    p_misc = cx.bank(4, name="pmisc")
    p_mbk = cx.bank(5, BF16, 0, 256, "pmbk")
    p_mby = cx.bank(6, BF16, 0, 256, "pmby")
    pjn = [0]

    def pslot():
        p = pj[pjn[0] % 3]
        pjn[0] += 1
        return p

    for j in range(4):
        norm.run(j * 128, hT2[0], j)
    def _tile(Tn):
        t0 = Tn * TT
        hT = hT2[Tn % 2]
        hbs = [norm.pre(t0 + TT + j * 128) for j in range(4)] if Tn + 1 < NT else None
        if DBG.get("stop") == "fm":
            return
        pgi = pslot()
        for kc in range(8):
            cx.PE(lambda e, kc=kc: e.matmul(pgi[0:2, :], lhsT=W[:, kc, C_GI:C_GI + 2], rhs=hT[:, kc, :],
                                            start=(kc == 0), stop=(kc == 7)), r=[W, hT], w=[pgi])
        pgf = pslot()
        for kc in range(8):
            cx.PE(lambda e, kc=kc: e.matmul(pgf[0:2, :], lhsT=W[:, kc, C_GF:C_GF + 2], rhs=hT[:, kc, :],
                                            start=(kc == 0), stop=(kc == 7)), r=[W, hT], w=[pgf])
        cx.A(lambda e: e.activation(out=g_e1[:], in_=pgf[0:2, :], func=AF.Exp, scale=-1.0, bias=nbf[:, 0:1]),
             r=[pgf, nbf], w=[g_e1])
        cx.A(lambda e: e.activation(out=g_e1[:], in_=g_e1[:], func=AF.Ln, bias=1.0), r=[g_e1], w=[g_e1])
        cx.V(lambda e: e.tensor_tensor_scan(out=g_B[:], data0=c["ones2"][:], data1=g_e1[:], initial=Bcarry[:, 0:1],
                                            op0=ALU.mult, op1=ALU.subtract), r=[c["ones2"], g_e1, Bcarry], w=[g_B])
        cx.V(lambda e: e.tensor_copy(out=Bcarry[:], in_=g_B[:, TT - 1:TT]), r=[g_B], w=[Bcarry])
        cx.V(lambda e: e.scalar_tensor_tensor(out=g_ag[:], in0=pgi[0:2, :], scalar=bi[:, 0:1], in1=g_B[:],
                                              op0=ALU.add, op1=ALU.subtract), r=[pgi, bi, g_B], w=[g_ag])
        cx.V(lambda e: e.tensor_reduce(out=g_s[:, 0:4], in_=g_ag[:].rearrange("p (c t) -> p c t", c=4),
                                       axis=AX.X, op=ALU.max), r=[g_ag], w=[g_s])
        cx.V(lambda e: e.tensor_tensor_scan(out=g_s[:, 4:8], data0=g_s[:, 0:4], data1=g_s[:, 0:4],
                                            initial=mucarry[:, 0:1], op0=ALU.max, op1=ALU.max),
             r=[g_s, mucarry], w=[g_s])
        cx.V(lambda e: e.tensor_copy(out=g_s[:, 8:9], in_=mucarry[:]), r=[mucarry, g_s], w=[g_s])
        cx.V(lambda e: e.tensor_copy(out=g_s[:, 9:12], in_=g_s[:, 4:7]), r=[g_s], w=[g_s])
        cx.V(lambda e: e.tensor_copy(out=mucarry[:], in_=g_s[:, 7:8]), r=[g_s], w=[mucarry])
        cx.V(lambda e: e.tensor_tensor(out=g_s[:, 12:16], in0=g_s[:, 8:12], in1=g_s[:, 4:8], op=ALU.subtract),
             r=[g_s], w=[g_s])
        cx.A(lambda e: e.activation(out=g_s[:, 16:20], in_=g_s[:, 12:16], func=AF.Exp), r=[g_s], w=[g_s])
        cx.V(lambda e: e.tensor_scalar(out=g_s[:, 20:24], in0=g_s[:, 4:8], scalar1=-1.0, scalar2=-LN16,
                                       op0=ALU.mult, op1=ALU.add), r=[g_s], w=[g_s])
        cx.V(lambda e: e.tensor_scalar_mul(out=g_s[:, 24:28], in0=g_s[:, 4:8], scalar1=-1.0), r=[g_s], w=[g_s])
        for j in range(4):
            cx.A(lambda e, j=j: e.activation(out=g_w[:, j * 128:(j + 1) * 128], in_=g_ag[:, j * 128:(j + 1) * 128],
                                             func=AF.Exp, bias=g_s[:, 20 + j:21 + j]), r=[g_ag, g_s], w=[g_w])
            cx.A(lambda e, j=j: e.activation(out=g_c[:, j * 128:(j + 1) * 128], in_=g_B[:, j * 128:(j + 1) * 128],
                                             func=AF.Exp, scale=-1.0, bias=g_s[:, 24 + j:25 + j]),
                 r=[g_B, g_s], w=[g_c])
        for j in range(4):
            cx.PE(lambda e, j=j: e.transpose(out=p_misc[:, 128 + 4 * j:128 + 4 * j + 2],
                                             in_=g_w[:, j * 128:(j + 1) * 128], identity=c["identf"][0:2, 0:2]),
                  r=[g_w, c["identf"]], w=[p_misc])
            cx.PE(lambda e, j=j: e.transpose(out=p_misc[:, 128 + 4 * j + 2:128 + 4 * j + 4],
                                             in_=g_c[:, j * 128:(j + 1) * 128], identity=c["identf"][0:2, 0:2]),
                  r=[g_c, c["identf"]], w=[p_misc])
        for h in range(2):
            cx.PE(lambda e, h=h: e.matmul(p_misc[:, 144 + 4 * h:148 + 4 * h], lhsT=c["sel"][:, h, :],
                                          rhs=g_s[:, 16:20], start=True, stop=True), r=[c["sel"], g_s], w=[p_misc])
        cx.V(lambda e: e.tensor_copy(out=wc[:], in_=p_misc[:, 128:144]), r=[p_misc], w=[wc])
        cx.V(lambda e: e.tensor_copy(out=decb[:], in_=p_misc[:, 144:152]), r=[p_misc], w=[decb])
        for cc in range(20):
            ps = pslot()
            for kc in range(8):
                cx.PE(lambda e, ps=ps, kc=kc, cc=cc: e.matmul(ps[:], lhsT=W[:, kc, cc * 128:(cc + 1) * 128],
                                                              rhs=hT[:, kc, :], start=(kc == 0), stop=(kc == 7)),
                      r=[W, hT], w=[ps])
            if cc < 8:
                cvt = cv[cc % 2]
                ac = acc[cc % 2]
                cx.A(lambda e, ps=ps, cvt=cvt: e.activation(out=cvt[:, 3:3 + TT], in_=ps[:], func=AF.Copy),
                     r=[ps], w=[cvt])
                cx.G(lambda e, cvt=cvt, cc=cc: e.tensor_copy(out=cvt[:, 0:3], in_=halo[:, cc, :]), r=[halo], w=[cvt])
                cx.V(lambda e, cvt=cvt, ac=ac, cc=cc: e.tensor_scalar_mul(out=ac[:], in0=cvt[:, 3:3 + TT],
                                                                         scalar1=convw[:, cc, 3:4]),
                     r=[cvt, convw], w=[ac])
                for tap in range(3):
                    cx.V(lambda e, cvt=cvt, ac=ac, cc=cc, tap=tap: e.scalar_tensor_tensor(
                        out=ac[:], in0=cvt[:, tap:tap + TT], scalar=convw[:, cc, tap:tap + 1], in1=ac[:],
                        op0=ALU.mult, op1=ALU.add), r=[cvt, convw, ac], w=[ac])
                cx.A(lambda e, ac=ac, cc=cc: e.activation(out=qkT[:, cc, :], in_=ac[:], func=AF.Silu),
                     r=[ac], w=[qkT])
                cx.G(lambda e, cvt=cvt, cc=cc: e.tensor_copy(out=halo[:, cc, :], in_=cvt[:, TT:TT + 3]), r=[cvt], w=[halo])
            else:
                g = (cc - 8) // 4
                hh = (cc - 8) % 4
                st = bst[g]
                if g == 0:
                    cx.A(lambda e, ps=ps, st=st, hh=hh: e.activation(out=st[:, hh, :], in_=ps[:], func=AF.Copy,
                                                                     scale=128.0 ** -0.5), r=[ps], w=[st])
                elif g == 1:
                    cx.V(lambda e, ps=ps, st=st, hh=hh: e.tensor_copy(out=st[:, hh, :], in_=ps[:]), r=[ps], w=[st])
                else:
                    cx.A(lambda e, ps=ps, st=st, hh=hh: e.activation(out=st[:, hh, :], in_=ps[:], func=AF.Silu),
                         r=[ps], w=[st])
                if hh == 3:
                    key = ("q", "k", "gz")[g]
                    dst = scr[key + "T"][:, :, t0:t0 + TT].rearrange("c p t -> p c t")
                    cx.S.dma("gpsimd", lambda e, dst=dst, st=st: e.dma_start(out=dst, in_=st[:]),
                             reads=[st.b], writes=[scr_dt[key][Tn].b], stream=f"bst{g}")
        if DBG.get("stop") == "gates":
            return
        def tm(j):
            for g in range(4):
                ps = pslot()
                for kc in range(8):
                    cx.PE(lambda e, ps=ps, kc=kc, g=g: e.matmul(
                        ps[:], lhsT=hT[:, kc, j * 128:(j + 1) * 128],
                        rhs=W[:, kc, C_TM + g * 512:C_TM + (g + 1) * 512], start=(kc == 0), stop=(kc == 7)),
                        r=[W, hT], w=[ps])
                if g == 0:
                    for hh in range(2):
                        cx.V(lambda e, ps=ps, hh=hh: e.tensor_copy(out=vextj[j][:, hh, 0:256],
                                                                   in_=ps[:, hh * 256:(hh + 1) * 256]),
                             r=[ps], w=[vextj[j]])
                elif g == 1:
                    cx.A(lambda e, ps=ps: e.activation(out=sgoj[j][:], in_=ps[:], func=AF.Sigmoid),
                         r=[ps], w=[sgoj[j]])
                elif g == 2:
                    cx.A(lambda e, ps=ps: e.activation(out=gzgj[j][:], in_=ps[:], func=AF.Silu),
                         r=[ps], w=[gzgj[j]])
                    cx.G(lambda e: e.tensor_tensor(out=gzgj[j][:], in0=gzgj[j][:], in1=hng[:], op=ALU.mult),
                         r=[gzgj[j], hng], w=[gzgj[j]])
                else:
                    cx.V(lambda e, ps=ps: e.tensor_copy(out=bvst[:, j, :], in_=ps[:]), r=[ps], w=[bvst])

        def unit(j, h):
            sl = slice(j * 128, (j + 1) * 128)
            n = (Tn * 4 + j) * 2 + h
            sc, kpt, hgt, yat, cs = scT[n % 2], kp[n % 2], hg[n % 2], ya[n % 2], cst[n % 2]
            vx, sg, gg = vextj[j], sgoj[j], gzgj[j]
            wcol = wc[:, 4 * j + h:4 * j + h + 1]
            ccol = wc[:, 4 * j + 2 + h:4 * j + 3 + h]
            dcol = decb[:, 4 * h + j:4 * h + j + 1]
            for d in range(2):
                cx.PE(lambda e, d=d: e.matmul(p_misc[:, 0:128], lhsT=qkT[:, 4 + 2 * h + d, sl],
                                              rhs=qkT[:, 2 * h + d, sl], start=(d == 0), stop=(d == 1)),
                      r=[qkT], w=[p_misc])
            cx.V(lambda e: e.scalar_tensor_tensor(out=sc[:], in0=p_misc[:, 0:128], scalar=wcol,
                                                  in1=c["maskA"][:], op0=ALU.mult, op1=ALU.mult),
                 r=[p_misc, wc, c["maskA"]], w=[sc])
            for d in range(2):
                cx.PE(lambda e, d=d: e.transpose(out=p_mbk[:, d * 128:(d + 1) * 128],
                                                 in_=qkT[:, 4 + 2 * h + d, sl], identity=c["identb"][:]),
                      r=[qkT, c["identb"]], w=[p_mbk])
            cx.A(lambda e: e.activation(out=kpt[:], in_=p_mbk[:, 0:256], func=AF.Copy, scale=wcol),
                 r=[p_mbk, wc], w=[kpt])
            for d in range(2):
                cx.A(lambda e, d=d: e.activation(out=Cb[h][d][:], in_=C32[h][d][:], func=AF.Copy, scale=dcol),
                     r=[C32[h][d], decb], w=[Cb[h][d]])
                cx.V(lambda e, d=d: e.tensor_scalar_mul(out=C32[h][d][:], in0=C32[h][d][:], scalar1=dcol),
                     r=[C32[h][d], decb], w=[C32[h][d]])
            yield
            cx.PE(lambda e: e.matmul(p_num[:, 0:257], lhsT=sc[:], rhs=vx[:, h, :], start=True, stop=False),
                  r=[sc, vx], w=[p_num])
            for d in range(2):
                cx.PE(lambda e, d=d: e.matmul(p_num[:, 0:257], lhsT=qkT[:, 2 * h + d, sl], rhs=Cb[h][d][:],
                                              start=False, stop=(d == 1)), r=[qkT, Cb[h][d]], w=[p_num])
            for d in range(2):
                pdc = pslot()
                cx.PE(lambda e, d=d, pdc=pdc: e.matmul(pdc[:, 0:257], lhsT=kpt[:, d * 128:(d + 1) * 128],
                                                       rhs=vx[:, h, :], start=True, stop=True),
                      r=[kpt, vx], w=[pdc])
                cx.V(lambda e, d=d, pdc=pdc: e.tensor_tensor(out=C32[h][d][:], in0=C32[h][d][:], in1=pdc[:, 0:257],
                                                             op=ALU.add), r=[C32[h][d], pdc], w=[C32[h][d]])
            cx.V(lambda e: e.tensor_copy(out=cs[:, 0:1], in_=p_num[:, 256:257]), r=[p_num], w=[cs])
            cx.V(lambda e: e.scalar_tensor_tensor(out=cs[:, 1:2], in0=cs[:, 0:1], scalar=-1.0, in1=cs[:, 0:1],
                                                  op0=ALU.mult, op1=ALU.max), r=[cs], w=[cs])
            cx.V(lambda e: e.tensor_tensor(out=cs[:, 0:1], in0=cs[:, 1:2], in1=ccol, op=ALU.max),
                 r=[cs, wc], w=[cs])
            cx.V(lambda e: e.reciprocal(out=cs[:, 1:2], in_=cs[:, 0:1]), r=[cs], w=[cs])
            cx.V(lambda e: e.scalar_tensor_tensor(out=hgt[:], in0=p_num[:, 0:256], scalar=cs[:, 1:2],
                                                  in1=sg[:, h * 256:(h + 1) * 256], op0=ALU.mult, op1=ALU.mult),
                 r=[p_num, cs, sg], w=[hgt])
            cx.A(lambda e: e.activation(out=junk2[:], in_=hgt[:], func=AF.Square, accum_out=cs[:, 2:3]),
                 r=[hgt], w=[junk2, cs])
            cx.A(lambda e: e.activation(out=cs[:, 3:4], in_=cs[:, 2:3], func=AF.Sqrt, scale=1.0 / 256, bias=EPS),
                 r=[cs], w=[cs])
            cx.V(lambda e: e.reciprocal(out=cs[:, 3:4], in_=cs[:, 3:4]), r=[cs], w=[cs])
            cx.V(lambda e: e.scalar_tensor_tensor(out=yat[:], in0=hgt[:], scalar=cs[:, 3:4],
                                                  in1=gg[:, h * 256:(h + 1) * 256], op0=ALU.mult, op1=ALU.mult),
                 r=[hgt, cs, gg], w=[yat])
            yield
            for d in range(2):
                cx.PE(lambda e, d=d: e.transpose(out=p_mby[:, d * 128:(d + 1) * 128],
                                                 in_=yat[:, d * 128:(d + 1) * 128], identity=c["identb"][:]),
                      r=[yat, c["identb"]], w=[p_mby])
            for d in range(2):
                cx.A(lambda e, d=d: e.activation(out=yaTs[:, 2 * h + d, sl], in_=p_mby[:, d * 128:(d + 1) * 128],
                                                 func=AF.Copy), r=[p_mby], w=[yaTs])
            yield

        units = [unit(j, h) for j in range(4) for h in range(2)]
        for step in range(8 + 2):
            for k in (0, 1, 2):
                u = step - k
                if 0 <= u < 8:
                    if k == 0 and u % 2 == 0:
                        tm(u // 2)
                    next(units[u])
        dstv = scr["vB"][t0:t0 + TT, :].rearrange("(j p) c -> p j c", p=128)
        cx.S.dma("gpsimd", lambda e, dstv=dstv: e.dma_start(out=dstv, in_=bvst[:]), reads=[bvst.b],
                 writes=[scr_dt["v"][Tn].b], stream="bvst")
        dsty = scr["yaT"][:, :, t0:t0 + TT].rearrange("c p t -> p c t")
        cx.S.dma("gpsimd", lambda e, dsty=dsty: e.dma_start(out=dsty, in_=yaTs[:]), reads=[yaTs.b],
                 writes=[scr_dt["ya"][Tn].b], stream="yaTs")
        if hbs is not None:
            for j in range(4):
                norm.post(hbs[j], hT2[(Tn + 1) % 2], j)

    for Tn in range(NT):
        _tile(Tn)


def even_pass2(cx, c, SL, wout_ap, scr, scr_dt, ypart_ap, pipe=True, ar=None):
    S = cx.S
    NT = SL // TT
    NB = SL // 128
    Wo = cx.sb([128, 8, D], BF16, "Wo")
    load_weight_bf16(cx, Wo, wout_ap, D, None, "wo")
    Kc = cx.sb([128, 4, SL], BF16, "Kc")
    Vc = cx.sb([128, NB, 512], BF16, "Vc")
    qt = [cx.sb([128, 4, TT], BF16, "qt") for _ in range(2)]
    gzt = cx.sb([128, 4, TT], BF16, "gzt")
    yat = cx.sb([128, 4, TT], BF16, "yat")
    ybT = cx.sb([128, 4, TT], BF16, "ybT")
    E = [cx.sb([128, 2, TT], BF16, "E") for _ in range(3)]
    sp = [cx.sb([128, 2, TT], BF16, "sp") for _ in range(2)]
    Xr = [cx.sb([128, 2, TT], BF16, "X") for _ in range(2)]
    At = [cx.sb([128, 2, TT], BF16, "At") for _ in range(2)]
    Rb = cx.sb([128, 2, TT], BF16, "Rb")
    yo = [cx.sb([128, D], F32, "yo") for _ in range(1)]
    zz = cx.bank2(0, "zz")
    zh = [cx.bank(0, name="zh"), cx.bank(1, name="zh")]
    cc2 = [cx.bank2(1, "cc"), cx.bank2(2, "cc")]
    ch = [[cx.bank(2, name="ch"), cx.bank(3, name="ch")], [cx.bank(4, name="ch"), cx.bank(5, name="ch")]]
    oph = [cx.bank(6, name="ops"), cx.bank(7, name="ops")]
    pps = zh
    m0p = c["m0p"]
    loaded = set()

    def ensure_loaded(Tn):
        if Tn in loaded:
            return
        loaded.add(Tn)
        t0 = Tn * TT
        cx.DMA("sync", lambda e: e.dma_start(out=Kc[:, :, t0:t0 + TT],
                                             in_=scr["kT"][:, :, t0:t0 + TT].rearrange("c p t -> p c t")),
               r=[scr_dt["k"][Tn]], w=[Kc], stream="kc")
        cx.DMA("sync", lambda e: e.dma_start(out=Vc[:, Tn * 4:Tn * 4 + 4, :],
                                             in_=scr["vB"][t0:t0 + TT, :].rearrange("(j p) c -> p j c", p=128)),
               r=[scr_dt["v"][Tn]], w=[Vc], stream="vc")
        q = qt[Tn % 2]
        cx.DMA("sync", lambda e: e.dma_start(out=q[:], in_=scr["qT"][:, :, t0:t0 + TT].rearrange("c p t -> p c t")),
               r=[scr_dt["q"][Tn]], w=[q], stream=f"qt{Tn % 2}")

    def load_late(Tn):
        t0 = Tn * TT
        cx.DMA("sync", lambda e: e.dma_start(out=gzt[:], in_=scr["gzT"][:, :, t0:t0 + TT].rearrange("c p t -> p c t")),
               r=[scr_dt["gz"][Tn]], w=[gzt], stream="gzt")
        cx.DMA("sync", lambda e: e.dma_start(out=yat[:], in_=scr["yaT"][:, :, t0:t0 + TT].rearrange("c p t -> p c t")),
               r=[scr_dt["ya"][Tn]], w=[yat], stream="yat")

    def s_z(blk):
        n, Tn, hp, kb, first, last, q0 = blk
        ensure_loaded(Tn)
        q = qt[Tn % 2]
        fr = slice(q0, TT)
        for hh in range(2):
            h = 2 * hp + hh
            cx.PE(lambda e, hh=hh, h=h: e.matmul(zh[hh][:, fr], lhsT=Kc[:, h, kb * 128:(kb + 1) * 128], rhs=q[:, h, fr],
                                                 start=True, stop=True), r=[Kc, q], w=[zh[hh]])

    def s_E(blk):
        n, Tn, hp, kb, first, last, q0 = blk
        Et, spt = E[n % 3], sp[n % 2]
        fr = slice(q0, TT)
        cx.A(lambda e: e.activation(out=Et[:, :, fr], in_=zz[:, :, fr], func=AF.Exp), r=[zz], w=[Et])
        cx.A(lambda e: e.activation(out=spt[:, :, fr], in_=Et[:, :, fr], func=AF.Ln, bias=1.0), r=[Et], w=[spt])
        if kb >= Tn * 4:
            cx.G(lambda e: e.tensor_tensor(out=spt[:, :, fr], in0=spt[:, :, fr], in1=m0p[:, :, 0:TT - q0], op=ALU.mult),
                 r=[spt, m0p], w=[spt])

    def s_cum(blk):
        n, Tn, hp, kb, first, last, q0 = blk
        spt, xt_ = sp[n % 2], Xr[n % 2]
        cpair, chh = cc2[n % 2], ch[n % 2]
        fr = slice(q0, TT)
        for hh in range(2):
            cx.PE(lambda e, hh=hh: e.matmul(chh[hh][:, fr], lhsT=c["tneg"][:], rhs=spt[:, hh, fr], start=True, stop=first),
                  r=[c["tneg"], spt], w=[chh[hh]])
            if not first:
                cx.PE(lambda e, hh=hh: e.matmul(chh[hh][:, fr], lhsT=c["onesneg"][:], rhs=Rb[:, hh, fr], start=False, stop=True),
                      r=[c["onesneg"], Rb], w=[chh[hh]])
        cx.A(lambda e: e.activation(out=xt_[:, :, fr], in_=cpair[:, :, fr], func=AF.Exp), r=[cpair], w=[xt_])
        if first:
            cx.G(lambda e: e.memset(Rb[:], 0.0), w=[Rb])
        if not last:
            cx.G(lambda e: e.tensor_tensor(out=Rb[:, :, fr], in0=Rb[:, :, fr], in1=spt[:, :, fr], op=ALU.add),
                 r=[Rb, spt], w=[Rb])

    def s_fin(blk):
        n, Tn, hp, kb, first, last, q0 = blk
        Et, xt_, at = E[n % 3], Xr[n % 2], At[n % 2]
        fr = slice(q0, TT)
        cx.V(lambda e: e.tensor_tensor(out=at[:, :, fr], in0=Et[:, :, fr], in1=xt_[:, :, fr], op=ALU.mult),
             r=[Et, xt_], w=[at])
        if kb >= Tn * 4:
            cx.V(lambda e: e.tensor_tensor(out=at[:, :, fr], in0=at[:, :, fr], in1=m0p[:, :, 0:TT - q0], op=ALU.mult),
                 r=[at, m0p], w=[at])
        for hh in range(2):
            h = 2 * hp + hh
            cx.PE(lambda e, hh=hh, h=h: e.matmul(oph[hh][:, fr], lhsT=Vc[:, kb, h * 128:(h + 1) * 128], rhs=at[:, hh, fr],
                                                 start=first, stop=last, skip_group_check=True), r=[Vc, at], w=[oph[hh]])
        if last:
            for hh in range(2):
                h = 2 * hp + hh
                cx.V(lambda e, hh=hh, h=h: e.tensor_tensor(out=ybT[:, h, :], in0=oph[hh][:], in1=gzt[:, h, :], op=ALU.mult),
                     r=[oph[hh], gzt], w=[ybT])

    def outproj(Tn):
        t0 = Tn * TT
        for j in range(4):
            yot = yo[0]
            for half in range(2):
                pp = pps[half]
                for kc in range(8):
                    src = yat if kc < 4 else ybT
                    cx.PE(lambda e, pp=pp, kc=kc, src=src, j=j, half=half: e.matmul(
                        pp[:], lhsT=src[:, kc % 4, j * 128:(j + 1) * 128], rhs=Wo[:, kc, half * 512:(half + 1) * 512],
                        start=(kc == 0), stop=(kc == 7)), r=[src, Wo], w=[pp])
                if half == 0:
                    cx.V(lambda e, pp=pp, yot=yot: e.tensor_copy(out=yot[:, 0:512], in_=pp[:]), r=[pp], w=[yot])
                else:
                    cx.A(lambda e, pp=pp, yot=yot: e.activation(out=yot[:, 512:1024], in_=pp[:], func=AF.Copy),
                         r=[pp], w=[yot])
            cx.DMA("gpsimd", lambda e, yot=yot, j=j, t0=t0: e.dma_start(out=ypart_ap[t0 + j * 128:t0 + (j + 1) * 128, :],
                                                                       in_=yot[:]), r=[yot],
                   w=([ar.store_dt(t0 + j * 128)] if ar is not None else []), stream="yo0")
        if ar is not None:
            ar.after_tile(Tn)

    blocks = []
    n = 0
    for Tn in range(NT):
        for hp in range(2):
            kbs = list(range(Tn * 4 + 3, -1, -1))
            for i, kb in enumerate(kbs):
                j = kb - Tn * 4
                q0 = 128 * j if j > 0 else 0
                blocks.append((n, Tn, hp, kb, i == 0, i == len(kbs) - 1, q0))
                n += 1
    if DBG.get("stop") == "p2load":
        return
    NBk = len(blocks)
    late_done = set()
    for t in range(-3, NBk):
        if 0 <= t + 2 < NBk:
            s_E(blocks[t + 2])
        if 0 <= t + 1 < NBk:
            s_cum(blocks[t + 1])
        if 0 <= t < NBk:
            blk = blocks[t]
            if blk[1] not in late_done:
                late_done.add(blk[1])
                load_late(blk[1])
            s_fin(blk)
            if t + 1 == NBk or blocks[t + 1][1] != blk[1]:
                outproj(blk[1])
        if 0 <= t + 3 < NBk:
            s_z(blocks[t + 3])


def even_pass2_old(cx, c, SL, wout_ap, scr, scr_dt, ypart_ap, pipe=True, ar=None):
    S = cx.S
    NT = SL // TT
    NB = SL // 128
    Wo = cx.sb([128, 8, D], BF16, "Wo")
    stg = [cx.sb([128, 1024], F32, "wstg") for _ in range(2)]
    load_weight_bf16(cx, Wo, wout_ap, D, stg, "wo")
    Kc = cx.sb([128, 4, SL], BF16, "Kc")
    Vc = cx.sb([128, NB, 512], BF16, "Vc")
    qt = [cx.sb([128, 4, TT], BF16, "qt") for _ in range(2)]
    gzt = cx.sb([128, 4, TT], BF16, "gzt")
    yat = cx.sb([128, 4, TT], BF16, "yat")
    ybT = cx.sb([128, 4, TT], BF16, "ybT")
    NBUF = 3
    E = [cx.sb([128, TT], BF16, "E") for _ in range(4)]
    sp = [cx.sb([128, TT], BF16, "sp") for _ in range(NBUF)]
    At = [cx.sb([128, TT], BF16, "At") for _ in range(2)]
    Rb = cx.sb([128, TT], BF16, "Rb")
    yo = [cx.sb([128, D], F32, "yo") for _ in range(2)]
    zps = [cx.bank(0, name="zps"), cx.bank(1, name="zps")]
    cps = [cx.bank(2, name="cps"), cx.bank(3, name="cps")]
    ops_ = [cx.bank(4, name="ops"), cx.bank(5, name="ops")]
    pps = [cx.bank(6, name="pps"), cx.bank(7, name="pps")]
    m0 = c["m0"]

    Xr = [cx.sb([128, TT], BF16, "X") for _ in range(3)]
    loaded = set()

    def ensure_loaded(Tn):
        if Tn in loaded:
            return
        loaded.add(Tn)
        t0 = Tn * TT
        cx.DMA("sync", lambda e: e.dma_start(out=Kc[:, :, t0:t0 + TT],
                                             in_=scr["kT"][:, :, t0:t0 + TT].rearrange("c p t -> p c t")),
               r=[scr_dt["k"][Tn]], w=[Kc], stream="kc")
        cx.DMA("sync", lambda e: e.dma_start(out=Vc[:, Tn * 4:Tn * 4 + 4, :],
                                             in_=scr["vB"][t0:t0 + TT, :].rearrange("(j p) c -> p j c", p=128)),
               r=[scr_dt["v"][Tn]], w=[Vc], stream="vc")
        q = qt[Tn % 2]
        cx.DMA("sync", lambda e: e.dma_start(out=q[:], in_=scr["qT"][:, :, t0:t0 + TT].rearrange("c p t -> p c t")),
               r=[scr_dt["q"][Tn]], w=[q], stream=f"qt{Tn % 2}")

    def load_late(Tn):
        t0 = Tn * TT
        cx.DMA("sync", lambda e: e.dma_start(out=gzt[:], in_=scr["gzT"][:, :, t0:t0 + TT].rearrange("c p t -> p c t")),
               r=[scr_dt["gz"][Tn]], w=[gzt], stream="gzt")
        cx.DMA("sync", lambda e: e.dma_start(out=yat[:], in_=scr["yaT"][:, :, t0:t0 + TT].rearrange("c p t -> p c t")),
               r=[scr_dt["ya"][Tn]], w=[yat], stream="yat")

    def s_z(blk):
        n, Tn, h, kb, first, last, q0 = blk
        ensure_loaded(Tn)
        z = zps[n % 2]
        q = qt[Tn % 2]
        fr = slice(q0, TT)
        cx.PE(lambda e: e.matmul(z[:, fr], lhsT=Kc[:, h, kb * 128:(kb + 1) * 128], rhs=q[:, h, fr],
                                 start=True, stop=True), r=[Kc, q], w=[z])

    def s_E(blk):
        n, Tn, h, kb, first, last, q0 = blk
        z = zps[n % 2]
        Et = E[n % 4]
        fr = slice(q0, TT)
        cx.A(lambda e: e.activation(out=Et[:, fr], in_=z[:, fr], func=AF.Exp), r=[z], w=[Et])

    def s_ln(blk):
        n, Tn, h, kb, first, last, q0 = blk
        Et, spt = E[n % 4], sp[n % NBUF]
        fr = slice(q0, TT)
        cx.A(lambda e: e.activation(out=spt[:, fr], in_=Et[:, fr], func=AF.Ln, bias=1.0), r=[Et], w=[spt])
        if kb >= Tn * 4:
            cx.G(lambda e: e.tensor_tensor(out=spt[:, fr], in0=spt[:, fr], in1=m0[:, 0:TT - q0], op=ALU.mult),
                 r=[spt, m0], w=[spt])

    def s_cum(blk):
        n, Tn, h, kb, first, last, q0 = blk
        cp = cps[n % 2]
        spt = sp[n % NBUF]
        fr = slice(q0, TT)
        cx.PE(lambda e: e.matmul(cp[:, fr], lhsT=c["tneg"][:], rhs=spt[:, fr], start=True, stop=first),
              r=[c["tneg"], spt], w=[cp])
        if not first:
            cx.PE(lambda e: e.matmul(cp[:, fr], lhsT=c["onesneg"][:], rhs=Rb[:, fr], start=False, stop=True),
                  r=[c["onesneg"], Rb], w=[cp])

    def s_X(blk):
        n, Tn, h, kb, first, last, q0 = blk
        cp = cps[n % 2]
        spt, xt_ = sp[n % NBUF], Xr[n % 3]
        fr = slice(q0, TT)
        cx.A(lambda e: e.activation(out=xt_[:, fr], in_=cp[:, fr], func=AF.Exp), r=[cp], w=[xt_])
        if first:
            cx.G(lambda e: e.memset(Rb[:], 0.0), w=[Rb])
        if not last:
            cx.G(lambda e: e.tensor_tensor(out=Rb[:, fr], in0=Rb[:, fr], in1=spt[:, fr], op=ALU.add),
                 r=[Rb, spt], w=[Rb])

    def s_fin(blk):
        n, Tn, h, kb, first, last, q0 = blk
        Et, xt_, at = E[n % 4], Xr[n % 3], At[n % 2]
        g = Tn * 4 + h
        op_ = ops_[g % 2]
        fr = slice(q0, TT)
        cx.V(lambda e: e.tensor_tensor(out=at[:, fr], in0=Et[:, fr], in1=xt_[:, fr], op=ALU.mult),
             r=[Et, xt_], w=[at])
        if kb >= Tn * 4:
            cx.V(lambda e: e.tensor_tensor(out=at[:, fr], in0=at[:, fr], in1=m0[:, 0:TT - q0], op=ALU.mult),
                 r=[at, m0], w=[at])
        cx.PE(lambda e: e.matmul(op_[:, fr], lhsT=Vc[:, kb, h * 128:(h + 1) * 128], rhs=at[:, fr],
                                 start=first, stop=last, skip_group_check=True), r=[Vc, at], w=[op_])
        if last:
            cx.V(lambda e: e.tensor_tensor(out=ybT[:, h, :], in0=op_[:], in1=gzt[:, h, :], op=ALU.mult),
                 r=[op_, gzt], w=[ybT])

    def outproj(Tn):
        t0 = Tn * TT
        for j in range(4):
            yot = yo[j % 2]
            for half in range(2):
                pp = pps[half]
                for kc in range(8):
                    src = yat if kc < 4 else ybT
                    cx.PE(lambda e, pp=pp, kc=kc, src=src, j=j, half=half: e.matmul(
                        pp[:], lhsT=src[:, kc % 4, j * 128:(j + 1) * 128], rhs=Wo[:, kc, half * 512:(half + 1) * 512],
                        start=(kc == 0), stop=(kc == 7)), r=[src, Wo], w=[pp])
                if half == 0:
                    cx.V(lambda e, pp=pp, yot=yot: e.tensor_copy(out=yot[:, 0:512], in_=pp[:]), r=[pp], w=[yot])
                else:
                    cx.A(lambda e, pp=pp, yot=yot: e.activation(out=yot[:, 512:1024], in_=pp[:], func=AF.Copy),
                         r=[pp], w=[yot])
            cx.DMA("gpsimd", lambda e, yot=yot, j=j, t0=t0: e.dma_start(out=ypart_ap[t0 + j * 128:t0 + (j + 1) * 128, :],
                                                                       in_=yot[:]), r=[yot],
                   w=([ar.store_dt(t0 + j * 128)] if ar is not None else []), stream=f"yo{j % 2}")
        if ar is not None:
            ar.after_tile(Tn)

    blocks = []
    n = 0
    for Tn in range(NT):
        for h in range(4):
            kbs = list(range(Tn * 4 + 3, -1, -1))
            for i, kb in enumerate(kbs):
                j = kb - Tn * 4
                q0 = 128 * j if j > 0 else 0
                blocks.append((n, Tn, h, kb, i == 0, i == len(kbs) - 1, q0))
                n += 1
    if DBG.get("stop") == "p2load":
        return
    NBk = len(blocks)
    late_done = set()
    for t in range(-3, NBk):
        if 0 <= t + 3 < NBk:
            s_z(blocks[t + 3])
        if 0 <= t + 2 < NBk:
            s_E(blocks[t + 2])
            s_ln(blocks[t + 2])
        if 0 <= t + 1 < NBk:
            s_cum(blocks[t + 1])
            s_X(blocks[t + 1])
        if 0 <= t < NBk:
            blk = blocks[t]
            if blk[1] not in late_done:
                late_done.add(blk[1])
                load_late(blk[1])
            s_fin(blk)
            if t + 1 == NBk or blocks[t + 1][1] != blk[1]:
                outproj(blk[1])


def _even_scratch(nc, SL):
    scr = {}
    for k in ("qT", "kT", "gzT", "yaT"):
        scr[k] = nc.dram_tensor("scr_" + k, [4, 128, SL], BF16).ap()
    scr["vB"] = nc.dram_tensor("scr_vB", [SL, 512], BF16).ap()
    scr_dt = {k: [DT(f"{k}{t}") for t in range(SL // TT)] for k in ("q", "k", "gz", "v", "ya")}
    return scr, scr_dt


def build_even(SL, n_parts, pipe=True):
    nc = bass.Bass("TRN2", target_bir_lowering=False)
    inp = lambda name, shape: nc.dram_tensor(name, list(shape), F32, kind="ExternalInput").ap()
    x = inp("x", [SL, D])
    parts = [inp(f"yp{k}", [SL, D]) for k in range(n_parts)]
    gpost = inp("gpost", [D]) if n_parts else None
    gpre = inp("gpre", [D])
    win = inp("win", [D, NCOL_E])
    convw = inp("convw", [128, 8, 4])
    bi = inp("bi", [2, 1])
    bf = inp("bf", [2, 1])
    hng = inp("hng", [512])
    wout = inp("wout", [D, D])
    xcur = nc.dram_tensor("xcur", [SL, D], F32, kind="ExternalOutput").ap() if n_parts else None
    ypart = nc.dram_tensor("ypart", [SL, D], F32, kind="ExternalOutput").ap()
    scr, scr_dt = _even_scratch(nc, SL)
    S = Sched(nc)
    cx = Ctx(nc, S)
    c = make_consts(cx)
    mark = cx.off
    even_pass1(cx, c, SL, x, parts, gpost, gpre, xcur, win, convw, bi, bf, hng, scr, scr_dt)
    if DBG.get("stop") is None or DBG.get("stop").startswith("p2"):
        S.barrier()
        cx.off = mark
        even_pass2(cx, c, SL, wout, scr, scr_dt, ypart, pipe=pipe)
    S.finish()
    return nc, S


def prep_even(inp, e, layer, c):
    w = np.asarray(inp["w_in_ab"][e])
    A = lambda g: w[:, g * 1024 + 512 * c: g * 1024 + 512 * c + 512]
    Bq = lambda g: w[:, 5128 + g * 1024 + 512 * c: 5128 + g * 1024 + 512 * c + 512]
    gi = w[:, 5120 + 2 * c: 5122 + 2 * c]
    gf = w[:, 5124 + 2 * c: 5126 + 2 * c]
    win = np.concatenate([A(0), A(1), Bq(0), Bq(1), Bq(3), A(2), A(3), A(4), Bq(2), gi, gf], axis=1)
    cq = np.asarray(inp["conv_qk"][e])
    cols = np.concatenate([cq[:, 512 * c:512 * c + 512], cq[:, 1024 + 512 * c:1024 + 512 * c + 512]], axis=1)
    convw = np.ascontiguousarray(cols.reshape(4, 8, 128).transpose(2, 1, 0))
    wo = np.asarray(inp["w_out_ab"][e])
    wout = np.concatenate([wo[512 * c:512 * c + 512], wo[1024 + 512 * c:1024 + 512 * c + 512]], axis=0)
    d = dict(
        gpre=np.ascontiguousarray(inp["pre_norm_g"][layer]),
        win=np.ascontiguousarray(win), convw=convw,
        bi=np.ascontiguousarray(np.asarray(inp["bias_i"][e])[2 * c:2 * c + 2].reshape(2, 1)),
        bf=np.ascontiguousarray(np.asarray(inp["bias_f"][e])[2 * c:2 * c + 2].reshape(2, 1)),
        hng=np.ascontiguousarray(np.asarray(inp["head_norm_g"][e])[512 * c:512 * c + 512]),
        wout=np.ascontiguousarray(wout),
    )
    if layer > 0:
        d["gpost"] = np.ascontiguousarray(inp["post_norm_g"][layer - 1])
    return {k: np.asarray(v, dtype=np.float32) for k, v in d.items()}


POOL_WINDOWS = (2, 4, 8, 16)
HALO = 16
SLOT_W = ((2, 4), (16, 8))


def odd_pass(cx, c, SL, wsel_ap, x_ap, parts, gpost_ap, gpre_ap, xcur_ap, win_ap, pw_ap, pscale_ap, wout_ap, ypart_ap,
             ar=None, parts_dt=None):
    NT = SL // TT
    Wc = cx.sb([128, 8, 2048], BF16, "Wc")
    stg = [cx.sb([128, 1024], F32, "wstg") for _ in range(2)]
    load_weight_bf16(cx, Wc, win_ap, 2048, stg, "wc")
    PW = cx.sb([128, 8, 512], BF16, "PW")
    load_weight_bf16(cx, PW, pw_ap, 512, stg, "pw")
    Wo = cx.sb([128, 8, D], BF16, "Wo")
    load_weight_bf16(cx, Wo, wout_ap, D, stg, "wo")
    pscale = cx.sb([128, 8], F32, "pscale")
    cx.DMA("sync", lambda e: e.dma_start(out=pscale[:], in_=pscale_ap), w=[pscale], stream="pscale")
    wsel = cx.sb([128, 4], F32, "wsel")
    cx.DMA("sync", lambda e: e.dma_start(out=wsel[:], in_=wsel_ap), w=[wsel], stream="wsel")
    kco = cx.sb([128, 4], F32, "kco")
    invc0 = []
    for si in range(2):
        for wi in range(2):
            w = SLOT_W[si][wi]
            col = 2 * si + wi
            cx.V(lambda e, col=col, w=w: e.tensor_scalar_mul(out=kco[:, col:col + 1], in0=wsel[:, col:col + 1],
                                                             scalar1=1.0 / w), r=[wsel], w=[kco])
            t = cx.sb([128, TT], F32, "invc0")
            cx.G(lambda e, t=t, w=w: e.memset(t[:], 1.0 / w), w=[t])
            for k in range(w - 1):
                cx.G(lambda e, t=t, k=k: e.memset(t[:, k:k + 1], 1.0 / (k + 1)), w=[t])
            cx.V(lambda e, t=t, col=col: e.tensor_scalar_mul(out=t[:], in0=t[:], scalar1=wsel[:, col:col + 1]),
                 r=[t, wsel], w=[t])
            invc0.append(t)
    norm = NormStage(cx, c, x_ap, parts, gpost_ap, gpre_ap, xcur_ap, "no", parts_dt=parts_dt, nhb=4, nyt=2)
    hT2 = [cx.sb([128, 8, TT], BF16, "hT") for _ in range(3)]
    pb = [cx.sb([128, HALO + TT], F32, "pb") for _ in range(8)]
    for t in pb:
        cx.G(lambda e, t=t: e.memset(t[:, 0:HALO], 0.0), w=[t])
    LV = [cx.sb([128, HALO + TT], F32, "lv") for _ in range(4)]
    tC = cx.sb([128, TT], F32, "tC")
    tD = cx.sb([128, TT], F32, "tD")
    pl2 = [cx.sb([128, 8, TT], BF16, "pl") for _ in range(2)]
    gz = [cx.sb([128, TT], BF16, "gz") for _ in range(8)]
    yT = cx.sb([128, 8, TT], BF16, "yT")
    yo = [cx.sb([128, D], F32, "yo") for _ in range(1)]
    pj = [cx.bank(0, name="pj"), cx.bank(1, name="pj"), cx.bank(2, name="pj")]
    pps = [cx.bank(3, name="pps"), cx.bank(4, name="pps")]
    pjn = [0]

    def pslot():
        p = pj[pjn[0] % 3]
        pjn[0] += 1
        return p

    W_ = HALO + TT
    for tt_ in range(min(2, NT)):
        for j in range(4):
            norm.run(tt_ * TT + j * 128, hT2[tt_], j)

    def stageA(Tn):
        hT = hT2[Tn % 3]
        pl = pl2[Tn % 2]
        for cc in range(8):
            gi = cc // 4
            ps = pslot()
            for kc in range(8):
                cx.PE(lambda e, ps=ps, kc=kc, cc=cc: e.matmul(ps[:], lhsT=Wc[:, kc, cc * 128:(cc + 1) * 128],
                                                              rhs=hT[:, kc, :], start=(kc == 0), stop=(kc == 7)),
                      r=[Wc, hT], w=[ps])
            pbt = pb[cc]
            cx.A(lambda e, ps=ps, pbt=pbt: e.activation(out=pbt[:, HALO:W_], in_=ps[:], func=AF.Copy), r=[ps], w=[pbt])
            w1, w2 = SLOT_W[gi]
            wmax = max(w1, w2)
            src, sh, lo, li = pbt, 1, 1, 0
            while sh < wmax:
                dst = LV[li]
                eng = cx.V if (li % 2 == 0) else cx.G
                eng(lambda e, src=src, dst=dst, sh=sh, lo=lo: e.tensor_tensor(out=dst[:, lo:W_], in0=src[:, lo:W_],
                                                                              in1=src[:, lo - sh:W_ - sh], op=ALU.add),
                    r=[src], w=[dst])
                src = dst
                sh *= 2
                lo += sh
                li += 1
            s1 = LV[int(math.log2(w1)) - 1]
            s2_ = LV[int(math.log2(w2)) - 1]
            c1, c2 = 2 * gi, 2 * gi + 1
            if Tn == 0:
                cx.V(lambda e, s1=s1, c1=c1: e.tensor_tensor(out=tC[:], in0=s1[:, HALO:W_], in1=invc0[c1][:], op=ALU.mult),
                     r=[s1, invc0[c1]], w=[tC])
                cx.G(lambda e, s2_=s2_, c2=c2: e.tensor_tensor(out=tD[:], in0=s2_[:, HALO:W_], in1=invc0[c2][:], op=ALU.mult),
                     r=[s2_, invc0[c2]], w=[tD])
                cx.V(lambda e: e.tensor_tensor(out=tC[:], in0=tC[:], in1=tD[:], op=ALU.add), r=[tC, tD], w=[tC])
            else:
                cx.V(lambda e, s1=s1, c1=c1, pbt=pbt: e.scalar_tensor_tensor(out=tC[:], in0=s1[:, HALO:W_], scalar=kco[:, c1:c1 + 1],
                                                                             in1=pbt[:, HALO:W_], op0=ALU.mult, op1=ALU.subtract),
                     r=[s1, kco, pbt], w=[tC])
                cx.V(lambda e, s2_=s2_, c2=c2, cc=cc: e.scalar_tensor_tensor(out=pl[:, cc, :], in0=s2_[:, HALO:W_], scalar=kco[:, c2:c2 + 1],
                                                                             in1=tC[:], op0=ALU.mult, op1=ALU.add),
                     r=[s2_, kco, tC], w=[pl])
            if Tn == 0:
                cx.V(lambda e, pbt=pbt, cc=cc: e.tensor_tensor(out=pl[:, cc, :], in0=tC[:], in1=pbt[:, HALO:W_],
                                                               op=ALU.subtract), r=[tC, pbt], w=[pl])
            cx.G(lambda e, pbt=pbt: e.tensor_copy(out=pbt[:, 0:HALO], in_=pbt[:, TT:W_]), r=[pbt], w=[pbt])

    def stageB(Tn):
        t0 = Tn * TT
        hT = hT2[Tn % 3]
        pl = pl2[Tn % 2]
        for cc in range(8):
            pz = pslot()
            for kc in range(8):
                cx.PE(lambda e, pz=pz, kc=kc, cc=cc: e.matmul(pz[:], lhsT=Wc[:, kc, 1024 + cc * 128:1024 + (cc + 1) * 128],
                                                              rhs=hT[:, kc, :], start=(kc == 0), stop=(kc == 7)),
                      r=[Wc, hT], w=[pz])
            gzt = gz[cc]
            cx.A(lambda e, pz=pz, gzt=gzt: e.activation(out=gzt[:], in_=pz[:], func=AF.Silu), r=[pz], w=[gzt])
        for cc in range(8):
            gi, ec = cc // 4, cc % 4
            gzt = gz[cc]
            pm = pslot()
            for kc in range(4):
                cx.PE(lambda e, pm=pm, kc=kc, gi=gi, ec=ec: e.matmul(pm[:], lhsT=PW[:, gi * 4 + kc, ec * 128:(ec + 1) * 128],
                                                                      rhs=pl[:, gi * 4 + kc, :], start=(kc == 0), stop=(kc == 3)),
                      r=[PW, pl], w=[pm])
            cx.V(lambda e, pm=pm, gzt=gzt, cc=cc: e.scalar_tensor_tensor(out=yT[:, cc, :], in0=pm[:], scalar=pscale[:, cc:cc + 1],
                                                                         in1=gzt[:], op0=ALU.mult, op1=ALU.mult),
                 r=[pm, pscale, gzt], w=[yT])
        for j in range(4):
            yot = yo[0]
            for half in range(2):
                pp = pps[half]
                for kc in range(8):
                    cx.PE(lambda e, pp=pp, kc=kc, j=j, half=half: e.matmul(
                        pp[:], lhsT=yT[:, kc, j * 128:(j + 1) * 128], rhs=Wo[:, kc, half * 512:(half + 1) * 512],
                        start=(kc == 0), stop=(kc == 7)), r=[yT, Wo], w=[pp])
                if half == 0:
                    cx.V(lambda e, pp=pp, yot=yot: e.tensor_copy(out=yot[:, 0:512], in_=pp[:]), r=[pp], w=[yot])
                else:
                    cx.A(lambda e, pp=pp, yot=yot: e.activation(out=yot[:, 512:1024], in_=pp[:], func=AF.Copy),
                         r=[pp], w=[yot])
            cx.DMA("gpsimd", lambda e, yot=yot, j=j, t0=t0: e.dma_start(out=ypart_ap[t0 + j * 128:t0 + (j + 1) * 128, :],
                                                                       in_=yot[:]), r=[yot],
                   w=([ar.store_dt(t0 + j * 128)] if ar is not None else []), stream="yo0")
        if ar is not None:
            ar.after_tile(Tn)

    stageA(0)
    for Tn in range(NT):
        hbs = [norm.pre((Tn + 2) * TT + j * 128) for j in range(4)] if Tn + 2 < NT else None
        if Tn + 1 < NT:
            stageA(Tn + 1)
        stageB(Tn)
        if hbs is not None:
            for j in range(4):
                norm.post(hbs[j], hT2[(Tn + 2) % 3], j)


def build_odd(SL, n_parts):
    nc = bass.Bass("TRN2", target_bir_lowering=False)
    inp = lambda name, shape: nc.dram_tensor(name, list(shape), F32, kind="ExternalInput").ap()
    x = inp("x", [SL, D])
    parts = [inp(f"yp{k}", [SL, D]) for k in range(n_parts)]
    gpost = inp("gpost", [D]) if n_parts else None
    gpre = inp("gpre", [D])
    win = inp("win", [D, 2048])
    pw = inp("pw", [1024, 512])
    pscale = inp("pscale", [128, 8])
    wsel = inp("wsel", [128, 4])
    wout = inp("wout", [D, D])
    xcur = nc.dram_tensor("xcur", [SL, D], F32, kind="ExternalOutput").ap() if n_parts else None
    ypart = nc.dram_tensor("ypart", [SL, D], F32, kind="ExternalOutput").ap()
    S = Sched(nc)
    cx = Ctx(nc, S)
    c = make_consts(cx)
    odd_pass(cx, c, SL, wsel, x, parts, gpost, gpre, xcur, win, pw, pscale, wout, ypart)
    S.finish()
    return nc, S


def prep_odd(inp, o, layer, groups):
    w = np.asarray(inp["w_in_c"][o])
    pcols = np.concatenate([w[:, 512 * g:512 * g + 512] for g in groups], axis=1)
    zcols = np.concatenate([w[:, 2048 + 512 * g:2048 + 512 * g + 512] for g in groups], axis=1)
    pwv = np.asarray(inp["pool_w"][o])
    pw = np.concatenate([pwv[g] for g in groups], axis=0)
    sc = np.concatenate([np.asarray(inp["pool_scale"][o])[512 * g:512 * g + 512] for g in groups])
    wo = np.asarray(inp["w_out_c"][o])
    wout = np.concatenate([wo[512 * g:512 * g + 512] for g in groups], axis=0)
    d = dict(
        gpre=inp["pre_norm_g"][layer], gpost=inp["post_norm_g"][layer - 1],
        win=np.concatenate([pcols, zcols], axis=1), pw=pw,
        pscale=sc.reshape(8, 128).T, wout=wout,
        wsel=np.tile(np.array([[float(POOL_WINDOWS[groups[si]] == SLOT_W[si][wi]) for si in range(2) for wi in range(2)]],
                              dtype=np.float32), (128, 1)),
    )
    return {k: np.ascontiguousarray(np.asarray(v, dtype=np.float32)) for k, v in d.items()}


def build_combine(NTOK):
    nc = bass.Bass("TRN2", target_bir_lowering=False)
    inp = lambda name, shape: nc.dram_tensor(name, list(shape), F32, kind="ExternalInput").ap()
    x = inp("x", [NTOK, D])
    parts = [inp(f"yp{k}", [NTOK, D]) for k in range(2)]
    gpost = inp("gpost", [D])
    gpre = inp("gpre", [D])
    out = nc.dram_tensor("xcur", [NTOK, D], F32, kind="ExternalOutput").ap()
    S = Sched(nc)
    cx = Ctx(nc, S)
    c = make_consts(cx)
    norm = NormStage(cx, c, x, parts, gpost, gpre, out, "nc")
    for r0 in range(0, NTOK, 128):
        norm.run(r0, None, 0)
    S.finish()
    return nc, S


ODD_GROUPS = ((0, 3), (1, 2))
_PROG = {}

EVEN_KEYS = ("gpre", "win", "convw", "bi", "bf", "hng", "wout")
ODD_KEYS = ("gpre", "win", "pw", "pscale", "wsel", "wout")
EVEN_SHAPES = dict(gpre=[D], gpost=[D], win=[D, NCOL_E], convw=[128, 8, 4], bi=[2, 1], bf=[2, 1], hng=[512], wout=[D, D])
ODD_SHAPES = dict(gpre=[D], gpost=[D], win=[D, 2048], pw=[1024, 512], pscale=[128, 8], wsel=[128, 4], wout=[D, D])


def build_fused(SL):
    nc = bass.Bass("TRN2", target_bir_lowering=False)
    inp = lambda name, shape: nc.dram_tensor(name, list(shape), F32, kind="ExternalInput").ap()
    x = inp("x", [SL, D])
    wts = []
    for l in range(DEPTH):
        shapes = EVEN_SHAPES if l % 2 == 0 else ODD_SHAPES
        keys = (EVEN_KEYS if l % 2 == 0 else ODD_KEYS) + (("gpost",) if l > 0 else ())
        wts.append({k: inp(f"{k}_l{l}", shapes[k]) for k in keys})
    gfin = inp("gpost_fin", [D])
    out = nc.dram_tensor("out", [SL, D], F32, kind="ExternalOutput").ap()
    ypart = [nc.dram_tensor(f"ypart{l}", [SL, D], F32).ap() for l in range(DEPTH)]
    ysum = [nc.dram_tensor(f"ysum{l}", [SL, D], F32).ap() for l in range(DEPTH)]
    xcur = [None] + [nc.dram_tensor(f"xcur{l}", [SL, D], F32).ap() for l in range(1, DEPTH)]
    scr, _ = _even_scratch(nc, SL)
    S = Sched(nc)
    cx = Ctx(nc, S)
    c = make_consts(cx)
    mark = cx.off
    prev_ar = None
    for l in range(DEPTH):
        w = wts[l]
        x_l = x if l <= 1 else xcur[l - 1]
        parts = [] if l == 0 else [ysum[l - 1]]
        parts_dt = None if l == 0 else [prev_ar.sum_dt]
        ar = ARHook(cx, ypart[l], ysum[l], SL)
        if l % 2 == 0:
            scr_dt = {k: [DT(f"{k}{t}") for t in range(SL // TT)] for k in ("q", "k", "gz", "v", "ya")}
            even_pass1(cx, c, SL, x_l, parts, w.get("gpost"), w["gpre"], xcur[l], w["win"], w["convw"], w["bi"], w["bf"],
                       w["hng"], scr, scr_dt, parts_dt=parts_dt)
            S.barrier()
            cx.off = mark
            even_pass2(cx, c, SL, w["wout"], scr, scr_dt, ypart[l], ar=ar)
        else:
            odd_pass(cx, c, SL, w["wsel"], x_l, parts, w.get("gpost"), w["gpre"], xcur[l], w["win"], w["pw"], w["pscale"],
                     w["wout"], ypart[l], ar=ar, parts_dt=parts_dt)
        S.barrier()
        cx.off = mark
        prev_ar = ar
    norm = NormStage(cx, c, xcur[DEPTH - 1], [ysum[DEPTH - 1]], gfin, wts[0]["gpre"], out, "nf", parts_dt=[prev_ar.sum_dt],
                     nyt=2)
    for r0 in range(0, SL, 128):
        norm.run(r0, None, 0)
    S.finish()
    return nc, S


def kernel(**inputs):
    inp = {k: np.asarray(v) for k, v in inputs.items()}
    x = np.ascontiguousarray(inp["x"], dtype=np.float32)
    B, SL, _ = x.shape
    maps = []
    for b in range(B):
        for c in range(2):
            d = {"x": x[b], "gpost_fin": np.ascontiguousarray(inp["post_norm_g"][DEPTH - 1], dtype=np.float32)}
            for l in range(DEPTH):
                p = prep_even(inp, l // 2, l, c) if l % 2 == 0 else prep_odd(inp, l // 2, l, ODD_GROUPS[c])
                if l == 0:
                    p.pop("gpost", None)
                for k, v in p.items():
                    d[f"{k}_l{l}"] = v
            maps.append(d)
    if ("fused", SL) not in _PROG:
        _PROG[("fused", SL)] = build_fused(SL)[0]
    res = run_bass_kernel_spmd(_PROG[("fused", SL)], maps, core_ids=list(range(2 * B))).results
    return np.stack([res[2 * b]["out"] for b in range(B)], axis=0).astype(np.float32)
```

```python
import math
import numpy as np
import concourse.bass as bass
import concourse.mybir as mybir
from concourse.bass_utils import run_bass_kernel_spmd

F32 = mybir.dt.float32
BF16 = mybir.dt.bfloat16
AF = mybir.ActivationFunctionType
ALU = mybir.AluOpType
AX = mybir.AxisListType

D = 1024
SEQ = 8192
BATCH = 4
DEPTH = 4
EPS = 1e-6
TT = 512
LN16 = math.log(16.0)
DBG = {}


class Buf:
    __slots__ = ("name", "writers", "readers")

    def __init__(self, name):
        self.name = name
        self.writers = []
        self.readers = []


class Op:
    __slots__ = ("eng", "fn", "deps", "signal", "ev", "stream", "idx", "is_dma", "inc")

    def __init__(self, eng, fn, stream, is_dma):
        self.inc = 16 if is_dma else 1
        self.eng = eng
        self.fn = fn
        self.deps = set()
        self.signal = False
        self.ev = None
        self.stream = stream
        self.is_dma = is_dma


class Sched:
    ENGS = ("sync", "scalar", "vector", "gpsimd", "tensor")

    def __init__(self, nc, tag=""):
        self.nc = nc
        self.ops = []
        self.tag = tag
        self.bar = None
        self.last_eng = {}
        self.last_stream = {}

    def barrier(self):
        deps = set(self.last_eng.values()) | {v for k, v in self.last_stream.items() if k != "cc"}
        dummy = self.dummy
        op = self._add("vector", lambda e: e.memset(dummy[0:1, 0:1], 0.0), (), (), None, False)
        op.deps |= deps
        op.deps.discard(op.idx)
        self.bar = op.idx
        return op

    def _add(self, eng, fn, reads, writes, stream, is_dma):
        op = Op(eng, fn, stream, is_dma)
        op.idx = len(self.ops)
        ops = self.ops
        for b in reads:
            op.deps.update(b.writers)
        for b in writes:
            op.deps.update(b.writers)
            op.deps.update(b.readers)
        op.deps.discard(op.idx)
        if eng == "tensor" and not is_dma:
            op.deps = {d for d in op.deps if not (ops[d].eng == "tensor" and not ops[d].is_dma)}
        if self.bar is not None:
            op.deps.add(self.bar)
        for b in reads:
            b.readers.append(op.idx)
        for b in writes:
            b.writers = [op.idx]
            b.readers = []
        ops.append(op)
        if is_dma:
            self.last_stream[stream] = op.idx
        else:
            self.last_eng[eng] = op.idx
        return op

    def op(self, eng, fn, reads=(), writes=()):
        return self._add(eng, fn, reads, writes, None, False)

    def dma(self, eng, fn, reads=(), writes=(), stream=None):
        return self._add(eng, fn, reads, writes, stream, True)

    def cc(self, eng, fn, reads=(), writes=(), stream=None, inc=1):
        op = self._add(eng, fn, reads, writes, stream, True)
        op.inc = inc
        return op

    def finish(self):
        nc = self.nc
        ops = self.ops
        for op in ops:
            for d in op.deps:
                ops[d].signal = True
        last_dma = {}
        for op in ops:
            if op.is_dma:
                op.signal = True
                last_dma[op.stream] = op.idx
        sems = {}
        cnt = {}
        for op in ops:
            if not op.signal:
                continue
            if op.is_dma:
                key = ("d", op.stream)
                cnt[key] = cnt.get(key, 0) + op.inc
            else:
                key = ("e", op.eng)
                cnt[key] = cnt.get(key, 0) + 1
            op.ev = (key, cnt[key])
            if key not in sems:
                sems[key] = nc.alloc_semaphore(self.tag + "s_" + "_".join(str(k) for k in key))
        known = {e: {} for e in self.ENGS}
        per_eng = {e: [] for e in self.ENGS}
        for op in ops:
            waits = {}
            kn = known[op.eng]
            for d in op.deps:
                key, val = ops[d].ev
                if kn.get(key, 0) >= val:
                    continue
                if waits.get(key, 0) < val:
                    waits[key] = val
            kn.update(waits)
            per_eng[op.eng].append((op, list(waits.items())))
        finals = [ops[i].ev for i in last_dma.values()]
        self.stats = dict(n_ops=len(ops), n_sems=len(sems),
                          per_eng={e: len(v) for e, v in per_eng.items()})
        with nc.Block() as block:
            def make(engname):
                def body(eng):
                    for op, waits in per_eng[engname]:
                        for key, val in waits:
                            eng.wait_ge(sems[key], val)
                        ins = op.fn(eng)
                        if op.ev is not None:
                            ins.then_inc(sems[op.ev[0]], op.inc)
                    if engname == "sync":
                        for key, val in finals:
                            eng.wait_ge(sems[key], val)
                return body
            block.sync(make("sync"))
            block.scalar(make("scalar"))
            block.vector(make("vector"))
            block.gpsimd(make("gpsimd"))
            block.tensor(make("tensor"))


class T:
    __slots__ = ("h", "b", "ps")

    def __init__(self, h, name, buf=None, ps=False):
        self.h = h
        self.b = buf if buf is not None else Buf(name)
        self.ps = ps

    def __getitem__(self, k):
        return self.h[k]


SB_BASE = 16640
SB_LIMIT = 229376


class Ctx:
    def __init__(self, nc, S):
        self.nc = nc
        self.S = S
        self._n = 0
        self.off = SB_BASE
        self.pairs = [nc.alloc_psum_tensor(f"pbank{i}", [128, 1024], F32) for i in range(4)]
        self.bank_bufs = [Buf(f"bank{i}") for i in range(8)]
        S.dummy = self.sb([128, 8], F32, "dummy")

    def sb(self, shape, dt, name=None):
        self._n += 1
        name = f"{name or 't'}_{self._n}"
        nbytes = int(np.prod(shape[1:])) * (4 if dt == F32 else 2)
        nbytes = (nbytes + 31) // 32 * 32
        off = self.off
        self.off += nbytes
        assert self.off <= SB_LIMIT, f"SBUF overflow at {name}: {self.off}"
        return T(self.nc.alloc_sbuf_tensor_at(name, list(shape), dt, offset=off), name)

    def bank(self, i, dt=F32, c0=0, c1=None, name=None):
        ap = self.pairs[i // 2][:, (i % 2) * 512:(i % 2 + 1) * 512]
        if dt != F32:
            ap = ap.bitcast(dt)
        if c1 is None:
            c1 = ap.shape[1]
        self._n += 1
        return T(ap[:, c0:c1], f"{name or 'ps'}_{self._n}", buf=self.bank_bufs[i], ps=True)

    def bank2(self, k, name=None):
        self._n += 1
        t = T(self.pairs[k][:].rearrange("p (h q) -> p h q", h=2), f"{name or 'ps2'}_{self._n}", buf=self.bank_bufs[2 * k], ps=True)
        t.b = (self.bank_bufs[2 * k], self.bank_bufs[2 * k + 1])
        return t

    @staticmethod
    def _bufs(t):
        return list(t.b) if isinstance(t.b, tuple) else [t.b]

    @classmethod
    def _rw(cls, r, w):
        reads = [b for t in r if not getattr(t, "ps", False) for b in cls._bufs(t)]
        writes = [b for t in w for b in cls._bufs(t)] + [b for t in r if getattr(t, "ps", False) for b in cls._bufs(t)]
        return reads, writes

    def V(self, fn, r=(), w=()):
        reads, writes = self._rw(r, w)
        self.S.op("vector", fn, reads, writes)

    def A(self, fn, r=(), w=()):
        reads, writes = self._rw(r, w)
        self.S.op("scalar", fn, reads, writes)

    def G(self, fn, r=(), w=()):
        reads, writes = self._rw(r, w)
        self.S.op("gpsimd", fn, reads, writes)

    def PE(self, fn, r=(), w=()):
        reads, writes = self._rw(r, w)
        self.S.op("tensor", fn, reads, writes)

    def DMA(self, q, fn, r=(), w=(), stream=None):
        self.S.dma(q, fn, [t.b for t in r], [t.b for t in w], stream=stream)


class DT:
    __slots__ = ("b",)

    def __init__(self, name):
        self.b = Buf(name)


def make_consts(cx):
    c = {}
    tmp = cx.sb([128, 512], F32, "ctmp")
    identf = cx.sb([128, 128], F32, "identf")
    identb = cx.sb([128, 128], BF16, "identb")
    cx.G(lambda e: e.memset(identf[:], 1.0), w=[identf])
    cx.G(lambda e: e.affine_select(out=identf[:], in_=identf[:], pattern=[[-1, 128]], compare_op=ALU.is_equal,
                                   fill=0.0, base=0, channel_multiplier=1), r=[identf], w=[identf])
    cx.V(lambda e: e.tensor_copy(out=identb[:], in_=identf[:]), r=[identf], w=[identb])
    c["identf"], c["identb"] = identf, identb
    maskA = cx.sb([128, 128], F32, "maskA")
    cx.G(lambda e: e.memset(maskA[:], 1.0), w=[maskA])
    cx.G(lambda e: e.affine_select(out=maskA[:], in_=maskA[:], pattern=[[1, 128]], compare_op=ALU.is_ge,
                                   fill=0.0, base=0, channel_multiplier=-1), r=[maskA], w=[maskA])
    c["maskA"] = maskA
    m0 = cx.sb([128, 512], BF16, "m0")
    cx.G(lambda e: e.memset(tmp[:], 1.0), w=[tmp])
    cx.G(lambda e: e.affine_select(out=tmp[:], in_=tmp[:], pattern=[[1, 512]], compare_op=ALU.is_gt,
                                   fill=0.0, base=0, channel_multiplier=-1), r=[tmp], w=[tmp])
    cx.V(lambda e: e.tensor_copy(out=m0[:], in_=tmp[:]), r=[tmp], w=[m0])
    c["m0"] = m0
    m0p = cx.sb([128, 2, 512], BF16, "m0p")
    for hh in range(2):
        cx.V(lambda e, hh=hh: e.tensor_copy(out=m0p[:, hh, :], in_=tmp[:]), r=[tmp], w=[m0p])
    c["m0p"] = m0p
    tneg = cx.sb([128, 128], BF16, "tneg")
    onesneg = cx.sb([128, 128], BF16, "onesneg")
    tmp2 = cx.sb([128, 128], F32, "ctmp2")
    cx.G(lambda e: e.memset(tmp2[:], -1.0), w=[tmp2])
    cx.V(lambda e: e.tensor_copy(out=onesneg[:], in_=tmp2[:]), r=[tmp2], w=[onesneg])
    cx.G(lambda e: e.affine_select(out=tmp2[:], in_=tmp2[:], pattern=[[-1, 128]], compare_op=ALU.is_ge,
                                   fill=0.0, base=0, channel_multiplier=1), r=[tmp2, onesneg], w=[tmp2])
    cx.V(lambda e: e.tensor_copy(out=tneg[:], in_=tmp2[:]), r=[tmp2], w=[tneg])
    c["tneg"], c["onesneg"] = tneg, onesneg
    sel = cx.sb([2, 2, 128], F32, "sel")
    cx.G(lambda e: e.memset(sel[:], 1.0), w=[sel])
    for h in range(2):
        cx.G(lambda e, h=h: e.affine_select(out=sel[:, h, :], in_=sel[:, h, :], pattern=[[0, 128]],
                                            compare_op=ALU.is_equal, fill=0.0, base=-h, channel_multiplier=1),
             r=[sel], w=[sel])
    c["sel"] = sel
    ones2 = cx.sb([2, 512], F32, "ones2")
    cx.G(lambda e: e.memset(ones2[:], 1.0), w=[ones2])
    c["ones2"] = ones2
    return c


def load_weight_bf16(cx, W, wdram, ncols, stg, tagname):
    nk = wdram.shape[0] // 128
    for kc in range(nk):
        cx.DMA("gpsimd", lambda e, kc=kc: e.dma_start(out=W[:, kc, 0:ncols], in_=wdram[kc * 128:(kc + 1) * 128, 0:ncols]),
               w=[W], stream=f"{tagname}w{kc % 4}")


class ARHook:
    GROUPS = [[0, 1], [2, 3], [4, 5], [6, 7]]

    def __init__(self, cx, ypart_ap, ysum_ap, SL):
        self.cx, self.yp, self.ys = cx, ypart_ap, ysum_ap
        self.row_dt = [DT(f"yprow{i}") for i in range(SL // 128)]
        self.sum_dt = [DT(f"ysum{i}") for i in range(SL // 1024)]
        self.sum_half = [DT(f"ysumh{i}") for i in range(SL // TT)]

    def store_dt(self, r0):
        return self.row_dt[r0 // 128]

    def after_tile(self, Tn):
        r0 = Tn * TT
        yp, ys = self.yp, self.ys
        self.cx.S.cc("gpsimd", lambda e: e.collective_compute("AllReduce", ALU.add, replica_groups=self.GROUPS,
                                                              ins=[yp[r0:r0 + TT]], outs=[ys[r0:r0 + TT]]),
                     reads=[d.b for d in self.row_dt[Tn * 4:(Tn + 1) * 4]], writes=[self.sum_half[Tn].b], stream="cc", inc=1)
        if Tn % 2 == 1:
            self.sum_dt[Tn // 2].b.writers = list(self.sum_half[Tn - 1].b.writers) + list(self.sum_half[Tn].b.writers)


class NormStage:
    def __init__(self, cx, consts, x_ap, parts, gpost_ap, gpre_ap, xcur_ap, name="n", pbank=6, parts_dt=None, nhb=1,
                 nyt=1):
        self.cx, self.c = cx, consts
        self.parts_dt = parts_dt
        self.x_ap, self.parts, self.xcur_ap = x_ap, parts, xcur_ap
        self.xt = [cx.sb([128, D], F32, "xt") for _ in range(2)]
        self.nyt = nyt
        self.yt = [[cx.sb([128, D], F32, "yt") for _ in range(len(parts))] for _ in range(nyt)]
        self.junk = cx.sb([128, D], BF16, "junk")
        self.hb = [cx.sb([128, D], BF16, "hb") for _ in range(nhb)]
        self.st = [cx.sb([128, 4], F32, "nst") for _ in range(2)]
        self.pT = [cx.bank(pbank, BF16, 0, 512, "pT"), cx.bank(pbank + 1, BF16, 0, 512, "pT")]
        self.gpre = cx.sb([128, D], F32, "gpre")
        cx.DMA("sync", lambda e: e.dma_start(out=self.gpre[:], in_=gpre_ap.partition_broadcast(128)),
               w=[self.gpre], stream=name + "gpre")
        if parts:
            self.gpost = cx.sb([128, D], F32, "gpost")
            cx.DMA("sync", lambda e: e.dma_start(out=self.gpost[:], in_=gpost_ap.partition_broadcast(128)),
                   w=[self.gpost], stream=name + "gpost")
        self.n = 0
        self.name = name

    def load(self, r0):
        cx = self.cx
        i = self.n
        xt = self.xt[i % 2]
        cx.DMA("sync", lambda e: e.dma_start(out=xt[:], in_=self.x_ap[r0:r0 + 128, :]), w=[xt],
               stream=f"{self.name}x{i % 2}")
        return xt

    def run(self, r0, hT, j):
        hb = self.pre(r0, want_h=hT is not None)
        if hT is not None:
            self.post(hb, hT, j)

    def pre(self, r0, want_h=True):
        cx, c = self.cx, self.c
        i = self.n
        self.n += 1
        xt = self.load(r0)
        st = self.st[i % 2]
        hb = self.hb[i % len(self.hb)]
        junk = self.junk
        if self.parts:
            yts = self.yt[i % self.nyt]
            for k, p_ap in enumerate(self.parts):
                yt = yts[k]
                rdt = [self.parts_dt[k][r0 // 1024]] if self.parts_dt is not None else []
                cx.DMA("sync", lambda e, yt=yt, p_ap=p_ap: e.dma_start(out=yt[:], in_=p_ap[r0:r0 + 128, :]),
                       r=rdt, w=[yt], stream=f"{self.name}y{k}_{i % self.nyt}")
            y0 = yts[0]
            if len(self.parts) == 2:
                y1 = yts[1]
                cx.G(lambda e: e.tensor_tensor(out=y0[:], in0=y0[:], in1=y1[:], op=ALU.add), r=[y0, y1], w=[y0])
            cx.A(lambda e: e.activation(out=junk[:], in_=y0[:], func=AF.Square, accum_out=st[:, 0:1]),
                 r=[y0], w=[junk, st])
            cx.A(lambda e: e.activation(out=st[:, 1:2], in_=st[:, 0:1], func=AF.Sqrt, scale=1.0 / D, bias=EPS),
                 r=[st], w=[st])
            cx.V(lambda e: e.reciprocal(out=st[:, 1:2], in_=st[:, 1:2]), r=[st], w=[st])
            cx.V(lambda e: e.scalar_tensor_tensor(out=y0[:], in0=y0[:], scalar=st[:, 1:2], in1=self.gpost[:],
                                                  op0=ALU.mult, op1=ALU.mult), r=[y0, st, self.gpost], w=[y0])
            cx.G(lambda e: e.tensor_tensor(out=xt[:], in0=xt[:], in1=y0[:], op=ALU.add), r=[xt, y0], w=[xt])
            if self.xcur_ap is not None:
                cx.DMA("gpsimd", lambda e: e.dma_start(out=self.xcur_ap[r0:r0 + 128, :], in_=xt[:]), r=[xt],
                       stream=f"{self.name}xo{i % 2}")
        if not want_h:
            return None
        cx.A(lambda e: e.activation(out=junk[:], in_=xt[:], func=AF.Square, accum_out=st[:, 2:3]),
             r=[xt], w=[junk, st])
        cx.A(lambda e: e.activation(out=st[:, 3:4], in_=st[:, 2:3], func=AF.Sqrt, scale=1.0 / D, bias=EPS),
             r=[st], w=[st])
        cx.V(lambda e: e.reciprocal(out=st[:, 3:4], in_=st[:, 3:4]), r=[st], w=[st])
        cx.V(lambda e: e.scalar_tensor_tensor(out=hb[:], in0=xt[:], scalar=st[:, 3:4], in1=self.gpre[:],
                                              op0=ALU.mult, op1=ALU.mult), r=[xt, st, self.gpre], w=[hb])
        return hb

    def post(self, hb, hT, j):
        cx, c = self.cx, self.c
        for g in range(2):
            pT = self.pT[g]
            for k in range(4):
                cc = g * 4 + k
                cx.PE(lambda e, pT=pT, k=k, cc=cc: e.transpose(out=pT[:, k * 128:(k + 1) * 128],
                                                              in_=hb[:, cc * 128:(cc + 1) * 128],
                                                              identity=c["identb"][:]),
                      r=[hb, c["identb"]], w=[pT])
            src = pT[:].rearrange("p (c t) -> p c t", c=4)
            if g == 0:
                cx.V(lambda e, src=src, g=g: e.tensor_copy(out=hT[:, g * 4:(g + 1) * 4, j * 128:(j + 1) * 128], in_=src),
                     r=[pT], w=[hT])
            else:
                cx.A(lambda e, src=src, g=g: e.activation(out=hT[:, g * 4:(g + 1) * 4, j * 128:(j + 1) * 128], in_=src,
                                                          func=AF.Copy), r=[pT], w=[hT])


NCOL_E = 4612
C_AQ, C_AK, C_BQ, C_BK, C_BZ = 0, 512, 1024, 1536, 2048
C_TM = 2560
C_GI, C_GF = 4608, 4610


def even_pass1(cx, c, SL, x_ap, parts, gpost_ap, gpre_ap, xcur_ap, win_ap, convw_ap, bi_ap, nbf_ap, hng_ap, scr,
               scr_dt, parts_dt=None):
    S = cx.S
    NT = SL // TT
    W = cx.sb([128, 8, NCOL_E], BF16, "W")
    stg = [cx.sb([128, 1024], F32, "wstg") for _ in range(1)]
    load_weight_bf16(cx, W, win_ap, NCOL_E, stg, "w")
    convw = cx.sb([128, 8, 4], F32, "convw")
    cx.DMA("sync", lambda e: e.dma_start(out=convw[:], in_=convw_ap), w=[convw], stream="convw")
    bi = cx.sb([2, 1], F32, "bi")
    nbf = cx.sb([2, 1], F32, "nbf")
    cx.DMA("sync", lambda e: e.dma_start(out=bi[:], in_=bi_ap), w=[bi], stream="bi")
    cx.DMA("sync", lambda e: e.dma_start(out=nbf[:], in_=nbf_ap), w=[nbf], stream="nbf")
    hng = cx.sb([128, 512], F32, "hng")
    cx.DMA("sync", lambda e: e.dma_start(out=hng[:], in_=hng_ap.partition_broadcast(128)), w=[hng], stream="hng")
    cx.V(lambda e: e.tensor_scalar_mul(out=nbf[:], in0=nbf[:], scalar1=-1.0), r=[nbf], w=[nbf])

    if DBG.get("stop") == "weights":
        return
    norm = NormStage(cx, c, x_ap, parts, gpost_ap, gpre_ap, xcur_ap, "n1", parts_dt=parts_dt, nhb=4)
    hT2 = [cx.sb([128, 8, TT], BF16, "hT") for _ in range(2)]
    cv = [cx.sb([128, 3 + TT], F32, "cv") for _ in range(2)]
    halo = cx.sb([128, 8, 3], F32, "halo")
    cx.G(lambda e: e.memset(halo[:], 0.0), w=[halo])
    acc = [cx.sb([128, TT], F32, "cacc") for _ in range(2)]
    qkT = cx.sb([128, 8, TT], BF16, "qkT")
    bst = [cx.sb([128, 4, TT], BF16, "bst") for _ in range(3)]
    bvst = cx.sb([128, 4, 512], BF16, "bvst")
    vextj = [cx.sb([128, 2, 257], BF16, "vext") for _ in range(4)]
    for t in vextj:
        cx.G(lambda e, t=t: e.memset(t[:], 1.0), w=[t])
    sgoj = [cx.sb([128, 512], BF16, "sgo") for _ in range(4)]
    gzgj = [cx.sb([128, 512], BF16, "gzg") for _ in range(4)]
    C32 = [[cx.sb([128, 257], F32, "C32") for _ in range(2)] for _ in range(2)]
    Cb = [[cx.sb([128, 257], BF16, "Cb") for _ in range(2)] for _ in range(2)]
    for h in range(2):
        for d in range(2):
            cx.G(lambda e, t=C32[h][d]: e.memset(t[:], 0.0), w=[C32[h][d]])
            cx.G(lambda e, t=Cb[h][d]: e.memset(t[:], 0.0), w=[Cb[h][d]])
    Bcarry = cx.sb([2, 1], F32, "Bcarry")
    mucarry = cx.sb([2, 1], F32, "mucarry")
    cx.G(lambda e: e.memset(Bcarry[:], 0.0), w=[Bcarry])
    cx.G(lambda e: e.memset(mucarry[:], 0.0), w=[mucarry])
    g_e1 = cx.sb([2, TT], F32, "g_e1")
    g_B = cx.sb([2, TT], F32, "g_B")
    g_ag = cx.sb([2, TT], F32, "g_ag")
    g_w = cx.sb([2, TT], F32, "g_w")
    g_c = cx.sb([2, TT], F32, "g_c")
    g_s = cx.sb([2, 32], F32, "g_s")
    wc = cx.sb([128, 16], F32, "wc")
    decb = cx.sb([128, 8], F32, "decb")
    scT = [cx.sb([128, 128], BF16, "scT") for _ in range(2)]
    kp = [cx.sb([128, 256], BF16, "kp") for _ in range(2)]
    hg = [cx.sb([128, 256], F32, "hg") for _ in range(2)]
    ya = [cx.sb([128, 256], BF16, "ya") for _ in range(2)]
    cst = [cx.sb([128, 4], F32, "cst") for _ in range(2)]
    junk2 = cx.sb([128, 256], BF16, "junk2")
    yaTs = cx.sb([128, 4, TT], BF16, "yaTs")
    pj = [cx.bank(0, name="pj"), cx.bank(1, name="pj"), cx.bank(2, name="pj")]
    p_num = cx.bank(3, name="pnum")
    p_misc = cx.bank(4, name="pmisc")
    p_mbk = cx.bank(5, BF16, 0, 256, "pmbk")
    p_mby = cx.bank(6, BF16, 0, 256, "pmby")
    pjn = [0]

    def pslot():
        p = pj[pjn[0] % 3]
        pjn[0] += 1
        return p

    for j in range(4):
        norm.run(j * 128, hT2[0], j)
    def _tile(Tn):
        t0 = Tn * TT
        hT = hT2[Tn % 2]
        hbs = [norm.pre(t0 + TT + j * 128) for j in range(4)] if Tn + 1 < NT else None
        if DBG.get("stop") == "fm":
            return
        pgi = pslot()
        for kc in range(8):
            cx.PE(lambda e, kc=kc: e.matmul(pgi[0:2, :], lhsT=W[:, kc, C_GI:C_GI + 2], rhs=hT[:, kc, :],
                                            start=(kc == 0), stop=(kc == 7)), r=[W, hT], w=[pgi])
        pgf = pslot()
        for kc in range(8):
            cx.PE(lambda e, kc=kc: e.matmul(pgf[0:2, :], lhsT=W[:, kc, C_GF:C_GF + 2], rhs=hT[:, kc, :],
                                            start=(kc == 0), stop=(kc == 7)), r=[W, hT], w=[pgf])
        cx.A(lambda e: e.activation(out=g_e1[:], in_=pgf[0:2, :], func=AF.Exp, scale=-1.0, bias=nbf[:, 0:1]),
             r=[pgf, nbf], w=[g_e1])
        cx.A(lambda e: e.activation(out=g_e1[:], in_=g_e1[:], func=AF.Ln, bias=1.0), r=[g_e1], w=[g_e1])
        cx.V(lambda e: e.tensor_tensor_scan(out=g_B[:], data0=c["ones2"][:], data1=g_e1[:], initial=Bcarry[:, 0:1],
                                            op0=ALU.mult, op1=ALU.subtract), r=[c["ones2"], g_e1, Bcarry], w=[g_B])
        cx.V(lambda e: e.tensor_copy(out=Bcarry[:], in_=g_B[:, TT - 1:TT]), r=[g_B], w=[Bcarry])
        cx.V(lambda e: e.scalar_tensor_tensor(out=g_ag[:], in0=pgi[0:2, :], scalar=bi[:, 0:1], in1=g_B[:],
                                              op0=ALU.add, op1=ALU.subtract), r=[pgi, bi, g_B], w=[g_ag])
        cx.V(lambda e: e.tensor_reduce(out=g_s[:, 0:4], in_=g_ag[:].rearrange("p (c t) -> p c t", c=4),
                                       axis=AX.X, op=ALU.max), r=[g_ag], w=[g_s])
        cx.V(lambda e: e.tensor_tensor_scan(out=g_s[:, 4:8], data0=g_s[:, 0:4], data1=g_s[:, 0:4],
                                            initial=mucarry[:, 0:1], op0=ALU.max, op1=ALU.max),
             r=[g_s, mucarry], w=[g_s])
        cx.V(lambda e: e.tensor_copy(out=g_s[:, 8:9], in_=mucarry[:]), r=[mucarry, g_s], w=[g_s])
        cx.V(lambda e: e.tensor_copy(out=g_s[:, 9:12], in_=g_s[:, 4:7]), r=[g_s], w=[g_s])
        cx.V(lambda e: e.tensor_copy(out=mucarry[:], in_=g_s[:, 7:8]), r=[g_s], w=[mucarry])
        cx.V(lambda e: e.tensor_tensor(out=g_s[:, 12:16], in0=g_s[:, 8:12], in1=g_s[:, 4:8], op=ALU.subtract),
             r=[g_s], w=[g_s])
        cx.A(lambda e: e.activation(out=g_s[:, 16:20], in_=g_s[:, 12:16], func=AF.Exp), r=[g_s], w=[g_s])
        cx.V(lambda e: e.tensor_scalar(out=g_s[:, 20:24], in0=g_s[:, 4:8], scalar1=-1.0, scalar2=-LN16,
                                       op0=ALU.mult, op1=ALU.add), r=[g_s], w=[g_s])
        cx.V(lambda e: e.tensor_scalar_mul(out=g_s[:, 24:28], in0=g_s[:, 4:8], scalar1=-1.0), r=[g_s], w=[g_s])
        for j in range(4):
            cx.A(lambda e, j=j: e.activation(out=g_w[:, j * 128:(j + 1) * 128], in_=g_ag[:, j * 128:(j + 1) * 128],
                                             func=AF.Exp, bias=g_s[:, 20 + j:21 + j]), r=[g_ag, g_s], w=[g_w])
            cx.A(lambda e, j=j: e.activation(out=g_c[:, j * 128:(j + 1) * 128], in_=g_B[:, j * 128:(j + 1) * 128],
                                             func=AF.Exp, scale=-1.0, bias=g_s[:, 24 + j:25 + j]),
                 r=[g_B, g_s], w=[g_c])
        for j in range(4):
            cx.PE(lambda e, j=j: e.transpose(out=p_misc[:, 128 + 4 * j:128 + 4 * j + 2],
                                             in_=g_w[:, j * 128:(j + 1) * 128], identity=c["identf"][0:2, 0:2]),
                  r=[g_w, c["identf"]], w=[p_misc])
            cx.PE(lambda e, j=j: e.transpose(out=p_misc[:, 128 + 4 * j + 2:128 + 4 * j + 4],
                                             in_=g_c[:, j * 128:(j + 1) * 128], identity=c["identf"][0:2, 0:2]),
                  r=[g_c, c["identf"]], w=[p_misc])
        for h in range(2):
            cx.PE(lambda e, h=h: e.matmul(p_misc[:, 144 + 4 * h:148 + 4 * h], lhsT=c["sel"][:, h, :],
                                          rhs=g_s[:, 16:20], start=True, stop=True), r=[c["sel"], g_s], w=[p_misc])
        cx.V(lambda e: e.tensor_copy(out=wc[:], in_=p_misc[:, 128:144]), r=[p_misc], w=[wc])
        cx.V(lambda e: e.tensor_copy(out=decb[:], in_=p_misc[:, 144:152]), r=[p_misc], w=[decb])
        for cc in range(20):
            ps = pslot()
            for kc in range(8):
                cx.PE(lambda e, ps=ps, kc=kc, cc=cc: e.matmul(ps[:], lhsT=W[:, kc, cc * 128:(cc + 1) * 128],
                                                              rhs=hT[:, kc, :], start=(kc == 0), stop=(kc == 7)),
                      r=[W, hT], w=[ps])
            if cc < 8:
                cvt = cv[cc % 2]
                ac = acc[cc % 2]
                cx.A(lambda e, ps=ps, cvt=cvt: e.activation(out=cvt[:, 3:3 + TT], in_=ps[:], func=AF.Copy),
                     r=[ps], w=[cvt])
                cx.G(lambda e, cvt=cvt, cc=cc: e.tensor_copy(out=cvt[:, 0:3], in_=halo[:, cc, :]), r=[halo], w=[cvt])
                cx.V(lambda e, cvt=cvt, ac=ac, cc=cc: e.tensor_scalar_mul(out=ac[:], in0=cvt[:, 3:3 + TT],
                                                                         scalar1=convw[:, cc, 3:4]),
                     r=[cvt, convw], w=[ac])
                for tap in range(3):
                    cx.V(lambda e, cvt=cvt, ac=ac, cc=cc, tap=tap: e.scalar_tensor_tensor(
                        out=ac[:], in0=cvt[:, tap:tap + TT], scalar=convw[:, cc, tap:tap + 1], in1=ac[:],
                        op0=ALU.mult, op1=ALU.add), r=[cvt, convw, ac], w=[ac])
                cx.A(lambda e, ac=ac, cc=cc: e.activation(out=qkT[:, cc, :], in_=ac[:], func=AF.Silu),
                     r=[ac], w=[qkT])
                cx.G(lambda e, cvt=cvt, cc=cc: e.tensor_copy(out=halo[:, cc, :], in_=cvt[:, TT:TT + 3]), r=[cvt], w=[halo])
            else:
                g = (cc - 8) // 4
                hh = (cc - 8) % 4
                st = bst[g]
                if g == 0:
                    cx.A(lambda e, ps=ps, st=st, hh=hh: e.activation(out=st[:, hh, :], in_=ps[:], func=AF.Copy,
                                                                     scale=128.0 ** -0.5), r=[ps], w=[st])
                elif g == 1:
                    cx.V(lambda e, ps=ps, st=st, hh=hh: e.tensor_copy(out=st[:, hh, :], in_=ps[:]), r=[ps], w=[st])
                else:
                    cx.A(lambda e, ps=ps, st=st, hh=hh: e.activation(out=st[:, hh, :], in_=ps[:], func=AF.Silu),
                         r=[ps], w=[st])
                if hh == 3:
                    key = ("q", "k", "gz")[g]
                    dst = scr[key + "T"][:, :, t0:t0 + TT].rearrange("c p t -> p c t")
                    cx.S.dma("gpsimd", lambda e, dst=dst, st=st: e.dma_start(out=dst, in_=st[:]),
                             reads=[st.b], writes=[scr_dt[key][Tn].b], stream=f"bst{g}")
        if DBG.get("stop") == "gates":
            return
        def tm(j):
            for g in range(4):
                ps = pslot()
                for kc in range(8):
                    cx.PE(lambda e, ps=ps, kc=kc, g=g: e.matmul(
                        ps[:], lhsT=hT[:, kc, j * 128:(j + 1) * 128],
                        rhs=W[:, kc, C_TM + g * 512:C_TM + (g + 1) * 512], start=(kc == 0), stop=(kc == 7)),
                        r=[W, hT], w=[ps])
                if g == 0:
                    for hh in range(2):
                        cx.V(lambda e, ps=ps, hh=hh: e.tensor_copy(out=vextj[j][:, hh, 0:256],
                                                                   in_=ps[:, hh * 256:(hh + 1) * 256]),
                             r=[ps], w=[vextj[j]])
                elif g == 1:
                    cx.A(lambda e, ps=ps: e.activation(out=sgoj[j][:], in_=ps[:], func=AF.Sigmoid),
                         r=[ps], w=[sgoj[j]])
                elif g == 2:
                    cx.A(lambda e, ps=ps: e.activation(out=gzgj[j][:], in_=ps[:], func=AF.Silu),
                         r=[ps], w=[gzgj[j]])
                    cx.G(lambda e: e.tensor_tensor(out=gzgj[j][:], in0=gzgj[j][:], in1=hng[:], op=ALU.mult),
                         r=[gzgj[j], hng], w=[gzgj[j]])
                else:
                    cx.V(lambda e, ps=ps: e.tensor_copy(out=bvst[:, j, :], in_=ps[:]), r=[ps], w=[bvst])

        def unit(j, h):
            sl = slice(j * 128, (j + 1) * 128)
            n = (Tn * 4 + j) * 2 + h
            sc, kpt, hgt, yat, cs = scT[n % 2], kp[n % 2], hg[n % 2], ya[n % 2], cst[n % 2]
            vx, sg, gg = vextj[j], sgoj[j], gzgj[j]
            wcol = wc[:, 4 * j + h:4 * j + h + 1]
            ccol = wc[:, 4 * j + 2 + h:4 * j + 3 + h]
            dcol = decb[:, 4 * h + j:4 * h + j + 1]
            for d in range(2):
                cx.PE(lambda e, d=d: e.matmul(p_misc[:, 0:128], lhsT=qkT[:, 4 + 2 * h + d, sl],
                                              rhs=qkT[:, 2 * h + d, sl], start=(d == 0), stop=(d == 1)),
                      r=[qkT], w=[p_misc])
            cx.V(lambda e: e.scalar_tensor_tensor(out=sc[:], in0=p_misc[:, 0:128], scalar=wcol,
                                                  in1=c["maskA"][:], op0=ALU.mult, op1=ALU.mult),
                 r=[p_misc, wc, c["maskA"]], w=[sc])
            for d in range(2):
                cx.PE(lambda e, d=d: e.transpose(out=p_mbk[:, d * 128:(d + 1) * 128],
                                                 in_=qkT[:, 4 + 2 * h + d, sl], identity=c["identb"][:]),
                      r=[qkT, c["identb"]], w=[p_mbk])
            cx.A(lambda e: e.activation(out=kpt[:], in_=p_mbk[:, 0:256], func=AF.Copy, scale=wcol),
                 r=[p_mbk, wc], w=[kpt])
            for d in range(2):
                cx.A(lambda e, d=d: e.activation(out=Cb[h][d][:], in_=C32[h][d][:], func=AF.Copy, scale=dcol),
                     r=[C32[h][d], decb], w=[Cb[h][d]])
                cx.V(lambda e, d=d: e.tensor_scalar_mul(out=C32[h][d][:], in0=C32[h][d][:], scalar1=dcol),
                     r=[C32[h][d], decb], w=[C32[h][d]])
            yield
            cx.PE(lambda e: e.matmul(p_num[:, 0:257], lhsT=sc[:], rhs=vx[:, h, :], start=True, stop=False),
                  r=[sc, vx], w=[p_num])
            for d in range(2):
                cx.PE(lambda e, d=d: e.matmul(p_num[:, 0:257], lhsT=qkT[:, 2 * h + d, sl], rhs=Cb[h][d][:],
                                              start=False, stop=(d == 1)), r=[qkT, Cb[h][d]], w=[p_num])
            for d in range(2):
                pdc = pslot()
                cx.PE(lambda e, d=d, pdc=pdc: e.matmul(pdc[:, 0:257], lhsT=kpt[:, d * 128:(d + 1) * 128],
                                                       rhs=vx[:, h, :], start=True, stop=True),
                      r=[kpt, vx], w=[pdc])
                cx.V(lambda e, d=d, pdc=pdc: e.tensor_tensor(out=C32[h][d][:], in0=C32[h][d][:], in1=pdc[:, 0:257],
                                                             op=ALU.add), r=[C32[h][d], pdc], w=[C32[h][d]])
            cx.V(lambda e: e.tensor_copy(out=cs[:, 0:1], in_=p_num[:, 256:257]), r=[p_num], w=[cs])
            cx.V(lambda e: e.scalar_tensor_tensor(out=cs[:, 1:2], in0=cs[:, 0:1], scalar=-1.0, in1=cs[:, 0:1],
                                                  op0=ALU.mult, op1=ALU.max), r=[cs], w=[cs])
            cx.V(lambda e: e.tensor_tensor(out=cs[:, 0:1], in0=cs[:, 1:2], in1=ccol, op=ALU.max),
                 r=[cs, wc], w=[cs])
            cx.V(lambda e: e.reciprocal(out=cs[:, 1:2], in_=cs[:, 0:1]), r=[cs], w=[cs])
            cx.V(lambda e: e.scalar_tensor_tensor(out=hgt[:], in0=p_num[:, 0:256], scalar=cs[:, 1:2],
                                                  in1=sg[:, h * 256:(h + 1) * 256], op0=ALU.mult, op1=ALU.mult),
                 r=[p_num, cs, sg], w=[hgt])
            cx.A(lambda e: e.activation(out=junk2[:], in_=hgt[:], func=AF.Square, accum_out=cs[:, 2:3]),
                 r=[hgt], w=[junk2, cs])
            cx.A(lambda e: e.activation(out=cs[:, 3:4], in_=cs[:, 2:3], func=AF.Sqrt, scale=1.0 / 256, bias=EPS),
                 r=[cs], w=[cs])
            cx.V(lambda e: e.reciprocal(out=cs[:, 3:4], in_=cs[:, 3:4]), r=[cs], w=[cs])
            cx.V(lambda e: e.scalar_tensor_tensor(out=yat[:], in0=hgt[:], scalar=cs[:, 3:4],
                                                  in1=gg[:, h * 256:(h + 1) * 256], op0=ALU.mult, op1=ALU.mult),
                 r=[hgt, cs, gg], w=[yat])
            yield
            for d in range(2):
                cx.PE(lambda e, d=d: e.transpose(out=p_mby[:, d * 128:(d + 1) * 128],
                                                 in_=yat[:, d * 128:(d + 1) * 128], identity=c["identb"][:]),
                      r=[yat, c["identb"]], w=[p_mby])
            for d in range(2):
                cx.A(lambda e, d=d: e.activation(out=yaTs[:, 2 * h + d, sl], in_=p_mby[:, d * 128:(d + 1) * 128],
                                                 func=AF.Copy), r=[p_mby], w=[yaTs])
            yield

        units = [unit(j, h) for j in range(4) for h in range(2)]
        for step in range(8 + 2):
            for k in (0, 1, 2):
                u = step - k
                if 0 <= u < 8:
                    if k == 0 and u % 2 == 0:
                        tm(u // 2)
                    next(units[u])
        dstv = scr["vB"][t0:t0 + TT, :].rearrange("(j p) c -> p j c", p=128)
        cx.S.dma("gpsimd", lambda e, dstv=dstv: e.dma_start(out=dstv, in_=bvst[:]), reads=[bvst.b],
                 writes=[scr_dt["v"][Tn].b], stream="bvst")
        dsty = scr["yaT"][:, :, t0:t0 + TT].rearrange("c p t -> p c t")
        cx.S.dma("gpsimd", lambda e, dsty=dsty: e.dma_start(out=dsty, in_=yaTs[:]), reads=[yaTs.b],
                 writes=[scr_dt["ya"][Tn].b], stream="yaTs")
        if hbs is not None:
            for j in range(4):
                norm.post(hbs[j], hT2[(Tn + 1) % 2], j)

    for Tn in range(NT):
        _tile(Tn)


def even_pass2(cx, c, SL, wout_ap, scr, scr_dt, ypart_ap, pipe=True, ar=None):
    S = cx.S
    NT = SL // TT
    NB = SL // 128
    Wo = cx.sb([128, 8, D], BF16, "Wo")
    load_weight_bf16(cx, Wo, wout_ap, D, None, "wo")
    Kc = cx.sb([128, 4, SL], BF16, "Kc")
    Vc = cx.sb([128, NB, 512], BF16, "Vc")
    qt = [cx.sb([128, 4, TT], BF16, "qt") for _ in range(2)]
    gzt = cx.sb([128, 4, TT], BF16, "gzt")
    yat = cx.sb([128, 4, TT], BF16, "yat")
    ybT = cx.sb([128, 4, TT], BF16, "ybT")
    E = [cx.sb([128, 2, TT], BF16, "E") for _ in range(3)]
    sp = [cx.sb([128, 2, TT], BF16, "sp") for _ in range(3)]
    Xr = [cx.sb([128, 2, TT], BF16, "X") for _ in range(2)]
    At = [cx.sb([128, 2, TT], BF16, "At") for _ in range(2)]
    Rb = cx.sb([128, 2, TT], BF16, "Rb")
    yo = [cx.sb([128, D], F32, "yo") for _ in range(2)]
    zz = cx.bank2(0, "zz")
    zh = [cx.bank(0, name="zh"), cx.bank(1, name="zh")]
    cc2 = [cx.bank2(1, "cc"), cx.bank2(2, "cc")]
    ch = [[cx.bank(2, name="ch"), cx.bank(3, name="ch")], [cx.bank(4, name="ch"), cx.bank(5, name="ch")]]
    oph = [cx.bank(6, name="ops"), cx.bank(7, name="ops")]
    pps = zh
    m0p = c["m0p"]
    loaded = set()

    def ensure_loaded(Tn):
        if Tn in loaded:
            return
        loaded.add(Tn)
        t0 = Tn * TT
        cx.DMA("sync", lambda e: e.dma_start(out=Kc[:, :, t0:t0 + TT],
                                             in_=scr["kT"][:, :, t0:t0 + TT].rearrange("c p t -> p c t")),
               r=[scr_dt["k"][Tn]], w=[Kc], stream="kc")
        cx.DMA("sync", lambda e: e.dma_start(out=Vc[:, Tn * 4:Tn * 4 + 4, :],
                                             in_=scr["vB"][t0:t0 + TT, :].rearrange("(j p) c -> p j c", p=128)),
               r=[scr_dt["v"][Tn]], w=[Vc], stream="vc")
        q = qt[Tn % 2]
        cx.DMA("sync", lambda e: e.dma_start(out=q[:], in_=scr["qT"][:, :, t0:t0 + TT].rearrange("c p t -> p c t")),
               r=[scr_dt["q"][Tn]], w=[q], stream=f"qt{Tn % 2}")

    def load_late(Tn):
        t0 = Tn * TT
        cx.DMA("sync", lambda e: e.dma_start(out=gzt[:], in_=scr["gzT"][:, :, t0:t0 + TT].rearrange("c p t -> p c t")),
               r=[scr_dt["gz"][Tn]], w=[gzt], stream="gzt")
        cx.DMA("sync", lambda e: e.dma_start(out=yat[:], in_=scr["yaT"][:, :, t0:t0 + TT].rearrange("c p t -> p c t")),
               r=[scr_dt["ya"][Tn]], w=[yat], stream="yat")

    def s_z(blk):
        n, Tn, hp, kb, first, last, q0 = blk
        ensure_loaded(Tn)
        q = qt[Tn % 2]
        fr = slice(q0, TT)
        for hh in range(2):
            h = 2 * hp + hh
            cx.PE(lambda e, hh=hh, h=h: e.matmul(zh[hh][:, fr], lhsT=Kc[:, h, kb * 128:(kb + 1) * 128], rhs=q[:, h, fr],
                                                 start=True, stop=True), r=[Kc, q], w=[zh[hh]])

    def s_E(blk):
        n, Tn, hp, kb, first, last, q0 = blk
        Et, spt = E[n % 3], sp[n % 3]
        fr = slice(q0, TT)
        cx.A(lambda e: e.activation(out=Et[:, :, fr], in_=zz[:, :, fr], func=AF.Exp), r=[zz], w=[Et])
        cx.A(lambda e: e.activation(out=spt[:, :, fr], in_=Et[:, :, fr], func=AF.Ln, bias=1.0), r=[Et], w=[spt])
        if kb >= Tn * 4:
            cx.G(lambda e: e.tensor_tensor(out=spt[:, :, fr], in0=spt[:, :, fr], in1=m0p[:, :, 0:TT - q0], op=ALU.mult),
                 r=[spt, m0p], w=[spt])

    def s_cum(blk):
        n, Tn, hp, kb, first, last, q0 = blk
        spt, xt_ = sp[n % 3], Xr[n % 2]
        cpair, chh = cc2[n % 2], ch[n % 2]
        fr = slice(q0, TT)
        for hh in range(2):
            cx.PE(lambda e, hh=hh: e.matmul(chh[hh][:, fr], lhsT=c["tneg"][:], rhs=spt[:, hh, fr], start=True, stop=first),
                  r=[c["tneg"], spt], w=[chh[hh]])
            if not first:
                cx.PE(lambda e, hh=hh: e.matmul(chh[hh][:, fr], lhsT=c["onesneg"][:], rhs=Rb[:, hh, fr], start=False, stop=True),
                      r=[c["onesneg"], Rb], w=[chh[hh]])
        cx.A(lambda e: e.activation(out=xt_[:, :, fr], in_=cpair[:, :, fr], func=AF.Exp), r=[cpair], w=[xt_])
        if first:
            cx.G(lambda e: e.memset(Rb[:], 0.0), w=[Rb])
        if not last:
            cx.G(lambda e: e.tensor_tensor(out=Rb[:, :, fr], in0=Rb[:, :, fr], in1=spt[:, :, fr], op=ALU.add),
                 r=[Rb, spt], w=[Rb])

    def s_fin(blk):
        n, Tn, hp, kb, first, last, q0 = blk
        Et, xt_, at = E[n % 3], Xr[n % 2], At[n % 2]
        fr = slice(q0, TT)
        cx.V(lambda e: e.tensor_tensor(out=at[:, :, fr], in0=Et[:, :, fr], in1=xt_[:, :, fr], op=ALU.mult),
             r=[Et, xt_], w=[at])
        if kb >= Tn * 4:
            cx.V(lambda e: e.tensor_tensor(out=at[:, :, fr], in0=at[:, :, fr], in1=m0p[:, :, 0:TT - q0], op=ALU.mult),
                 r=[at, m0p], w=[at])
        for hh in range(2):
            h = 2 * hp + hh
            cx.PE(lambda e, hh=hh, h=h: e.matmul(oph[hh][:, fr], lhsT=Vc[:, kb, h * 128:(h + 1) * 128], rhs=at[:, hh, fr],
                                                 start=first, stop=last, skip_group_check=True), r=[Vc, at], w=[oph[hh]])
        if last:
            for hh in range(2):
                h = 2 * hp + hh
                cx.V(lambda e, hh=hh, h=h: e.tensor_tensor(out=ybT[:, h, :], in0=oph[hh][:], in1=gzt[:, h, :], op=ALU.mult),
                     r=[oph[hh], gzt], w=[ybT])

    def outproj(Tn):
        t0 = Tn * TT
        for j in range(4):
            yot = yo[j % 2]
            for half in range(2):
                pp = pps[half]
                for kc in range(8):
                    src = yat if kc < 4 else ybT
                    cx.PE(lambda e, pp=pp, kc=kc, src=src, j=j, half=half: e.matmul(
                        pp[:], lhsT=src[:, kc % 4, j * 128:(j + 1) * 128], rhs=Wo[:, kc, half * 512:(half + 1) * 512],
                        start=(kc == 0), stop=(kc == 7)), r=[src, Wo], w=[pp])
                if half == 0:
                    cx.V(lambda e, pp=pp, yot=yot: e.tensor_copy(out=yot[:, 0:512], in_=pp[:]), r=[pp], w=[yot])
                else:
                    cx.A(lambda e, pp=pp, yot=yot: e.activation(out=yot[:, 512:1024], in_=pp[:], func=AF.Copy),
                         r=[pp], w=[yot])
            cx.DMA("gpsimd", lambda e, yot=yot, j=j, t0=t0: e.dma_start(out=ypart_ap[t0 + j * 128:t0 + (j + 1) * 128, :],
                                                                       in_=yot[:]), r=[yot],
                   w=([ar.store_dt(t0 + j * 128)] if ar is not None else []), stream=f"yo{j % 2}")
        if ar is not None:
            ar.after_tile(Tn)

    blocks = []
    n = 0
    for Tn in range(NT):
        for hp in range(2):
            kbs = list(range(Tn * 4 + 3, -1, -1))
            for i, kb in enumerate(kbs):
                j = kb - Tn * 4
                q0 = 128 * j if j > 0 else 0
                blocks.append((n, Tn, hp, kb, i == 0, i == len(kbs) - 1, q0))
                n += 1
    if DBG.get("stop") == "p2load":
        return
    NBk = len(blocks)
    late_done = set()
    for t in range(-3, NBk):
        if 0 <= t + 2 < NBk:
            s_E(blocks[t + 2])
        if 0 <= t + 1 < NBk:
            s_cum(blocks[t + 1])
        if 0 <= t < NBk:
            blk = blocks[t]
            if blk[1] not in late_done:
                late_done.add(blk[1])
                load_late(blk[1])
            s_fin(blk)
            if t + 1 == NBk or blocks[t + 1][1] != blk[1]:
                outproj(blk[1])
        if 0 <= t + 3 < NBk:
            s_z(blocks[t + 3])


def even_pass2_old(cx, c, SL, wout_ap, scr, scr_dt, ypart_ap, pipe=True, ar=None):
    S = cx.S
    NT = SL // TT
    NB = SL // 128
    Wo = cx.sb([128, 8, D], BF16, "Wo")
    stg = [cx.sb([128, 1024], F32, "wstg") for _ in range(2)]
    load_weight_bf16(cx, Wo, wout_ap, D, stg, "wo")
    Kc = cx.sb([128, 4, SL], BF16, "Kc")
    Vc = cx.sb([128, NB, 512], BF16, "Vc")
    qt = [cx.sb([128, 4, TT], BF16, "qt") for _ in range(2)]
    gzt = cx.sb([128, 4, TT], BF16, "gzt")
    yat = cx.sb([128, 4, TT], BF16, "yat")
    ybT = cx.sb([128, 4, TT], BF16, "ybT")
    NBUF = 3
    E = [cx.sb([128, TT], BF16, "E") for _ in range(4)]
    sp = [cx.sb([128, TT], BF16, "sp") for _ in range(NBUF)]
    At = [cx.sb([128, TT], BF16, "At") for _ in range(2)]
    Rb = cx.sb([128, TT], BF16, "Rb")
    yo = [cx.sb([128, D], F32, "yo") for _ in range(2)]
    zps = [cx.bank(0, name="zps"), cx.bank(1, name="zps")]
    cps = [cx.bank(2, name="cps"), cx.bank(3, name="cps")]
    ops_ = [cx.bank(4, name="ops"), cx.bank(5, name="ops")]
    pps = [cx.bank(6, name="pps"), cx.bank(7, name="pps")]
    m0 = c["m0"]

    Xr = [cx.sb([128, TT], BF16, "X") for _ in range(3)]
    loaded = set()

    def ensure_loaded(Tn):
        if Tn in loaded:
            return
        loaded.add(Tn)
        t0 = Tn * TT
        cx.DMA("sync", lambda e: e.dma_start(out=Kc[:, :, t0:t0 + TT],
                                             in_=scr["kT"][:, :, t0:t0 + TT].rearrange("c p t -> p c t")),
               r=[scr_dt["k"][Tn]], w=[Kc], stream="kc")
        cx.DMA("sync", lambda e: e.dma_start(out=Vc[:, Tn * 4:Tn * 4 + 4, :],
                                             in_=scr["vB"][t0:t0 + TT, :].rearrange("(j p) c -> p j c", p=128)),
               r=[scr_dt["v"][Tn]], w=[Vc], stream="vc")
        q = qt[Tn % 2]
        cx.DMA("sync", lambda e: e.dma_start(out=q[:], in_=scr["qT"][:, :, t0:t0 + TT].rearrange("c p t -> p c t")),
               r=[scr_dt["q"][Tn]], w=[q], stream=f"qt{Tn % 2}")

    def load_late(Tn):
        t0 = Tn * TT
        cx.DMA("sync", lambda e: e.dma_start(out=gzt[:], in_=scr["gzT"][:, :, t0:t0 + TT].rearrange("c p t -> p c t")),
               r=[scr_dt["gz"][Tn]], w=[gzt], stream="gzt")
        cx.DMA("sync", lambda e: e.dma_start(out=yat[:], in_=scr["yaT"][:, :, t0:t0 + TT].rearrange("c p t -> p c t")),
               r=[scr_dt["ya"][Tn]], w=[yat], stream="yat")

    def s_z(blk):
        n, Tn, h, kb, first, last, q0 = blk
        ensure_loaded(Tn)
        z = zps[n % 2]
        q = qt[Tn % 2]
        fr = slice(q0, TT)
        cx.PE(lambda e: e.matmul(z[:, fr], lhsT=Kc[:, h, kb * 128:(kb + 1) * 128], rhs=q[:, h, fr],
                                 start=True, stop=True), r=[Kc, q], w=[z])

    def s_E(blk):
        n, Tn, h, kb, first, last, q0 = blk
        z = zps[n % 2]
        Et = E[n % 4]
        fr = slice(q0, TT)
        cx.A(lambda e: e.activation(out=Et[:, fr], in_=z[:, fr], func=AF.Exp), r=[z], w=[Et])

    def s_ln(blk):
        n, Tn, h, kb, first, last, q0 = blk
        Et, spt = E[n % 4], sp[n % NBUF]
        fr = slice(q0, TT)
        cx.A(lambda e: e.activation(out=spt[:, fr], in_=Et[:, fr], func=AF.Ln, bias=1.0), r=[Et], w=[spt])
        if kb >= Tn * 4:
            cx.G(lambda e: e.tensor_tensor(out=spt[:, fr], in0=spt[:, fr], in1=m0[:, 0:TT - q0], op=ALU.mult),
                 r=[spt, m0], w=[spt])

    def s_cum(blk):
        n, Tn, h, kb, first, last, q0 = blk
        cp = cps[n % 2]
        spt = sp[n % NBUF]
        fr = slice(q0, TT)
        cx.PE(lambda e: e.matmul(cp[:, fr], lhsT=c["tneg"][:], rhs=spt[:, fr], start=True, stop=first),
              r=[c["tneg"], spt], w=[cp])
        if not first:
            cx.PE(lambda e: e.matmul(cp[:, fr], lhsT=c["onesneg"][:], rhs=Rb[:, fr], start=False, stop=True),
                  r=[c["onesneg"], Rb], w=[cp])

    def s_X(blk):
        n, Tn, h, kb, first, last, q0 = blk
        cp = cps[n % 2]
        spt, xt_ = sp[n % NBUF], Xr[n % 3]
        fr = slice(q0, TT)
        cx.A(lambda e: e.activation(out=xt_[:, fr], in_=cp[:, fr], func=AF.Exp), r=[cp], w=[xt_])
        if first:
            cx.G(lambda e: e.memset(Rb[:], 0.0), w=[Rb])
        if not last:
            cx.G(lambda e: e.tensor_tensor(out=Rb[:, fr], in0=Rb[:, fr], in1=spt[:, fr], op=ALU.add),
                 r=[Rb, spt], w=[Rb])

    def s_fin(blk):
        n, Tn, h, kb, first, last, q0 = blk
        Et, xt_, at = E[n % 4], Xr[n % 3], At[n % 2]
        g = Tn * 4 + h
        op_ = ops_[g % 2]
        fr = slice(q0, TT)
        cx.V(lambda e: e.tensor_tensor(out=at[:, fr], in0=Et[:, fr], in1=xt_[:, fr], op=ALU.mult),
             r=[Et, xt_], w=[at])
        if kb >= Tn * 4:
            cx.V(lambda e: e.tensor_tensor(out=at[:, fr], in0=at[:, fr], in1=m0[:, 0:TT - q0], op=ALU.mult),
                 r=[at, m0], w=[at])
        cx.PE(lambda e: e.matmul(op_[:, fr], lhsT=Vc[:, kb, h * 128:(h + 1) * 128], rhs=at[:, fr],
                                 start=first, stop=last, skip_group_check=True), r=[Vc, at], w=[op_])
        if last:
            cx.V(lambda e: e.tensor_tensor(out=ybT[:, h, :], in0=op_[:], in1=gzt[:, h, :], op=ALU.mult),
                 r=[op_, gzt], w=[ybT])

    def outproj(Tn):
        t0 = Tn * TT
        for j in range(4):
            yot = yo[j % 2]
            for half in range(2):
                pp = pps[half]
                for kc in range(8):
                    src = yat if kc < 4 else ybT
                    cx.PE(lambda e, pp=pp, kc=kc, src=src, j=j, half=half: e.matmul(
                        pp[:], lhsT=src[:, kc % 4, j * 128:(j + 1) * 128], rhs=Wo[:, kc, half * 512:(half + 1) * 512],
                        start=(kc == 0), stop=(kc == 7)), r=[src, Wo], w=[pp])
                if half == 0:
                    cx.V(lambda e, pp=pp, yot=yot: e.tensor_copy(out=yot[:, 0:512], in_=pp[:]), r=[pp], w=[yot])
                else:
                    cx.A(lambda e, pp=pp, yot=yot: e.activation(out=yot[:, 512:1024], in_=pp[:], func=AF.Copy),
                         r=[pp], w=[yot])
            cx.DMA("gpsimd", lambda e, yot=yot, j=j, t0=t0: e.dma_start(out=ypart_ap[t0 + j * 128:t0 + (j + 1) * 128, :],
                                                                       in_=yot[:]), r=[yot],
                   w=([ar.store_dt(t0 + j * 128)] if ar is not None else []), stream=f"yo{j % 2}")
        if ar is not None:
            ar.after_tile(Tn)

    blocks = []
    n = 0
    for Tn in range(NT):
        for h in range(4):
            kbs = list(range(Tn * 4 + 3, -1, -1))
            for i, kb in enumerate(kbs):
                j = kb - Tn * 4
                q0 = 128 * j if j > 0 else 0
                blocks.append((n, Tn, h, kb, i == 0, i == len(kbs) - 1, q0))
                n += 1
    if DBG.get("stop") == "p2load":
        return
    NBk = len(blocks)
    late_done = set()
    for t in range(-3, NBk):
        if 0 <= t + 3 < NBk:
            s_z(blocks[t + 3])
        if 0 <= t + 2 < NBk:
            s_E(blocks[t + 2])
            s_ln(blocks[t + 2])
        if 0 <= t + 1 < NBk:
            s_cum(blocks[t + 1])
            s_X(blocks[t + 1])
        if 0 <= t < NBk:
            blk = blocks[t]
            if blk[1] not in late_done:
                late_done.add(blk[1])
                load_late(blk[1])
            s_fin(blk)
            if t + 1 == NBk or blocks[t + 1][1] != blk[1]:
                outproj(blk[1])


def _even_scratch(nc, SL):
    scr = {}
    for k in ("qT", "kT", "gzT", "yaT"):
        scr[k] = nc.dram_tensor("scr_" + k, [4, 128, SL], BF16).ap()
    scr["vB"] = nc.dram_tensor("scr_vB", [SL, 512], BF16).ap()
    scr_dt = {k: [DT(f"{k}{t}") for t in range(SL // TT)] for k in ("q", "k", "gz", "v", "ya")}
    return scr, scr_dt


def build_even(SL, n_parts, pipe=True):
    nc = bass.Bass("TRN2", target_bir_lowering=False)
    inp = lambda name, shape: nc.dram_tensor(name, list(shape), F32, kind="ExternalInput").ap()
    x = inp("x", [SL, D])
    parts = [inp(f"yp{k}", [SL, D]) for k in range(n_parts)]
    gpost = inp("gpost", [D]) if n_parts else None
    gpre = inp("gpre", [D])
    win = inp("win", [D, NCOL_E])
    convw = inp("convw", [128, 8, 4])
    bi = inp("bi", [2, 1])
    bf = inp("bf", [2, 1])
    hng = inp("hng", [512])
    wout = inp("wout", [D, D])
    xcur = nc.dram_tensor("xcur", [SL, D], F32, kind="ExternalOutput").ap() if n_parts else None
    ypart = nc.dram_tensor("ypart", [SL, D], F32, kind="ExternalOutput").ap()
    scr, scr_dt = _even_scratch(nc, SL)
    S = Sched(nc)
    cx = Ctx(nc, S)
    c = make_consts(cx)
    mark = cx.off
    even_pass1(cx, c, SL, x, parts, gpost, gpre, xcur, win, convw, bi, bf, hng, scr, scr_dt)
    if DBG.get("stop") is None or DBG.get("stop").startswith("p2"):
        S.barrier()
        cx.off = mark
        even_pass2(cx, c, SL, wout, scr, scr_dt, ypart, pipe=pipe)
    S.finish()
    return nc, S


def prep_even(inp, e, layer, c):
    w = np.asarray(inp["w_in_ab"][e])
    A = lambda g: w[:, g * 1024 + 512 * c: g * 1024 + 512 * c + 512]
    Bq = lambda g: w[:, 5128 + g * 1024 + 512 * c: 5128 + g * 1024 + 512 * c + 512]
    gi = w[:, 5120 + 2 * c: 5122 + 2 * c]
    gf = w[:, 5124 + 2 * c: 5126 + 2 * c]
    win = np.concatenate([A(0), A(1), Bq(0), Bq(1), Bq(3), A(2), A(3), A(4), Bq(2), gi, gf], axis=1)
    cq = np.asarray(inp["conv_qk"][e])
    cols = np.concatenate([cq[:, 512 * c:512 * c + 512], cq[:, 1024 + 512 * c:1024 + 512 * c + 512]], axis=1)
    convw = np.ascontiguousarray(cols.reshape(4, 8, 128).transpose(2, 1, 0))
    wo = np.asarray(inp["w_out_ab"][e])
    wout = np.concatenate([wo[512 * c:512 * c + 512], wo[1024 + 512 * c:1024 + 512 * c + 512]], axis=0)
    d = dict(
        gpre=np.ascontiguousarray(inp["pre_norm_g"][layer]),
        win=np.ascontiguousarray(win), convw=convw,
        bi=np.ascontiguousarray(np.asarray(inp["bias_i"][e])[2 * c:2 * c + 2].reshape(2, 1)),
        bf=np.ascontiguousarray(np.asarray(inp["bias_f"][e])[2 * c:2 * c + 2].reshape(2, 1)),
        hng=np.ascontiguousarray(np.asarray(inp["head_norm_g"][e])[512 * c:512 * c + 512]),
        wout=np.ascontiguousarray(wout),
    )
    if layer > 0:
        d["gpost"] = np.ascontiguousarray(inp["post_norm_g"][layer - 1])
    return {k: np.asarray(v, dtype=np.float32) for k, v in d.items()}


POOL_WINDOWS = (2, 4, 8, 16)
HALO = 16
SLOT_W = ((2, 4), (16, 8))


def odd_pass(cx, c, SL, wsel_ap, x_ap, parts, gpost_ap, gpre_ap, xcur_ap, win_ap, pw_ap, pscale_ap, wout_ap, ypart_ap,
             ar=None, parts_dt=None):
    NT = SL // TT
    Wc = cx.sb([128, 8, 2048], BF16, "Wc")
    stg = [cx.sb([128, 1024], F32, "wstg") for _ in range(2)]
    load_weight_bf16(cx, Wc, win_ap, 2048, stg, "wc")
    PW = cx.sb([128, 8, 512], BF16, "PW")
    load_weight_bf16(cx, PW, pw_ap, 512, stg, "pw")
    Wo = cx.sb([128, 8, D], BF16, "Wo")
    load_weight_bf16(cx, Wo, wout_ap, D, stg, "wo")
    pscale = cx.sb([128, 8], F32, "pscale")
    cx.DMA("sync", lambda e: e.dma_start(out=pscale[:], in_=pscale_ap), w=[pscale], stream="pscale")
    wsel = cx.sb([128, 4], F32, "wsel")
    cx.DMA("sync", lambda e: e.dma_start(out=wsel[:], in_=wsel_ap), w=[wsel], stream="wsel")
    kco = cx.sb([128, 4], F32, "kco")
    invc0 = []
    for si in range(2):
        for wi in range(2):
            w = SLOT_W[si][wi]
            col = 2 * si + wi
            cx.V(lambda e, col=col, w=w: e.tensor_scalar_mul(out=kco[:, col:col + 1], in0=wsel[:, col:col + 1],
                                                             scalar1=1.0 / w), r=[wsel], w=[kco])
            t = cx.sb([128, TT], F32, "invc0")
            cx.G(lambda e, t=t, w=w: e.memset(t[:], 1.0 / w), w=[t])
            for k in range(w - 1):
                cx.G(lambda e, t=t, k=k: e.memset(t[:, k:k + 1], 1.0 / (k + 1)), w=[t])
            cx.V(lambda e, t=t, col=col: e.tensor_scalar_mul(out=t[:], in0=t[:], scalar1=wsel[:, col:col + 1]),
                 r=[t, wsel], w=[t])
            invc0.append(t)
    norm = NormStage(cx, c, x_ap, parts, gpost_ap, gpre_ap, xcur_ap, "no", parts_dt=parts_dt, nhb=4, nyt=2)
    hT2 = [cx.sb([128, 8, TT], BF16, "hT") for _ in range(3)]
    pb = [cx.sb([128, HALO + TT], F32, "pb") for _ in range(8)]
    for t in pb:
        cx.G(lambda e, t=t: e.memset(t[:, 0:HALO], 0.0), w=[t])
    LV = [cx.sb([128, HALO + TT], F32, "lv") for _ in range(4)]
    tC = cx.sb([128, TT], F32, "tC")
    tD = cx.sb([128, TT], F32, "tD")
    pl2 = [cx.sb([128, 8, TT], BF16, "pl") for _ in range(2)]
    gz = [cx.sb([128, TT], BF16, "gz") for _ in range(8)]
    yT = cx.sb([128, 8, TT], BF16, "yT")
    yo = [cx.sb([128, D], F32, "yo") for _ in range(1)]
    pj = [cx.bank(0, name="pj"), cx.bank(1, name="pj"), cx.bank(2, name="pj")]
    pps = [cx.bank(3, name="pps"), cx.bank(4, name="pps")]
    pjn = [0]

    def pslot():
        p = pj[pjn[0] % 3]
        pjn[0] += 1
        return p

    W_ = HALO + TT
    for tt_ in range(min(2, NT)):
        for j in range(4):
            norm.run(tt_ * TT + j * 128, hT2[tt_], j)

    def stageA(Tn):
        hT = hT2[Tn % 3]
        pl = pl2[Tn % 2]
        for cc in range(8):
            gi = cc // 4
            ps = pslot()
            for kc in range(8):
                cx.PE(lambda e, ps=ps, kc=kc, cc=cc: e.matmul(ps[:], lhsT=Wc[:, kc, cc * 128:(cc + 1) * 128],
                                                              rhs=hT[:, kc, :], start=(kc == 0), stop=(kc == 7)),
                      r=[Wc, hT], w=[ps])
            pbt = pb[cc]
            cx.A(lambda e, ps=ps, pbt=pbt: e.activation(out=pbt[:, HALO:W_], in_=ps[:], func=AF.Copy), r=[ps], w=[pbt])
            w1, w2 = SLOT_W[gi]
            wmax = max(w1, w2)
            src, sh, lo, li = pbt, 1, 1, 0
            while sh < wmax:
                dst = LV[li]
                eng = cx.V if (li % 2 == 0) else cx.G
                eng(lambda e, src=src, dst=dst, sh=sh, lo=lo: e.tensor_tensor(out=dst[:, lo:W_], in0=src[:, lo:W_],
                                                                              in1=src[:, lo - sh:W_ - sh], op=ALU.add),
                    r=[src], w=[dst])
                src = dst
                sh *= 2
                lo += sh
                li += 1
            s1 = LV[int(math.log2(w1)) - 1]
            s2_ = LV[int(math.log2(w2)) - 1]
            c1, c2 = 2 * gi, 2 * gi + 1
            if Tn == 0:
                cx.V(lambda e, s1=s1, c1=c1: e.tensor_tensor(out=tC[:], in0=s1[:, HALO:W_], in1=invc0[c1][:], op=ALU.mult),
                     r=[s1, invc0[c1]], w=[tC])
                cx.G(lambda e, s2_=s2_, c2=c2: e.tensor_tensor(out=tD[:], in0=s2_[:, HALO:W_], in1=invc0[c2][:], op=ALU.mult),
                     r=[s2_, invc0[c2]], w=[tD])
                cx.V(lambda e: e.tensor_tensor(out=tC[:], in0=tC[:], in1=tD[:], op=ALU.add), r=[tC, tD], w=[tC])
            else:
                cx.V(lambda e, s1=s1, c1=c1, pbt=pbt: e.scalar_tensor_tensor(out=tC[:], in0=s1[:, HALO:W_], scalar=kco[:, c1:c1 + 1],
                                                                             in1=pbt[:, HALO:W_], op0=ALU.mult, op1=ALU.subtract),
                     r=[s1, kco, pbt], w=[tC])
                cx.V(lambda e, s2_=s2_, c2=c2, cc=cc: e.scalar_tensor_tensor(out=pl[:, cc, :], in0=s2_[:, HALO:W_], scalar=kco[:, c2:c2 + 1],
                                                                             in1=tC[:], op0=ALU.mult, op1=ALU.add),
                     r=[s2_, kco, tC], w=[pl])
            if Tn == 0:
                cx.V(lambda e, pbt=pbt, cc=cc: e.tensor_tensor(out=pl[:, cc, :], in0=tC[:], in1=pbt[:, HALO:W_],
                                                               op=ALU.subtract), r=[tC, pbt], w=[pl])
            cx.G(lambda e, pbt=pbt: e.tensor_copy(out=pbt[:, 0:HALO], in_=pbt[:, TT:W_]), r=[pbt], w=[pbt])

    def stageB(Tn):
        t0 = Tn * TT
        hT = hT2[Tn % 3]
        pl = pl2[Tn % 2]
        for cc in range(8):
            pz = pslot()
            for kc in range(8):
                cx.PE(lambda e, pz=pz, kc=kc, cc=cc: e.matmul(pz[:], lhsT=Wc[:, kc, 1024 + cc * 128:1024 + (cc + 1) * 128],
                                                              rhs=hT[:, kc, :], start=(kc == 0), stop=(kc == 7)),
                      r=[Wc, hT], w=[pz])
            gzt = gz[cc]
            cx.A(lambda e, pz=pz, gzt=gzt: e.activation(out=gzt[:], in_=pz[:], func=AF.Silu), r=[pz], w=[gzt])
        for cc in range(8):
            gi, ec = cc // 4, cc % 4
            gzt = gz[cc]
            pm = pslot()
            for kc in range(4):
                cx.PE(lambda e, pm=pm, kc=kc, gi=gi, ec=ec: e.matmul(pm[:], lhsT=PW[:, gi * 4 + kc, ec * 128:(ec + 1) * 128],
                                                                      rhs=pl[:, gi * 4 + kc, :], start=(kc == 0), stop=(kc == 3)),
                      r=[PW, pl], w=[pm])
            cx.V(lambda e, pm=pm, gzt=gzt, cc=cc: e.scalar_tensor_tensor(out=yT[:, cc, :], in0=pm[:], scalar=pscale[:, cc:cc + 1],
                                                                         in1=gzt[:], op0=ALU.mult, op1=ALU.mult),
                 r=[pm, pscale, gzt], w=[yT])
        for j in range(4):
            yot = yo[0]
            for half in range(2):
                pp = pps[half]
                for kc in range(8):
                    cx.PE(lambda e, pp=pp, kc=kc, j=j, half=half: e.matmul(
                        pp[:], lhsT=yT[:, kc, j * 128:(j + 1) * 128], rhs=Wo[:, kc, half * 512:(half + 1) * 512],
                        start=(kc == 0), stop=(kc == 7)), r=[yT, Wo], w=[pp])
                if half == 0:
                    cx.V(lambda e, pp=pp, yot=yot: e.tensor_copy(out=yot[:, 0:512], in_=pp[:]), r=[pp], w=[yot])
                else:
                    cx.A(lambda e, pp=pp, yot=yot: e.activation(out=yot[:, 512:1024], in_=pp[:], func=AF.Copy),
                         r=[pp], w=[yot])
            cx.DMA("gpsimd", lambda e, yot=yot, j=j, t0=t0: e.dma_start(out=ypart_ap[t0 + j * 128:t0 + (j + 1) * 128, :],
                                                                       in_=yot[:]), r=[yot],
                   w=([ar.store_dt(t0 + j * 128)] if ar is not None else []), stream="yo0")
        if ar is not None:
            ar.after_tile(Tn)

    stageA(0)
    for Tn in range(NT):
        hbs = [norm.pre((Tn + 2) * TT + j * 128) for j in range(4)] if Tn + 2 < NT else None
        if Tn + 1 < NT:
            stageA(Tn + 1)
        stageB(Tn)
        if hbs is not None:
            for j in range(4):
                norm.post(hbs[j], hT2[(Tn + 2) % 3], j)


def build_odd(SL, n_parts):
    nc = bass.Bass("TRN2", target_bir_lowering=False)
    inp = lambda name, shape: nc.dram_tensor(name, list(shape), F32, kind="ExternalInput").ap()
    x = inp("x", [SL, D])
    parts = [inp(f"yp{k}", [SL, D]) for k in range(n_parts)]
    gpost = inp("gpost", [D]) if n_parts else None
    gpre = inp("gpre", [D])
    win = inp("win", [D, 2048])
    pw = inp("pw", [1024, 512])
    pscale = inp("pscale", [128, 8])
    wsel = inp("wsel", [128, 4])
    wout = inp("wout", [D, D])
    xcur = nc.dram_tensor("xcur", [SL, D], F32, kind="ExternalOutput").ap() if n_parts else None
    ypart = nc.dram_tensor("ypart", [SL, D], F32, kind="ExternalOutput").ap()
    S = Sched(nc)
    cx = Ctx(nc, S)
    c = make_consts(cx)
    odd_pass(cx, c, SL, wsel, x, parts, gpost, gpre, xcur, win, pw, pscale, wout, ypart)
    S.finish()
    return nc, S


def prep_odd(inp, o, layer, groups):
    w = np.asarray(inp["w_in_c"][o])
    pcols = np.concatenate([w[:, 512 * g:512 * g + 512] for g in groups], axis=1)
    zcols = np.concatenate([w[:, 2048 + 512 * g:2048 + 512 * g + 512] for g in groups], axis=1)
    pwv = np.asarray(inp["pool_w"][o])
    pw = np.concatenate([pwv[g] for g in groups], axis=0)
    sc = np.concatenate([np.asarray(inp["pool_scale"][o])[512 * g:512 * g + 512] for g in groups])
    wo = np.asarray(inp["w_out_c"][o])
    wout = np.concatenate([wo[512 * g:512 * g + 512] for g in groups], axis=0)
    d = dict(
        gpre=inp["pre_norm_g"][layer], gpost=inp["post_norm_g"][layer - 1],
        win=np.concatenate([pcols, zcols], axis=1), pw=pw,
        pscale=sc.reshape(8, 128).T, wout=wout,
        wsel=np.tile(np.array([[float(POOL_WINDOWS[groups[si]] == SLOT_W[si][wi]) for si in range(2) for wi in range(2)]],
                              dtype=np.float32), (128, 1)),
    )
    return {k: np.ascontiguousarray(np.asarray(v, dtype=np.float32)) for k, v in d.items()}


def build_combine(NTOK):
    nc = bass.Bass("TRN2", target_bir_lowering=False)
    inp = lambda name, shape: nc.dram_tensor(name, list(shape), F32, kind="ExternalInput").ap()
    x = inp("x", [NTOK, D])
    parts = [inp(f"yp{k}", [NTOK, D]) for k in range(2)]
    gpost = inp("gpost", [D])
    gpre = inp("gpre", [D])
    out = nc.dram_tensor("xcur", [NTOK, D], F32, kind="ExternalOutput").ap()
    S = Sched(nc)
    cx = Ctx(nc, S)
    c = make_consts(cx)
    norm = NormStage(cx, c, x, parts, gpost, gpre, out, "nc")
    for r0 in range(0, NTOK, 128):
        norm.run(r0, None, 0)
    S.finish()
    return nc, S


ODD_GROUPS = ((0, 3), (1, 2))
_PROG = {}

EVEN_KEYS = ("gpre", "win", "convw", "bi", "bf", "hng", "wout")
ODD_KEYS = ("gpre", "win", "pw", "pscale", "wsel", "wout")
EVEN_SHAPES = dict(gpre=[D], gpost=[D], win=[D, NCOL_E], convw=[128, 8, 4], bi=[2, 1], bf=[2, 1], hng=[512], wout=[D, D])
ODD_SHAPES = dict(gpre=[D], gpost=[D], win=[D, 2048], pw=[1024, 512], pscale=[128, 8], wsel=[128, 4], wout=[D, D])


def build_fused(SL):
    nc = bass.Bass("TRN2", target_bir_lowering=False)
    inp = lambda name, shape: nc.dram_tensor(name, list(shape), F32, kind="ExternalInput").ap()
    x = inp("x", [SL, D])
    wts = []
    for l in range(DEPTH):
        shapes = EVEN_SHAPES if l % 2 == 0 else ODD_SHAPES
        keys = (EVEN_KEYS if l % 2 == 0 else ODD_KEYS) + (("gpost",) if l > 0 else ())
        wts.append({k: inp(f"{k}_l{l}", shapes[k]) for k in keys})
    gfin = inp("gpost_fin", [D])
    out = nc.dram_tensor("out", [SL, D], F32, kind="ExternalOutput").ap()
    ypart = [nc.dram_tensor(f"ypart{l}", [SL, D], F32).ap() for l in range(DEPTH)]
    ysum = [nc.dram_tensor(f"ysum{l}", [SL, D], F32).ap() for l in range(DEPTH)]
    xcur = [None] + [nc.dram_tensor(f"xcur{l}", [SL, D], F32).ap() for l in range(1, DEPTH)]
    scr, _ = _even_scratch(nc, SL)
    S = Sched(nc)
    cx = Ctx(nc, S)
    c = make_consts(cx)
    mark = cx.off
    prev_ar = None
    for l in range(DEPTH):
        w = wts[l]
        x_l = x if l <= 1 else xcur[l - 1]
        parts = [] if l == 0 else [ysum[l - 1]]
        parts_dt = None if l == 0 else [prev_ar.sum_dt]
        ar = ARHook(cx, ypart[l], ysum[l], SL)
        if l % 2 == 0:
            scr_dt = {k: [DT(f"{k}{t}") for t in range(SL // TT)] for k in ("q", "k", "gz", "v", "ya")}
            even_pass1(cx, c, SL, x_l, parts, w.get("gpost"), w["gpre"], xcur[l], w["win"], w["convw"], w["bi"], w["bf"],
                       w["hng"], scr, scr_dt, parts_dt=parts_dt)
            S.barrier()
            cx.off = mark
            even_pass2(cx, c, SL, w["wout"], scr, scr_dt, ypart[l], ar=ar)
        else:
            odd_pass(cx, c, SL, w["wsel"], x_l, parts, w.get("gpost"), w["gpre"], xcur[l], w["win"], w["pw"], w["pscale"],
                     w["wout"], ypart[l], ar=ar, parts_dt=parts_dt)
        S.barrier()
        cx.off = mark
        prev_ar = ar
    norm = NormStage(cx, c, xcur[DEPTH - 1], [ysum[DEPTH - 1]], gfin, wts[0]["gpre"], out, "nf", parts_dt=[prev_ar.sum_dt],
                     nyt=2)
    for r0 in range(0, SL, 128):
        norm.run(r0, None, 0)
    S.finish()
    return nc, S


def kernel(**inputs):
    inp = {k: np.asarray(v) for k, v in inputs.items()}
    x = np.ascontiguousarray(inp["x"], dtype=np.float32)
    B, SL, _ = x.shape
    maps = []
    for b in range(B):
        for c in range(2):
            d = {"x": x[b], "gpost_fin": np.ascontiguousarray(inp["post_norm_g"][DEPTH - 1], dtype=np.float32)}
            for l in range(DEPTH):
                p = prep_even(inp, l // 2, l, c) if l % 2 == 0 else prep_odd(inp, l // 2, l, ODD_GROUPS[c])
                if l == 0:
                    p.pop("gpost", None)
                for k, v in p.items():
                    d[f"{k}_l{l}"] = v
            maps.append(d)
    if ("fused", SL) not in _PROG:
        _PROG[("fused", SL)] = build_fused(SL)[0]
    res = run_bass_kernel_spmd(_PROG[("fused", SL)], maps, core_ids=list(range(2 * B))).results
    return np.stack([res[2 * b]["out"] for b in range(B)], axis=0).astype(np.float32)
```
